# Optimizing a Trainium2 kernel written in Bass

```python
import math
import jax, jax.numpy as jnp
from jax import lax
import numpy as np

D_MODEL = 1024
BATCH = 16
SEQ = 2048
DEPTH = 1

CHUNK = 64
N_META = 16
D_MIX = D_MODEL
D_LRU = D_MIX // 2
D_ATT = D_MIX - D_LRU
N_ATT_HEADS = 4
ATT_VDIM = D_ATT // N_ATT_HEADS
ATT_QKDIM = ATT_VDIM // 2
D_QK = N_ATT_HEADS * 2 * ATT_QKDIM
ROT_DIM = ATT_QKDIM // 4
ROPE_THETA = 500000.0
N_LRU_BLOCKS = 8
LRU_BLOCK = D_LRU // N_LRU_BLOCKS
CONV_W = 4
LRU_C = 8.0
Q_BLOCK = 128
EPS = 1e-6
SUBLN_EPS = 1e-5
D_IN = 2 * D_QK + D_ATT + D_ATT + D_LRU + D_LRU

kernel_name = "hybrid_diffattn_rglru_meta_chunk_causal"


def rms_norm(x, w, eps):
    xf = x.astype(jnp.float32)
    y = xf * lax.rsqrt(jnp.mean(xf * xf, axis=-1, keepdims=True) + eps)
    return (y * w.astype(jnp.float32)).astype(x.dtype)


def rope_tables(T):
    pos = jnp.arange(T, dtype=jnp.float32)
    inv = jnp.power(ROPE_THETA, -jnp.arange(0, ROT_DIM, 2, dtype=jnp.float32) / ROT_DIM)
    ang = pos[:, None] * inv[None, :]
    return jnp.cos(ang), jnp.sin(ang)


def apply_partial_rope(x, cos, sin):
    xr = x[..., :ROT_DIM].astype(jnp.float32)
    x1, x2 = xr[..., : ROT_DIM // 2], xr[..., ROT_DIM // 2:]
    c = cos[None, :, None, :]
    s = sin[None, :, None, :]
    rot = jnp.concatenate([x1 * c - x2 * s, x2 * c + x1 * s], axis=-1)
    return jnp.concatenate([rot.astype(x.dtype), x[..., ROT_DIM:]], axis=-1)


def chunk_ids(T):
    p = jnp.arange(T)
    return jnp.where(p < N_META, 0, (p - N_META) // CHUNK + 1)


def chunk_end(p):
    if p < N_META:
        return N_META
    return N_META + ((p - N_META) // CHUNK + 1) * CHUNK


def diff_attention(q, k, v, lam, cid):
    T = q.shape[1]
    scale = ATT_QKDIM ** -0.5
    outs = []
    for s in range(0, T, Q_BLOCK):
        e = min(s + Q_BLOCK, T)
        kend = min(chunk_end(e - 1), T)
        qb, kb, vb = q[:, s:e], k[:, :kend], v[:, :kend]
        scores = jnp.einsum('bqhmd,bkhmd->bhmqk', qb, kb).astype(jnp.float32) * scale
        mask = cid[s:e, None] >= cid[None, :kend]
        scores = jnp.where(mask[None, None, None], scores, -jnp.inf)
        p = jax.nn.softmax(scores, axis=-1)
        w = p[:, :, 0] - lam * p[:, :, 1]
        outs.append(jnp.einsum('bhqk,bkhd->bqhd', w.astype(vb.dtype), vb))
    return jnp.concatenate(outs, axis=1)


def rg_lru(x, w_r, b_r, w_i, b_i, lru_L):
    B, T, C = x.shape
    xb = x.reshape(B, T, N_LRU_BLOCKS, LRU_BLOCK)
    r = jax.nn.sigmoid(jnp.einsum('btnc,ncd->btnd', xb, w_r).reshape(B, T, C) + b_r)
    i = jax.nn.sigmoid(jnp.einsum('btnc,ncd->btnd', xb, w_i).reshape(B, T, C) + b_i)
    log_a = -LRU_C * r.astype(jnp.float32) * jax.nn.softplus(-lru_L.astype(jnp.float32))
    a = jnp.exp(log_a)
    mult = jnp.sqrt(-jnp.expm1(2.0 * log_a))
    mult = mult.at[:, 0].set(1.0)
    b = mult * (i * x).astype(jnp.float32)

    def combine(left, right):
        a1, b1 = left
        a2, b2 = right
        return a1 * a2, a2 * b1 + b2

    _, h = lax.associative_scan(combine, (a, b), axis=1)
    return h.astype(x.dtype)


def hybrid_layer(h, norm_w, w_in, conv_w, conv_b, w_r, b_r, w_i, b_i, lru_L,
                 lq1, lk1, lq2, lk2, subln_w, w_out, lam_init, cos, sin, cid):
    B, T, _ = h.shape
    u = rms_norm(h, norm_w, EPS)
    proj = u @ w_in
    q, k, v, g_att, x_lru, g_lru = jnp.split(
        proj, [D_QK, 2 * D_QK, 2 * D_QK + D_ATT, 2 * D_QK + 2 * D_ATT,
               2 * D_QK + 2 * D_ATT + D_LRU], axis=-1)

    q = apply_partial_rope(q.reshape(B, T, 2 * N_ATT_HEADS, ATT_QKDIM), cos, sin)
    k = apply_partial_rope(k.reshape(B, T, 2 * N_ATT_HEADS, ATT_QKDIM), cos, sin)
    q = q.reshape(B, T, N_ATT_HEADS, 2, ATT_QKDIM)
    k = k.reshape(B, T, N_ATT_HEADS, 2, ATT_QKDIM)
    v = v.reshape(B, T, N_ATT_HEADS, ATT_VDIM)
    lam = (jnp.exp(jnp.sum(lq1.astype(jnp.float32) * lk1.astype(jnp.float32)))
           - jnp.exp(jnp.sum(lq2.astype(jnp.float32) * lk2.astype(jnp.float32)))
           + lam_init)
    o = diff_attention(q, k, v, lam, cid)
    o = rms_norm(o, subln_w, SUBLN_EPS) * (1.0 - lam_init)
    att_out = o.reshape(B, T, D_ATT) * jax.nn.silu(g_att)

    xc = lax.conv_general_dilated(
        x_lru, conv_w[:, None, :], window_strides=(1,), padding=[(CONV_W - 1, 0)],
        dimension_numbers=('NWC', 'WIO', 'NWC'), feature_group_count=D_LRU) + conv_b
    y = rg_lru(xc, w_r, b_r, w_i, b_i, lru_L)
    lru_out = y * jax.nn.silu(g_lru)

    mixed = jnp.concatenate([att_out, lru_out], axis=-1)
    return h + mixed @ w_out


def setup_inputs(seed: int = 0) -> dict:
    key = jax.random.key(seed)
    ks = jax.random.split(key, 20)
    f32 = jnp.float32
    x = jax.random.normal(ks[0], (BATCH, SEQ, D_MODEL), f32)
    meta_tokens = jax.random.normal(ks[1], (N_META, D_MODEL), f32)
    norm_w = 1.0 + 0.02 * jax.random.normal(ks[2], (DEPTH, D_MODEL), f32)
    w_in = jax.random.normal(ks[3], (DEPTH, D_MODEL, D_IN), f32) * D_MODEL ** -0.5
    conv_w = jax.random.normal(ks[4], (DEPTH, CONV_W, D_LRU), f32) * CONV_W ** -0.5
    conv_b = 0.01 * jax.random.normal(ks[5], (DEPTH, D_LRU), f32)
    w_rgate = jax.random.normal(ks[6], (DEPTH, N_LRU_BLOCKS, LRU_BLOCK, LRU_BLOCK), f32) * LRU_BLOCK ** -0.5
    b_rgate = 0.01 * jax.random.normal(ks[7], (DEPTH, D_LRU), f32)
    w_igate = jax.random.normal(ks[8], (DEPTH, N_LRU_BLOCKS, LRU_BLOCK, LRU_BLOCK), f32) * LRU_BLOCK ** -0.5
    b_igate = 0.01 * jax.random.normal(ks[9], (DEPTH, D_LRU), f32)
    u = jax.random.uniform(ks[10], (DEPTH, D_LRU), f32, 0.9, 0.999)
    s = u ** (1.0 / LRU_C)
    lru_L = jnp.log(s) - jnp.log1p(-s)
    lambda_q1 = 0.1 * jax.random.normal(ks[11], (DEPTH, ATT_QKDIM), f32)
    lambda_k1 = 0.1 * jax.random.normal(ks[12], (DEPTH, ATT_QKDIM), f32)
    lambda_q2 = 0.1 * jax.random.normal(ks[13], (DEPTH, ATT_QKDIM), f32)
    lambda_k2 = 0.1 * jax.random.normal(ks[14], (DEPTH, ATT_QKDIM), f32)
    subln_w = 1.0 + 0.02 * jax.random.normal(ks[15], (DEPTH, ATT_VDIM), f32)
    w_out = jax.random.normal(ks[16], (DEPTH, D_MIX, D_MODEL), f32) * D_MIX ** -0.5
    final_norm_w = 1.0 + 0.02 * jax.random.normal(ks[17], (D_MODEL,), f32)
    return {"x": x, "meta_tokens": meta_tokens, "norm_w": norm_w, "w_in": w_in,
            "conv_w": conv_w, "conv_b": conv_b, "w_rgate": w_rgate, "b_rgate": b_rgate,
            "w_igate": w_igate, "b_igate": b_igate, "lru_L": lru_L,
            "lambda_q1": lambda_q1, "lambda_k1": lambda_k1, "lambda_q2": lambda_q2,
            "lambda_k2": lambda_k2, "subln_w": subln_w, "w_out": w_out,
            "final_norm_w": final_norm_w}


def reference(x, meta_tokens, norm_w, w_in, conv_w, conv_b, w_rgate, b_rgate, w_igate,
              b_igate, lru_L, lambda_q1, lambda_k1, lambda_q2, lambda_k2, subln_w, w_out,
              final_norm_w):
    B = x.shape[0]
    meta = jnp.broadcast_to(meta_tokens[None].astype(x.dtype), (B, N_META, D_MODEL))
    h = jnp.concatenate([meta, x], axis=1)
    T = h.shape[1]
    cos, sin = rope_tables(T)
    cid = chunk_ids(T)
    for l in range(DEPTH):
        lam_init = 0.8 - 0.6 * math.exp(-0.3 * l)
        h = hybrid_layer(h, norm_w[l], w_in[l], conv_w[l], conv_b[l], w_rgate[l], b_rgate[l],
                         w_igate[l], b_igate[l], lru_L[l], lambda_q1[l], lambda_k1[l],
                         lambda_q2[l], lambda_k2[l], subln_w[l], w_out[l], lam_init,
                         cos, sin, cid)
    h = rms_norm(h, final_norm_w, EPS)
    return h[:, N_META:]
```

```python
import numpy as np
import ml_dtypes
import concourse.bass as bass
import concourse.mybir as mybir
from concourse.bass_utils import run_bass_kernel_spmd
from contextlib import ExitStack

F32 = mybir.dt.float32
BF16 = mybir.dt.bfloat16
AF = mybir.ActivationFunctionType
ALU = mybir.AluOpType


class _Op:
    __slots__ = ("eng", "idx", "emit", "is_dma", "deps", "signal", "sigval", "sem", "semval", "prev_same_sem", "is_out")

    def __init__(self, eng, idx, emit, is_dma):
        self.eng = eng
        self.idx = idx
        self.emit = emit
        self.is_dma = is_dma
        self.deps = {}
        self.signal = False
        self.sigval = None
        self.sem = None
        self.semval = None
        self.prev_same_sem = None
        self.is_out = False


class Prog:
    ENGS = ("pe", "act", "dve", "pool", "sp")

    def __init__(self, nc, es, n_dma_sems=24, same_engine_raw=True):
        self.nc = nc
        self.es = es
        self.ops = {e: [] for e in self.ENGS}
        self.buf = {}
        self.same_engine_raw = same_engine_raw
        self.eng_sem = {e: es.enter_context(nc.semaphore("sem_" + e)) for e in ("pe", "act", "dve", "pool")}
        self.dma_sems = [es.enter_context(nc.semaphore("sem_dma%d" % i)) for i in range(n_dma_sems)]
        self.n_dma = 0
        self.dma_last = [None] * n_dma_sems

    def _segs(self, name, lo, hi):
        L = self.buf.get(name, [])
        out, new, cur = [], [], lo
        for s in L:
            if s[1] <= lo or s[0] >= hi:
                new.append(s)
                continue
            if s[0] < lo:
                new.append([s[0], lo, s[2], dict(s[3])])
                s = [lo, s[1], s[2], s[3]]
            tail = None
            if s[1] > hi:
                tail = [hi, s[1], s[2], dict(s[3])]
                s = [s[0], hi, s[2], s[3]]
            if cur < s[0]:
                g = [cur, s[0], None, {}]
                new.append(g)
                out.append(g)
            new.append(s)
            out.append(s)
            cur = s[1]
            if tail is not None:
                new.append(tail)
        if cur < hi:
            g = [cur, hi, None, {}]
            new.append(g)
            out.append(g)
        new.sort(key=lambda s: s[0])
        self.buf[name] = new
        return out

    def _add_dep(self, op, d, kind):
        if d is None or d is op:
            return
        if not d.is_dma and d.eng == op.eng and not op.is_dma:
            if d.eng == "pe":
                return
            if not self.same_engine_raw:
                return
        key = id(d)
        if key not in op.deps:
            op.deps[key] = d

    PSUM_NAMES = ("pj", "psc", "pv", "ptr")

    def _record(self, op, reads, writes):
        def is_psum(name):
            return (name[0] if isinstance(name, tuple) else name) in self.PSUM_NAMES
        excl = [(n, 0, 1 << 30) for (n, lo, hi) in list(reads) + list(writes) if is_psum(n)]
        reads = [r for r in reads if not is_psum(r[0])]
        writes = [w for w in writes if not is_psum(w[0])] + excl
        for (name, lo, hi) in reads:
            for s in self._segs(name, lo, hi):
                self._add_dep(op, s[2], "raw")
                rk = ("dma", id(op)) if op.is_dma else op.eng
                s[3][rk] = op
        for (name, lo, hi) in writes:
            for s in self._segs(name, lo, hi):
                self._add_dep(op, s[2], "waw")
                for r in s[3].values():
                    self._add_dep(op, r, "war")
                s[2] = op
                s[3] = {}

    def op(self, eng, emit, reads=(), writes=()):
        o = _Op(eng, len(self.ops[eng]), emit, False)
        self.ops[eng].append(o)
        self._record(o, reads, writes)
        return o

    def dma(self, out_ap, in_ap, reads=(), writes=(), out=False, **kw):
        def emit(e):
            return e.dma_start(out=out_ap, in_=in_ap, **kw)
        o = _Op("sp", len(self.ops["sp"]), emit, True)
        j = self.n_dma
        self.n_dma += 1
        n = len(self.dma_sems)
        o.sem = self.dma_sems[j % n]
        o.semval = 16 * (j // n + 1)
        o.prev_same_sem = self.dma_last[j % n]
        self.dma_last[j % n] = o
        o.is_out = out
        self.ops["sp"].append(o)
        self._record(o, reads, writes)
        return o

    def _finalize(self):
        self.waits = {}
        for e in self.ENGS:
            waited = {}
            for o in self.ops[e]:
                need = {}
                deps = list(o.deps.values())
                if o.is_dma and o.prev_same_sem is not None:
                    deps.append(o.prev_same_sem)
                for d in deps:
                    if d.is_dma:
                        k = ("dma", id(d.sem))
                        v = d.semval
                    else:
                        k = ("eng", d.eng)
                        v = d.idx
                    if waited.get(k, -1) >= v:
                        continue
                    if k not in need or need[k][0] < v:
                        need[k] = (v, d)
                for k, (v, d) in need.items():
                    waited[k] = v
                    if not d.is_dma:
                        d.signal = True
                self.waits[id(o)] = [d for (_, d) in need.values()]
        for e in ("pe", "act", "dve", "pool"):
            c = 0
            for o in self.ops[e]:
                if o.signal:
                    c += 1
                    o.sigval = c

    def emit(self):
        self._finalize()
        nc = self.nc
        handles = {"pe": nc.tensor, "act": nc.scalar, "dve": nc.vector, "pool": nc.gpsimd, "sp": nc.sync}

        def run(e, h):
            for o in self.ops[e]:
                for d in self.waits[id(o)]:
                    if d.is_dma:
                        h.wait_ge(d.sem, d.semval)
                    else:
                        h.wait_ge(self.eng_sem[d.eng], d.sigval)
                inst = o.emit(h)
                if o.is_dma:
                    inst.then_inc(o.sem, 16)
                elif o.signal:
                    inst.then_inc(self.eng_sem[e], 1)
            if e == "sp":
                for o in self.dma_last:
                    if o is not None:
                        h.wait_ge(o.sem, o.semval)

        with nc.Block() as block:
            @block.tensor
            def _(h):
                run("pe", h)

            @block.scalar
            def _(h):
                run("act", h)

            @block.vector
            def _(h):
                run("dve", h)

            @block.gpsimd
            def _(h):
                run("pool", h)

            @block.sync
            def _(h):
                run("sp", h)


D = 1024
DIN = 3072
NMETA = 16
EPS = 1e-6
SUBLN_EPS = 1e-5
LAM_INIT = 0.8 - 0.6
ROPE_THETA = 500000.0
AX = mybir.AxisListType.X

C_NW, C_CW, C_CB, C_BR, C_BI, C_L, C_LAM, C_SUB, C_FNW, NSM = 0, 8, 24, 28, 32, 36, 40, 296, 808, 1832


def R(name, lo=0, hi=1):
    return (name, lo, hi)


def build_nc(S, NCH):
    SEQ = 512 * NCH
    T = NMETA + SEQ
    NT = SEQ // 128
    nc = bass.Bass("TRN2", target_bir_lowering=False)

    def dram(name, shape, dtype=F32, kind="ExternalInput"):
        return nc.dram_tensor(name, shape, dtype, kind=kind).ap()

    x_d = dram("x", [S, SEQ, D])
    meta_d = dram("meta", [NMETA, D])
    win_d = dram("w_in", [D, DIN])
    wout_d = dram("w_out", [D, D])
    small_d = dram("small", [128, NSM])
    wgate_d = dram("wgate", [128, 1024])
    c16_d = dram("c16", [128, 256], BF16)
    cs_d = dram("cs", [128, 2, T])
    out_d = dram("out", [S, SEQ, D], kind="ExternalOutput")

    with ExitStack() as es:
        P = Prog(nc, es)

        def sb(name, shape, dtype):
            return es.enter_context(nc.sbuf_tensor("s_" + name, shape, dtype))

        def ps(name, shape, dtype):
            return es.enter_context(nc.psum_tensor("p_" + name, shape, dtype))

        win_bf = sb("win_bf", [128, 8, DIN], BF16)
        wout_bf = sb("wout_bf", [128, 8, D], BF16)
        NST = 3
        stage = [sb("stage%d" % i, [128, 1024], F32) for i in range(NST)]
        xn = [sb("xn%d" % i, [128, 1024], BF16) for i in range(2)]
        sm = sb("sm", [128, NSM], F32)
        c16 = sb("c16", [128, 256], BF16)
        wg_bf = sb("wg_bf", [128, 8, 128], BF16)
        kT = sb("kT", [128, 4, T], BF16)
        Vaug = sb("Vaug", [128, (NT + 1) * 4, 130], BF16)
        uT = sb("uT", [128, 8, 512], BF16)
        muT = sb("muT", [128, 8, NMETA], BF16)
        qT = sb("qT", [128, 4, 512], BF16)
        xl = sb("xl", [128, 4, 515], F32)
        xlm = sb("xlm", [128, 4, 3 + NMETA], F32)
        gl = sb("gl", [128, 4, 512], BF16)
        gw = sb("gw", [128, 4, 512], BF16)
        mixT = sb("mixT", [128, 8, 512], BF16)
        att = sb("att", [128, 4, 512], BF16)
        cs = sb("cs", [128, 2, 512], F32)
        csm = sb("csm", [128, 2, NMETA], F32)
        rt = [sb("rt%d" % i, [128, 512], F32) for i in range(2)]
        xc2 = [sb("xc%d" % i, [128, 512], F32) for i in range(2)]
        xcb2 = [sb("xcb%d" % i, [128, 512], BF16) for i in range(2)]
        l_tr = sb("l_tr", [128, 512], F32)
        l_ti = sb("l_ti", [128, 512], F32)
        l_a = sb("l_a", [128, 512], F32)
        l_m = sb("l_m", [128, 512], F32)
        tg = [l_a, l_m]
        TGN = ["l_a", "l_m"]
        NET = 4
        ET = [sb("ET%d" % i, [128, 512], BF16) for i in range(NET)]
        oj = sb("oj", [128, 128], BF16)
        xres = [sb("xres%d" % i, [128, 1024], F32) for i in range(3)]
        sc = sb("sc", [128, 128], F32)
        lt = sb("lt", [128, 128], F32)
        xlm_tail = sb("xlm_tail", [128, 4, 3], F32)
        sqj = sb("sqj", [128, 1024], BF16)
        rsx = [sb("rsx%d" % i, [128, 4], F32) for i in range(2)]
        rsy = [sb("rsy%d" % i, [128, 4], F32) for i in range(2)]
        rsa = [sb("rsa%d" % i, [128, 4], F32) for i in range(2)]

        ident = c16[:, 0:128]
        permT = c16[:, 128:256]
        MHALF, LAM, NLAM, LS, LE, SPE, SP, C8, C4, HB, CARRY, CARRYM = 0, 1, 2, 3, 5, 7, 11, 15, 19, 23, 31, 35
        SSA, RSA, SSE, RSE, SSD, RSD, RL = 80, 84, 44, 46, 48, 56, 64

        def col(i, n=1, rows=128):
            return sc[0:rows, i:i + n]

        pj = [ps("pj%d" % i, [128, 512], F32) for i in range(2)]
        ptr = ps("ptr", [128, 8, 128], BF16)
        psc = [ps("psc%d" % i, [128, 512], F32) for i in range(2)]
        pv = [ps("pv%d" % i, [128, 512], F32) for i in range(3)]

        PJB = [pj[0], pj[1], psc[0], psc[1], pv[0], pv[1], pv[2]]
        PJN = [("pj", 0), ("pj", 1), ("psc", 0), ("psc", 1), ("pv", 0), ("pv", 1), ("pv", 2)]
        NPJ = 7
        cnt = {"st": 0, "pj": 0, "xn": 0, "rt": 0, "tg": 0, "psc": 0, "ET": 0, "ot": 0, "xres": 0, "sl": 0, "sla": 0, "pjc": 0}

        def nxt(key, n):
            v = cnt[key] % n
            cnt[key] += 1
            return v

        P.dma(sm[:], small_d, writes=[R("sm")])
        P.dma(c16[:], c16_d, writes=[R("c16")])
        P.dma(csm[:], cs_d[:, :, 0:NMETA], writes=[R("csm")])
        P.op("pool", lambda e: e.memset(Vaug[:, :, 128:129], 1.0), writes=[R("V", 0, NT + 1)])
        P.op("pool", lambda e: e.memset(xlm[:, :, 0:3], 0.0), writes=[R(("xlm", ct), 0, 3) for ct in range(4)])
        P.op("pool", lambda e: e.memset(col(CARRY, 4), 0.0), writes=[R("carry")])
        P.op("pool", lambda e: e.memset(sc[0:64, 39:40], 0.0), writes=[R("maskb", 0, 1)])
        P.op("pool", lambda e: e.memset(sc[64:128, 39:40], -30000.0), writes=[R("maskb", 1, 2)])
        for i in range(2):
            P.op("dve", lambda e, i=i: e.tensor_tensor(out=lt[:, 64 * i:64 * i + 64], in0=sm[:, C_LAM + 128 * i:C_LAM + 128 * i + 64],
                                                       in1=sm[:, C_LAM + 128 * i + 64:C_LAM + 128 * i + 128], op=ALU.mult),
                 reads=[R("sm")], writes=[R("lt", i, i + 1)])
            P.op("dve", lambda e, i=i: e.reduce_sum(out=col(LS + i), in_=lt[:, 64 * i:64 * i + 64], axis=AX),
                 reads=[R("lt", i, i + 1)], writes=[R("ls", i, i + 1)])
        P.op("act", lambda e: e.activation(out=col(LE, 2), in_=col(LS, 2), func=AF.Exp), reads=[R("ls", 0, 2)], writes=[R("le")])
        P.op("pool", lambda e: e.tensor_tensor(out=col(LAM), in0=col(LE), in1=col(LE + 1), op=ALU.subtract),
             reads=[R("le")], writes=[R("lam")])
        P.op("pool", lambda e: e.tensor_scalar(out=col(NLAM), in0=col(LAM), scalar1=-1.0, scalar2=-LAM_INIT, op0=ALU.mult, op1=ALU.add),
             reads=[R("lam")], writes=[R("nlam")])
        P.op("act", lambda e: e.activation(out=col(SPE, 4), in_=sm[:, C_L:C_L + 4], func=AF.Exp, scale=-1.0),
             reads=[R("sm")], writes=[R("spe")])
        P.op("act", lambda e: e.activation(out=col(SP, 4), in_=col(SPE, 4), func=AF.Ln, bias=1.0, scale=1.0),
             reads=[R("spe")], writes=[R("sp")])
        P.op("pool", lambda e: e.tensor_scalar(out=col(C8, 4), in0=col(SP, 4), scalar1=-8.0, scalar2=None, op0=ALU.mult),
             reads=[R("sp")], writes=[R("c8")])
        P.op("pool", lambda e: e.tensor_scalar(out=col(C4, 4), in0=col(SP, 4), scalar1=-4.0, scalar2=None, op0=ALU.mult),
             reads=[R("sp")], writes=[R("c4")])
        P.op("pool", lambda e: e.tensor_scalar(out=col(HB, 8), in0=sm[:, C_BR:C_BR + 8], scalar1=0.5, scalar2=None, op0=ALU.mult),
             reads=[R("sm")], writes=[R("hb")])
        P.op("pool", lambda e: e.tensor_scalar(out=sm[:, C_SUB:C_SUB + 512], in0=sm[:, C_SUB:C_SUB + 512], scalar1=0.5 * (1.0 - LAM_INIT),
                                               scalar2=None, op0=ALU.mult),
             reads=[R("sm")], writes=[R("subg")])
        sublng = sm[:, C_SUB:C_SUB + 512]
        fnw = sm[:, C_FNW:C_FNW + 1024]
        b = nxt("st", NST)
        P.dma(stage[b][:], wgate_d, writes=[R(("st", b))])
        P.op("pool", lambda e, b=b: e.tensor_copy(out=wg_bf[:].rearrange("p a b -> p (a b)"), in_=stage[b][:]),
             reads=[R(("st", b))], writes=[R("wg")])

        I32 = mybir.dt.int32
        RS_MAGIC = 1597463007.0
        rs_cnt = [0]

        def rsqrt_dve(dst, src, scale, eps, src_regs, dst_regs, rows=128, n=1):
            u = rs_cnt[0] % 2
            rs_cnt[0] += 1
            X = rsx[u][0:rows, 0:n]
            Y = rsy[u][0:rows, 0:n]
            A = rsa[u][0:rows, 0:n]
            rx, ry, ra = R(("rsx", u)), R(("rsy", u)), R(("rsa", u))
            P.op("dve", lambda e: e.tensor_scalar(out=X, in0=src, scalar1=scale, scalar2=eps, op0=ALU.mult, op1=ALU.add),
                 reads=src_regs, writes=[rx])
            P.op("dve", lambda e: e.tensor_scalar(out=Y.bitcast(I32), in0=X.bitcast(I32), scalar1=-0.5, scalar2=RS_MAGIC, op0=ALU.mult, op1=ALU.add),
                 reads=[rx], writes=[ry])
            for it in range(3):
                last = it == 2
                P.op("dve", lambda e: e.tensor_tensor(out=A, in0=Y, in1=Y, op=ALU.mult), reads=[ry], writes=[ra])
                P.op("dve", lambda e: e.tensor_tensor(out=A, in0=A, in1=X, op=ALU.mult), reads=[ra, rx], writes=[ra])
                P.op("dve", lambda e: e.tensor_scalar(out=A, in0=A, scalar1=-0.5, scalar2=1.5, op0=ALU.mult, op1=ALU.add), reads=[ra], writes=[ra])
                if last:
                    P.op("dve", lambda e: e.tensor_tensor(out=dst, in0=Y, in1=A, op=ALU.mult), reads=[ry, ra], writes=dst_regs)
                else:
                    P.op("dve", lambda e: e.tensor_tensor(out=Y, in0=Y, in1=A, op=ALU.mult), reads=[ry, ra], writes=[ry])

        def stage_A1a(src_ap, rows, defer_rs=False):
            b = nxt("st", NST)
            sl = nxt("sla", 4)
            P.dma(stage[b][0:rows, :], src_ap, writes=[R(("st", b))])
            P.op("dve", lambda e: e.scalar_tensor_tensor(out=sqj[0:rows, :], in0=stage[b][0:rows, :], scalar=1.0, in1=stage[b][0:rows, :],
                                                         op0=ALU.mult, op1=ALU.mult, accum_out=col(SSA + sl, 1, rows)),
                 reads=[R(("st", b))], writes=[R("sqj"), R(("ssa", sl))])
            if not defer_rs:
                rsqrt_dve(col(RSA + sl, 1, rows), col(SSA + sl, 1, rows), 1.0 / D, EPS, [R(("ssa", sl))], [R(("rsa", sl))], rows=rows)
            return (b, sl, rows)

        def rsqrt_act(dst, src, scale, eps, src_regs, dst_regs, n=1):
            u = rs_cnt[0] % 2
            rs_cnt[0] += 1
            X = rsx[u][:, 0:n]
            P.op("act", lambda e: e.activation(out=X, in_=src, func=AF.Sqrt, bias=eps, scale=scale), reads=src_regs, writes=[R(("rsx", u))])
            P.op("dve", lambda e: e.reciprocal(out=dst, in_=X), reads=[R(("rsx", u))], writes=dst_regs)

        def stage_A1b(ctx):
            b, sl, rows = ctx
            xb = nxt("xn", 2)
            P.op("dve", lambda e: e.tensor_scalar(out=xn[xb][0:rows, :], in0=stage[b][0:rows, :], scalar1=col(RSA + sl, 1, rows),
                                                  scalar2=None, op0=ALU.mult),
                 reads=[R(("st", b)), R(("rsa", sl))], writes=[R(("xn", xb))])
            return xb

        def stage_A1(src_ap, rows):
            return stage_A1b(stage_A1a(src_ap, rows))

        def stage_A2(xb, rows, dst, dst_name, col0):
            for k in range(8):
                P.op("pe", lambda e, k=k: e.transpose(ptr[:, k, 0:rows], xn[xb][0:rows, 128 * k:128 * k + 128], ident[0:rows, 0:rows]),
                     reads=[R(("xn", xb)), R("c16")], writes=[R("ptr", k, k + 1)])
            P.op("dve", lambda e: e.tensor_copy(out=dst[:, :, col0:col0 + rows], in_=ptr[:, :, 0:rows]),
                 reads=[R("ptr", 0, 8)], writes=[R(dst_name, col0, col0 + rows)])

        def stage_A(src_ap, rows, dst, dst_name, col0):
            b = nxt("st", NST)
            sl = nxt("sla", 4)
            xb = nxt("xn", 2)
            P.dma(stage[b][0:rows, :], src_ap, writes=[R(("st", b))])
            P.op("act", lambda e: e.activation(out=sqj[0:rows, :], in_=stage[b][0:rows, :], func=AF.Square, accum_out=col(SSA + sl, 1, rows)),
                 reads=[R(("st", b))], writes=[R("sqj"), R(("ssa", sl))])
            u = rs_cnt[0] % 2
            rs_cnt[0] += 1
            X = rsx[u][0:rows, 0:1]
            P.op("act", lambda e: e.activation(out=X, in_=col(SSA + sl, 1, rows), func=AF.Sqrt, bias=EPS, scale=1.0 / D),
                 reads=[R(("ssa", sl))], writes=[R(("rsx", u))])
            P.op("dve", lambda e: e.reciprocal(out=col(RSA + sl, 1, rows), in_=X), reads=[R(("rsx", u))], writes=[R(("rsa", sl))])
            P.op("act", lambda e: e.activation(out=xn[xb][0:rows, :], in_=stage[b][0:rows, :], func=AF.Copy, scale=col(RSA + sl, 1, rows)),
                 reads=[R(("st", b)), R(("rsa", sl))], writes=[R(("xn", xb))])
            stage_A2(xb, rows, dst, dst_name, col0)

        def weight_pieces():
            L = []
            cntw = [0]

            def win_piece(cg, k):
                def f():
                    b = cntw[0] % 3
                    cntw[0] += 1
                    P.dma(xres[b][:], win_d[128 * k:128 * k + 128, 1024 * cg:1024 * cg + 1024], writes=[R(("xres", b))])
                    if cntw[0] % 2 == 0:
                        P.op("dve", lambda e: e.tensor_scalar(out=win_bf[:, k, 1024 * cg:1024 * cg + 1024], in0=xres[b][:],
                                                              scalar1=sm[:, C_NW + k:C_NW + k + 1], scalar2=None, op0=ALU.mult),
                             reads=[R(("xres", b)), R("sm")], writes=[R(("win", k), 1024 * cg, 1024 * cg + 1024)])
                    else:
                        P.op("act", lambda e: e.activation(out=win_bf[:, k, 1024 * cg:1024 * cg + 1024], in_=xres[b][:],
                                                           func=AF.Copy, scale=sm[:, C_NW + k:C_NW + k + 1]),
                             reads=[R(("xres", b)), R("sm")], writes=[R(("win", k), 1024 * cg, 1024 * cg + 1024)])
                return f

            def wout_piece(k):
                def f():
                    b = cntw[0] % 3
                    cntw[0] += 1
                    P.dma(xres[b][:], wout_d[128 * k:128 * k + 128, :], writes=[R(("xres", b))])
                    if k % 2 == 0:
                        P.op("dve", lambda e: e.tensor_scalar(out=wout_bf[:, k, :], in0=xres[b][:], scalar1=(1.0 if k < 4 else 0.5),
                                                              scalar2=None, op0=ALU.mult),
                             reads=[R(("xres", b))], writes=[R(("wout", k), 0, 1024)])
                    else:
                        P.op("act", lambda e: e.activation(out=wout_bf[:, k, :], in_=xres[b][:], func=AF.Copy, scale=(1.0 if k < 4 else 0.5)),
                             reads=[R(("xres", b))], writes=[R(("wout", k), 0, 1024)])
                return f

            for cg in range(3):
                for k in range(8):
                    L.append(win_piece(cg, k))
            for k in range(8):
                L.append(wout_piece(k))
            return L

        def proj_group(lhs_fn, rhs_fn, outrows, N, reads_fn):
            b = nxt("pj", NPJ)
            for k in range(8):
                P.op("pe", lambda e, k=k: e.matmul(PJB[b][0:outrows, 0:N], lhsT=lhs_fn(k), rhs=rhs_fn(k), start=(k == 0), stop=(k == 7)),
                     reads=reads_fn(k), writes=[R(PJN[b])])
            return b

        def stage_B(u, u_name, N, p0, meta, tiles, after_xl0=None):
            cos = csm[:, 0, :] if meta else cs[:, 0, :]
            sin = csm[:, 1, :] if meta else cs[:, 1, :]
            cs_name = "csm" if meta else "cs"

            def fm(c0):
                return proj_group(lambda k: win_bf[:, k, c0:c0 + 128], lambda k: u[:, k, 0:N], 128, N,
                                  lambda k: [R(("win", k), c0, c0 + 128), R(u_name, 0, N)])

            ropes = []
            for h in range(4):
                b = fm(512 + 128 * h)
                dest = kT[:, h, p0:p0 + N]
                dreg = R(("kT", h), p0, p0 + N)
                P.op("act", lambda e, b=b, dest=dest: e.activation(out=dest, in_=PJB[b][:, 0:N], func=AF.Copy),
                     reads=[R(PJN[b])], writes=[dreg])
                ropes.append((dest, dreg))
            if not meta:
                for h in range(4):
                    b = fm(128 * h)
                    dest = qT[:, h, 0:N]
                    dreg = R(("qT", h), 0, N)
                    P.op("act", lambda e, b=b, dest=dest: e.activation(out=dest, in_=PJB[b][:, 0:N], func=AF.Copy),
                         reads=[R(PJN[b])], writes=[dreg])
                    ropes.append((dest, dreg))

            def do_rope(dest, dreg):
                b2 = nxt("pj", NPJ)
                r = nxt("rt", 1)
                t1, t2 = rt[2 * r], rt[2 * r + 1]
                P.op("pe", lambda e: e.matmul(PJB[b2][:, 0:N], lhsT=permT, rhs=dest, start=True, stop=True),
                     reads=[dreg, R("c16")], writes=[R(PJN[b2])])
                P.op("pool", lambda e: e.tensor_tensor(out=t1[:, 0:N], in0=dest, in1=cos, op=ALU.mult),
                     reads=[dreg, R(cs_name)], writes=[R(("rt", 2 * r), 0, 512)])
                P.op("dve", lambda e: e.tensor_tensor(out=t2[:, 0:N], in0=PJB[b2][:, 0:N], in1=sin, op=ALU.mult),
                     reads=[R(PJN[b2]), R(cs_name)], writes=[R(("rt", 2 * r + 1), 0, 512)])
                P.op("pool", lambda e: e.tensor_tensor(out=dest, in0=t1[:, 0:N], in1=t2[:, 0:N], op=ALU.add),
                     reads=[R(("rt", 2 * r), 0, 512), R(("rt", 2 * r + 1), 0, 512)], writes=[dreg])

            for (rows, vt, c0l) in tiles:
                b = proj_group(lambda k, c0l=c0l, rows=rows: u[:, k, c0l:c0l + rows], lambda k: win_bf[:, k, 1024:1536], rows, 512,
                               lambda k: [R(("win", k), 1024, 1536), R(u_name, c0l, c0l + rows)])
                P.op("dve", lambda e, b=b, rows=rows, vt=vt: e.tensor_copy(out=Vaug[0:rows, 4 * vt:4 * vt + 4, 0:128],
                                                                          in_=PJB[b][0:rows, :].rearrange("p (h d) -> p h d", h=4)),
                     reads=[R(PJN[b])], writes=[R("V", vt, vt + 1)])
            for (dest, dreg) in ropes[0:4]:
                do_rope(dest, dreg)
            if not meta:
                for jl, (rows, vt, c0l) in enumerate(tiles):
                    b = proj_group(lambda k, c0l=c0l, rows=rows: u[:, k, c0l:c0l + rows], lambda k: win_bf[:, k, 1536:2048], rows, 512,
                                   lambda k: [R(("win", k), 1536, 2048), R(u_name, c0l, c0l + rows)])
                    g = nxt("tg", 2)
                    P.op("act", lambda e, b=b, g=g: e.activation(out=tg[g][:], in_=PJB[b][:, :], func=AF.Tanh, scale=0.5),
                         reads=[R(PJN[b])], writes=[R(TGN[g])])
                    P.op("dve", lambda e, b=b, g=g: e.scalar_tensor_tensor(out=tg[g][:], in0=tg[g][:], scalar=1.0, in1=PJB[b][:, :],
                                                                           op0=ALU.add, op1=ALU.mult),
                         reads=[R(PJN[b]), R(TGN[g])], writes=[R(TGN[g])])
                    P.op("pool", lambda e, g=g, jl=jl: e.tensor_tensor(out=gw[:, jl, :], in0=tg[g][:], in1=sublng, op=ALU.mult),
                         reads=[R(TGN[g]), R("subg")], writes=[R(("gw", jl), 0, 512)])
                for (dest, dreg) in ropes[4:8]:
                    do_rope(dest, dreg)
            for ct in range(4):
                if ct == 1 and after_xl0 is not None:
                    after_xl0()
                b = fm(2048 + 128 * ct)
                P.op("act", lambda e, b=b, ct=ct: e.activation(out=(xlm if meta else xl)[:, ct, 3:3 + N], in_=PJB[b][:, 0:N], func=AF.Copy),
                     reads=[R(PJN[b])], writes=[R(("xlm" if meta else "xl", ct), 3, 3 + N)])
            if not meta:
                for ct in range(4):
                    b = fm(2560 + 128 * ct)
                    g = nxt("tg", 2)
                    P.op("act", lambda e, b=b, g=g: e.activation(out=tg[g][:, 0:N], in_=PJB[b][:, 0:N], func=AF.Tanh, scale=0.5),
                         reads=[R(PJN[b])], writes=[R(TGN[g])])
                    P.op("dve", lambda e, b=b, g=g, ct=ct: e.scalar_tensor_tensor(out=gl[:, ct, 0:N], in0=tg[g][:, 0:N], scalar=1.0,
                                                                                 in1=PJB[b][:, 0:N], op0=ALU.add, op1=ALU.mult),
                         reads=[R(PJN[b]), R(TGN[g])], writes=[R(("gl", ct), 0, N)])

        def stage_C_conv(ct, N, meta, xl=xl):
            xc, xcb = xc2[ct % 2], xcb2[ct % 2]
            xcn, xcbn = ("xc", ct % 2), ("xcb", ct % 2)
            xln = "xlm" if meta else "xl"
            if meta:
                xl = xlm
            xlr = R((xln, ct), 0, 3 + N)
            P.op("pool", lambda e: e.tensor_scalar(out=xc[:, 0:N], in0=xl[:, ct, 0:N], scalar1=sm[:, C_CW + 4 * ct:C_CW + 4 * ct + 1],
                                                   scalar2=sm[:, C_CB + ct:C_CB + ct + 1], op0=ALU.mult, op1=ALU.add),
                 reads=[xlr, R("sm")], writes=[R(xcn)])
            for j in range(1, 4):
                P.op("dve", lambda e, j=j: e.scalar_tensor_tensor(out=xc[:, 0:N], in0=xl[:, ct, j:j + N],
                                                                  scalar=sm[:, C_CW + 4 * ct + j:C_CW + 4 * ct + j + 1],
                                                                  in1=xc[:, 0:N], op0=ALU.mult, op1=ALU.add),
                     reads=[xlr, R("sm"), R(xcn)], writes=[R(xcn)])
            if meta:
                P.op("pool", lambda e: e.tensor_copy(out=xlm_tail[:, ct, :], in_=xl[:, ct, N:N + 3]),
                     reads=[R((xln, ct), N, N + 3)], writes=[R("xlm_tail", ct, ct + 1)])
            else:
                P.op("pool", lambda e: e.tensor_copy(out=xl[:, ct, 0:3], in_=xl[:, ct, N:N + 3]),
                     reads=[R(("xl", ct), N, N + 3)], writes=[R(("xl", ct), 0, 3)])
            P.op("pool", lambda e: e.tensor_copy(out=xcb[:, 0:N], in_=xc[:, 0:N]), reads=[R(xcn)], writes=[R(xcbn)])

        def stage_C_rest(ct, N, meta, in_hook=False):
            xc, xcb = xc2[ct % 2], xcb2[ct % 2]
            xcn, xcbn = ("xc", ct % 2), ("xcb", ct % 2)
            for gi, dst, dname in ((0, l_tr, "l_tr"), (1, l_ti, "l_ti")):
                b = nxt("pjc", 2) if in_hook else nxt("pj", NPJ)
                P.op("pe", lambda e, b=b, gi=gi: e.matmul(PJB[b][:, 0:N], lhsT=wg_bf[:, 4 * gi + ct, :], rhs=xcb[:, 0:N],
                                                         start=True, stop=True),
                     reads=[R("wg"), R(xcbn)], writes=[R(PJN[b])])
                P.op("act", lambda e, b=b, gi=gi, dst=dst: e.activation(out=dst[:, 0:N], in_=PJB[b][:, 0:N], func=AF.Tanh,
                                                                       bias=col(HB + 4 * gi + ct), scale=0.5),
                     reads=[R(PJN[b]), R("hb")], writes=[R(dname)])
            P.op("act", lambda e: e.activation(out=l_a[:, 0:N], in_=l_tr[:, 0:N], func=AF.Exp, bias=col(C4 + ct), scale=col(C4 + ct)),
                 reads=[R("l_tr"), R("c4")], writes=[R("l_a")])
            P.op("pool", lambda e: e.tensor_tensor(out=l_m[:, 0:N], in0=l_a[:, 0:N], in1=l_a[:, 0:N], op=ALU.mult),
                 reads=[R("l_a")], writes=[R("l_m")])
            P.op("act", lambda e: e.activation(out=l_m[:, 0:N], in_=l_m[:, 0:N], func=AF.Sqrt, bias=0.25, scale=-0.25),
                 reads=[R("l_m")], writes=[R("l_m")])
            if meta:
                P.op("pool", lambda e: e.memset(l_m[:, 0:1], 0.5), reads=[], writes=[R("l_m")])
            P.op("dve", lambda e: e.scalar_tensor_tensor(out=l_ti[:, 0:N], in0=l_ti[:, 0:N], scalar=1.0, in1=xc[:, 0:N],
                                                         op0=ALU.add, op1=ALU.mult),
                 reads=[R("l_ti"), R(xcn)], writes=[R("l_ti")])
            P.op("pool", lambda e: e.tensor_tensor(out=l_ti[:, 0:N], in0=l_ti[:, 0:N], in1=l_m[:, 0:N], op=ALU.mult),
                 reads=[R("l_ti"), R("l_m")], writes=[R("l_ti")])
            P.op("dve", lambda e: e.tensor_tensor_scan(out=l_tr[:, 0:N], data0=l_a[:, 0:N], data1=l_ti[:, 0:N],
                                                       initial=col(CARRY + ct), op0=ALU.mult, op1=ALU.add),
                 reads=[R("l_a"), R("l_ti"), R("carry", ct, ct + 1)], writes=[R("l_tr")])
            P.op("pool", lambda e: e.tensor_copy(out=col(CARRY + ct), in_=l_tr[:, N - 1:N]),
                 reads=[R("l_tr")], writes=[R("carry", ct, ct + 1)])
            if meta:
                P.op("pool", lambda e: e.tensor_copy(out=col(CARRYM + ct), in_=l_tr[:, N - 1:N]),
                     reads=[R("l_tr")], writes=[R("carrym", ct, ct + 1)])
            else:
                P.op("pool", lambda e: e.tensor_tensor(out=mixT[:, 4 + ct, 0:N], in0=l_tr[:, 0:N], in1=gl[:, ct, 0:N], op=ALU.mult),
                     reads=[R("l_tr"), R(("gl", ct), 0, N)], writes=[R(("mixT", 4 + ct), 0, N)])

        def stage_C(N, meta):
            for ct in range(4):
                stage_C_conv(ct, N, meta)
                stage_C_rest(ct, N, meta)

        def c_pieces(ct, N):
            xc, xcb = xc2[ct % 2], xcb2[ct % 2]
            xcn, xcbn = ("xc", ct % 2), ("xcb", ct % 2)

            def gate(gi, dst, dname):
                def f():
                    b = nxt("pjc", 2)
                    P.op("pe", lambda e: e.matmul(PJB[b][:, 0:N], lhsT=wg_bf[:, 4 * gi + ct, :], rhs=xcb[:, 0:N], start=True, stop=True),
                         reads=[R("wg"), R(xcbn)], writes=[R(PJN[b])])
                    P.op("act", lambda e: e.activation(out=dst[:, 0:N], in_=PJB[b][:, 0:N], func=AF.Tanh, bias=col(HB + 4 * gi + ct), scale=0.5),
                         reads=[R(PJN[b]), R("hb")], writes=[R(dname)])
                return f

            def p_exp():
                P.op("act", lambda e: e.activation(out=l_a[:, 0:N], in_=l_tr[:, 0:N], func=AF.Exp, bias=col(C4 + ct), scale=col(C4 + ct)),
                     reads=[R("l_tr"), R("c4")], writes=[R("l_a")])
                P.op("pool", lambda e: e.tensor_tensor(out=l_m[:, 0:N], in0=l_a[:, 0:N], in1=l_a[:, 0:N], op=ALU.mult),
                     reads=[R("l_a")], writes=[R("l_m")])
                P.op("dve", lambda e: e.scalar_tensor_tensor(out=l_ti[:, 0:N], in0=l_ti[:, 0:N], scalar=1.0, in1=xc[:, 0:N],
                                                             op0=ALU.add, op1=ALU.mult),
                     reads=[R("l_ti"), R(xcn)], writes=[R("l_ti")])

            def p_sqrt():
                P.op("act", lambda e: e.activation(out=l_m[:, 0:N], in_=l_m[:, 0:N], func=AF.Sqrt, bias=0.25, scale=-0.25),
                     reads=[R("l_m")], writes=[R("l_m")])

            def p_tail():
                P.op("pool", lambda e: e.tensor_tensor(out=l_ti[:, 0:N], in0=l_ti[:, 0:N], in1=l_m[:, 0:N], op=ALU.mult),
                     reads=[R("l_ti"), R("l_m")], writes=[R("l_ti")])
                P.op("dve", lambda e: e.tensor_tensor_scan(out=l_tr[:, 0:N], data0=l_a[:, 0:N], data1=l_ti[:, 0:N],
                                                           initial=col(CARRY + ct), op0=ALU.mult, op1=ALU.add),
                     reads=[R("l_a"), R("l_ti"), R("carry", ct, ct + 1)], writes=[R("l_tr")])
                P.op("pool", lambda e: e.tensor_copy(out=col(CARRY + ct), in_=l_tr[:, N - 1:N]),
                     reads=[R("l_tr")], writes=[R("carry", ct, ct + 1)])
                P.op("pool", lambda e: e.tensor_tensor(out=mixT[:, 4 + ct, 0:N], in0=l_tr[:, 0:N], in1=gl[:, ct, 0:N], op=ALU.mult),
                     reads=[R("l_tr"), R(("gl", ct), 0, N)], writes=[R(("mixT", 4 + ct), 0, N)])

            return [gate(0, l_tr, "l_tr"), gate(1, l_ti, "l_ti"), p_exp, p_sqrt, p_tail]

        def att_T(h):
            for jl in range(4):
                P.op("pe", lambda e, jl=jl: e.transpose(ptr[:, jl, :], att[:, jl, 128 * h:128 * h + 128], ident),
                     reads=[R(("att", jl), 128 * h, 128 * h + 128), R("c16")], writes=[R("ptr", jl, jl + 1)])
            P.op("dve", lambda e: e.tensor_copy(out=mixT[:, h, :], in_=ptr[:, 0:4, :].rearrange("p j n -> p (j n)")),
                 reads=[R("ptr", 0, 4)], writes=[R(("mixT", h), 0, 512)])

        def stage_D(c, pieces_for=None):
            ktiles = [None] + list(range(4 * c + 4))
            def head(h, pieces):
                items = [(kt, m) for kt in ktiles for m in range(2)]
                started = set()
                info = {}

                def emit_scores(n):
                    kt, m = items[n]
                    b = nxt("psc", 2)
                    if kt is None:
                        K, kp0, qlo = NMETA, 0, 0
                    else:
                        K, kp0, qlo = 128, NMETA + 128 * kt, max(0, kt - 4 * c)
                    Nq = 512 - 128 * qlo
                    P.op("pe", lambda e: e.matmul(psc[b][0:K, 0:Nq], lhsT=kT[64 * m:64 * m + 64, h, kp0:kp0 + K],
                                                  rhs=qT[64 * m:64 * m + 64, h, 128 * qlo:512], start=True, stop=True),
                         reads=[R(("kT", h), kp0, kp0 + K), R(("qT", h), 128 * qlo, 512)], writes=[R(("psc", b))])
                    et = nxt("ET", NET)
                    if kt is not None and kt >= 4 * c:
                        P.op("act", lambda e: e.activation(out=ET[et][:, 0:64], in_=psc[b][:, 0:64], func=AF.Exp, scale=0.125, bias=sc[:, 39:40]),
                             reads=[R(("psc", b)), R("maskb", 0, 2)], writes=[R(("ET", et), 0, 64)])
                        P.op("act", lambda e: e.activation(out=ET[et][:, 64:Nq], in_=psc[b][:, 64:Nq], func=AF.Exp, scale=0.125),
                             reads=[R(("psc", b))], writes=[R(("ET", et), 64, 512)])
                    else:
                        P.op("act", lambda e: e.activation(out=ET[et][0:K, 0:Nq], in_=psc[b][0:K, 0:Nq], func=AF.Exp, scale=0.125),
                             reads=[R(("psc", b))], writes=[R(("ET", et), 0, 512)])
                    info[n] = (et, K, qlo)

                def emit_pv(n):
                    kt, m = items[n]
                    et, K, qlo = info[n]
                    vt = 0 if kt is None else kt + 1
                    for jl in range(qlo, 4):
                        a = 2 * jl + m
                        bank, off = a // 3, 130 * (a % 3)
                        first = bank not in started
                        started.add(bank)
                        last = (kt is not None) and (kt == 4 * c + jl)
                        P.op("pe", lambda e, jl=jl, bank=bank, off=off, first=first, last=last: e.matmul(
                            pv[bank][:, off:off + 129], lhsT=ET[et][0:K, 128 * (jl - qlo):128 * (jl - qlo) + 128],
                            rhs=Vaug[0:K, 4 * vt + h, 0:129], start=first, stop=last, skip_group_check=True),
                            reads=[R(("ET", et), 0, 512), R("V", vt, vt + 1)], writes=[R(("pv", bank), off, off + 129)])

                emit_scores(0)
                emit_scores(1)
                pi = 0
                pstride = max(1, (len(items) - 2) // max(1, len(pieces)))
                for n in range(len(items)):
                    emit_pv(n)
                    if n + 2 < len(items):
                        emit_scores(n + 2)
                    if n >= 1 and (n - 1) % pstride == 0 and pi < len(pieces):
                        pieces[pi]()
                        pi += 1
                sl = nxt("sl", 2)
                accs = []
                for a in range(8):
                    bank, off = a // 3, 130 * (a % 3)
                    raw = rt[a // 4][:, 128 * (a % 4):128 * (a % 4) + 128]
                    rawr = R(("rt", a // 4), 128 * (a % 4), 128 * (a % 4) + 128)
                    accs.append((None, None, raw, rawr, R(("rl", sl), a, a + 1), RL + 8 * sl + a))
                for bank, na in ((0, 3), (1, 3), (2, 2)):
                    a0 = 3 * bank
                    P.op("dve", lambda e, bank=bank, na=na, a0=a0: e.reciprocal(
                        out=col(RL + 8 * sl + a0, na), in_=pv[bank][:, 0:130 * na].rearrange("p (a c) -> p a c", c=130)[:, :, 128]),
                        reads=[R(("pv", bank), 0, 130 * na)], writes=[R(("rl", sl), a0, a0 + na)])
                def rawslots(lo, n):
                    t = rt[lo // 4]
                    return t[:, 128 * (lo % 4):128 * (lo % 4) + 128 * n].rearrange("p (a c) -> p a c", c=128)

                def accslots(bank, j0, n):
                    return pv[bank][:, 130 * j0:130 * (j0 + n)].rearrange("p (a c) -> p a c", c=130)[:, :, 0:128]

                for eng, bank, j0, n, lo in (("act", 0, 0, 3, 0), ("dve", 1, 0, 1, 3), ("dve", 1, 1, 2, 4), ("act", 2, 0, 2, 6)):
                    rd = [R(("pv", bank), 130 * j0, 130 * (j0 + n))]
                    wr = [R(("rt", lo // 4), 128 * (lo % 4), 128 * (lo % 4) + 128 * n)]
                    if eng == "act":
                        P.op("act", lambda e, bank=bank, j0=j0, n=n, lo=lo: e.activation(out=rawslots(lo, n), in_=accslots(bank, j0, n), func=AF.Copy),
                             reads=rd, writes=wr)
                    else:
                        P.op("dve", lambda e, bank=bank, j0=j0, n=n, lo=lo: e.tensor_copy(out=rawslots(lo, n), in_=accslots(bank, j0, n)),
                             reads=rd, writes=wr)
                while pi < len(pieces):
                    pieces[pi]()
                    pi += 1

                def p_norm():
                    for jl in range(4):
                        (acc0, rr0, raw0, rawr0, rl0, c0) = accs[2 * jl]
                        (acc1, rr1, raw1, rawr1, rl1, c1) = accs[2 * jl + 1]
                        P.op("dve", lambda e, c1=c1: e.tensor_tensor(out=col(c1), in0=col(c1), in1=col(NLAM), op=ALU.mult),
                             reads=[rl1, R("nlam")], writes=[rl1])
                        P.op("dve", lambda e, raw0=raw0, c0=c0: e.tensor_scalar(out=raw0, in0=raw0, scalar1=col(c0), scalar2=None, op0=ALU.mult),
                             reads=[rawr0, rl0], writes=[rawr0])
                        P.op("dve", lambda e, raw0=raw0, raw1=raw1, c1=c1: e.scalar_tensor_tensor(out=raw0, in0=raw1, scalar=col(c1), in1=raw0,
                                                                                                op0=ALU.mult, op1=ALU.add),
                             reads=[rawr1, rl1, rawr0], writes=[rawr0])

                o_ap = [accs[2 * jl][2] for jl in range(4)]
                o_rg = [accs[2 * jl][3] for jl in range(4)]

                class Post:
                    pass
                post = Post()
                rs_args = (col(RSD + 4 * sl, 4), col(SSD + 4 * sl, 4), 1.0 / 128, SUBLN_EPS, [R(("ssd", sl), 0, 4)], [R(("rsd", sl), 0, 4)])

                def p_sq():
                    p_norm()
                    for jl in range(4):
                        P.op("act", lambda e, jl=jl: e.activation(out=oj[:], in_=o_ap[jl], func=AF.Square, accum_out=col(SSD + 4 * sl + jl)),
                             reads=[o_rg[jl]], writes=[R("oj"), R(("ssd", sl), jl, jl + 1)])

                def p_stt():
                    for jl in range(4):
                        P.op("dve", lambda e, jl=jl: e.scalar_tensor_tensor(out=att[:, jl, 128 * h:128 * h + 128], in0=o_ap[jl],
                                                                            scalar=col(RSD + 4 * sl + jl), in1=gw[:, jl, 128 * h:128 * h + 128],
                                                                            op0=ALU.mult, op1=ALU.mult),
                             reads=[o_rg[jl], R(("rsd", sl), jl, jl + 1), R(("gw", jl), 128 * h, 128 * h + 128)],
                             writes=[R(("att", jl), 128 * h, 128 * h + 128)])

                def p_full():
                    p_sq()
                    rsqrt_dve(*rs_args, n=4)
                    p_stt()

                post.sq, post.rs_args, post.stt, post.full = p_sq, rs_args, p_stt, p_full
                return post

            pend = None
            for h in range(4):
                pc = pieces_for(h)
                L = []
                if pend is not None:
                    L.append(pend.sq)
                L += [pc["g0"], pc["g1"], pc["p_exp"]]
                if pc.get("A2"):
                    L.append(pc["A2"])

                def sqrt_window(pc=pc, pend=pend):
                    pc["p_sqrt"]()
                    if pend is not None:
                        rsqrt_act(*pend.rs_args, n=4)
                    if pc.get("a1a_rs"):
                        rsqrt_act(*pc["a1a_rs"]())
                L.append(sqrt_window)
                if pend is not None:
                    L.append(pend.stt)
                if h >= 1:
                    L.append(lambda h=h: att_T(h - 1))
                if pc.get("A1b"):
                    L.append(pc["A1b"])
                if pc.get("A1a"):
                    L.append(pc["A1a"])
                if pc.get("conv"):
                    L.append(pc["conv"])
                L.append(pc["p_tail"])
                pend = head(h, L)
            pend.full()


        xr = {}

        def e_load(s, c, jl):
            r0 = 512 * c + 128 * jl
            xb = nxt("xres", 3)
            P.dma(xres[xb][:], x_d[s, r0:r0 + 128, :], writes=[R(("xres", xb))])
            xr[jl] = xb

        KORD = (4, 5, 6, 0, 1, 2, 7, 3)

        def stage_E(s, c, mid=None):
            hbs = {}
            for jl in range(3):
                hbs[jl] = [nxt("pj", NPJ), nxt("pj", NPJ)]
                e_mm(jl, hbs[jl], range(0, 7))
            if mid is not None:
                mid()
            for jl in range(3):
                e_mm(jl, hbs[jl], range(7, 8))
                e_adds(jl, hbs[jl])
            e_norm_store(s, c, 0)
            hbs[3] = [nxt("pj", NPJ), nxt("pj", NPJ)]
            e_mm(3, hbs[3], range(0, 8))
            e_adds(3, hbs[3])
            for jl in range(1, 4):
                e_norm_store(s, c, jl)

        def e_mm(jl, hb, kis):
            for half in range(2):
                for ki in kis:
                    k = KORD[ki]
                    P.op("pe", lambda e, half=half, k=k, ki=ki: e.matmul(PJB[hb[half]][:, :], lhsT=mixT[:, k, 128 * jl:128 * jl + 128],
                                                                        rhs=wout_bf[:, k, 512 * half:512 * half + 512], start=(ki == 0), stop=(ki == 7)),
                         reads=[R(("mixT", k), 128 * jl, 128 * jl + 128), R(("wout", k), 512 * half, 512 * half + 512)],
                         writes=[R(PJN[hb[half]])])

        def e_adds(jl, hb):
            xb = xr[jl]
            for half in range(2):
                P.op("dve", lambda e, half=half: e.tensor_tensor(out=xres[xb][:, 512 * half:512 * half + 512], in0=PJB[hb[half]][:, :],
                                                                 in1=xres[xb][:, 512 * half:512 * half + 512], op=ALU.add),
                     reads=[R(PJN[hb[half]]), R(("xres", xb))], writes=[R(("xres", xb))])

        def e_norm_store(s, c, jl):
            r0 = 512 * c + 128 * jl
            xb = xr[jl]
            sl = nxt("sl", 2)
            P.op("act", lambda e: e.activation(out=sqj[:], in_=xres[xb][:], func=AF.Square, accum_out=col(SSE + sl)),
                 reads=[R(("xres", xb))], writes=[R("sqj"), R(("sse", sl))])
            rsqrt_dve(col(RSE + sl), col(SSE + sl), 1.0 / D, EPS, [R(("sse", sl))], [R(("rse", sl))])
            P.op("act", lambda e: e.activation(out=xres[xb][:], in_=xres[xb][:], func=AF.Copy, scale=col(RSE + sl)),
                 reads=[R(("xres", xb)), R(("rse", sl))], writes=[R(("xres", xb))])
            P.op("pool", lambda e: e.tensor_tensor(out=xres[xb][:], in0=xres[xb][:], in1=fnw, op=ALU.mult),
                 reads=[R(("xres", xb)), R("sm")], writes=[R(("xres", xb))])
            P.dma(out_d[s, r0:r0 + 128, :], xres[xb][:], reads=[R(("xres", xb))], out=True)
            if jl == 0:
                e_load(s, c, 3)

        chunks = [(s, c) for s in range(S) for c in range(NCH)]

        def x_tile(i, jl):
            s, c = chunks[i]
            r0 = 512 * c + 128 * jl
            return x_d[s, r0:r0 + 128, :]

        wp = weight_pieces()
        stage_A(meta_d, NMETA, muT, "muT", 0)
        for jl in range(4):
            for f in wp[4 * jl:4 * jl + 4]:
                f()
            stage_A(x_tile(0, jl), 128, uT, "uT", 128 * jl)
        for f in wp[16:]:
            f()
        def restore_state():
            P.op("pool", lambda e: e.tensor_copy(out=xl[:, :, 0:3], in_=xlm_tail[:, :, :]),
                 reads=[R("xlm_tail", 0, 4)], writes=[R(("xl", ct), 0, 3) for ct in range(4)])
            P.op("pool", lambda e: e.tensor_copy(out=col(CARRY, 4), in_=col(CARRYM, 4)),
                 reads=[R("carrym", 0, 4)], writes=[R("carry", 0, 4)])

        for i, (s, c) in enumerate(chunks):
            P.dma(cs[:], cs_d[:, :, NMETA + 512 * c:NMETA + 512 * c + 512], writes=[R("cs")])
            if i == 0:
                def meta_pass():
                    stage_B(muT, "muT", NMETA, 0, True, [(NMETA, 0, 0)])
                    stage_C(NMETA, True)
                    restore_state()
                    stage_C_conv(0, 512, False)

                stage_B(uT, "uT", 512, NMETA + 512 * c, False, [(128, 1 + 4 * c + jl, 128 * jl) for jl in range(4)], after_xl0=meta_pass)
            if c == 0 and i > 0:
                restore_state()
            if i > 0:
                stage_B(uT, "uT", 512, NMETA + 512 * c, False, [(128, 1 + 4 * c + jl, 128 * jl) for jl in range(4)],
                        after_xl0=lambda: stage_C_conv(0, 512, False))
            nxt_i = i + 1 if i + 1 < len(chunks) else None
            st = {}

            def a_pieces(k, nxt_i=nxt_i, st=st):
                Dd = {}
                if nxt_i is None:
                    return Dd
                if 2 <= k <= 5:
                    Dd["A2"] = lambda: stage_A2(st[("xb", k - 2)], 128, uT, "uT", 128 * (k - 2))
                if 1 <= k <= 4:
                    def f1():
                        st[("xb", k - 1)] = stage_A1b(st[("a", k - 1)])
                    Dd["A1b"] = f1
                    if k <= 3:
                        def rs():
                            b, sl, rows = st[("a", k - 1)]
                            return (col(RSA + sl), col(SSA + sl), 1.0 / D, EPS, [R(("ssa", sl))], [R(("rsa", sl))])
                        Dd["a1a_rs"] = rs
                if k <= 3:
                    def f2():
                        st[("a", k)] = stage_A1a(x_tile(nxt_i, k), 128, defer_rs=(k <= 2))
                    Dd["A1a"] = f2
                return Dd

            def pieces_for(h):
                g0, g1, p_exp, p_sqrt, p_tail = c_pieces(h, 512)
                pc = {"g0": g0, "g1": g1, "p_exp": p_exp, "p_sqrt": p_sqrt, "p_tail": p_tail}
                if h < 3:
                    pc["conv"] = lambda: stage_C_conv(h + 1, 512, False)
                pc.update(a_pieces(h))
                return pc

            for jl in range(3):
                e_load(s, c, jl)
            stage_D(c, pieces_for)
            ap4 = a_pieces(4)
            for key in ("A2", "A1b"):
                if key in ap4:
                    ap4[key]()

            def mid():
                att_T(3)
                ap5 = a_pieces(5)
                if "A2" in ap5:
                    ap5["A2"]()

            stage_E(s, c, mid=mid)
        P.emit()
    return nc


def _host_layout(inputs, n_cores):
    f32 = np.float32
    x = np.asarray(inputs["x"], dtype=f32)
    B, SEQ, _ = x.shape
    S = B // n_cores
    T = NMETA + SEQ
    small = np.zeros((128, NSM), f32)
    small[:, C_NW:C_NW + 8] = np.asarray(inputs["norm_w"], f32).reshape(8, 128).T
    cw = np.asarray(inputs["conv_w"], f32).reshape(4, 4, 128)
    small[:, C_CW:C_CW + 16] = cw.transpose(2, 1, 0).reshape(128, 16)
    small[:, C_CB:C_CB + 4] = np.asarray(inputs["conv_b"], f32).reshape(4, 128).T
    small[:, C_BR:C_BR + 4] = np.asarray(inputs["b_rgate"], f32).reshape(4, 128).T
    small[:, C_BI:C_BI + 4] = np.asarray(inputs["b_igate"], f32).reshape(4, 128).T
    small[:, C_L:C_L + 4] = np.asarray(inputs["lru_L"], f32).reshape(4, 128).T
    lamv = np.concatenate([np.asarray(inputs[k], f32).reshape(64) for k in ("lambda_q1", "lambda_k1", "lambda_q2", "lambda_k2")])
    small[:, C_LAM:C_LAM + 256] = np.broadcast_to(lamv[None, :], (128, 256))
    small[:, C_SUB:C_SUB + 512] = np.broadcast_to(np.tile(np.asarray(inputs["subln_w"], f32).reshape(128), 4)[None, :], (128, 512))
    small[:, C_FNW:C_FNW + 1024] = np.broadcast_to(np.asarray(inputs["final_norm_w"], f32).reshape(1024)[None, :], (128, 1024))
    wg = np.zeros((128, 2, 4, 128), f32)
    for gi, key in enumerate(("w_rgate", "w_igate")):
        w = np.asarray(inputs[key], f32).reshape(8, 64, 64)
        for ct in range(4):
            wg[0:64, gi, ct, 0:64] = w[2 * ct]
            wg[64:128, gi, ct, 64:128] = w[2 * ct + 1]
    wg = wg.reshape(128, 1024)
    c16 = np.zeros((128, 256), f32)
    c16[:, 0:128] = np.eye(128, dtype=f32)
    for p in range(128):
        d = p % 64
        if d < 8:
            c16[p + 8, 128 + p] = -1.0
        elif d < 16:
            c16[p - 8, 128 + p] = 1.0
    c16 = c16.astype(ml_dtypes.bfloat16)
    pos = np.arange(T, dtype=f32)
    inv = np.power(f32(ROPE_THETA), -np.arange(0, 16, 2, dtype=f32) / f32(16)).astype(f32)
    ang = (pos[:, None] * inv[None, :]).astype(f32)
    cs = np.zeros((128, 2, T), f32)
    cs[:, 0, :] = 1.0
    for p in range(128):
        d = p % 64
        if d < 16:
            cs[p, 0, :] = np.cos(ang[:, d % 8])
            cs[p, 1, :] = np.sin(ang[:, d % 8])
    shared = {
        "meta": np.ascontiguousarray(np.asarray(inputs["meta_tokens"], f32)),
        "w_in": np.ascontiguousarray(np.asarray(inputs["w_in"], f32).reshape(D, DIN)),
        "w_out": np.ascontiguousarray(np.asarray(inputs["w_out"], f32).reshape(D, D)),
        "small": small, "wgate": wg, "c16": c16, "cs": cs,
    }
    in_maps = []
    for i in range(n_cores):
        m = dict(shared)
        m["x"] = np.ascontiguousarray(x[i * S:(i + 1) * S])
        in_maps.append(m)
    return in_maps, S, SEQ


def kernel(**inputs):
    n_cores = 8
    in_maps, S, SEQ = _host_layout(inputs, n_cores)
    nc = build_nc(S, SEQ // 512)
    res = run_bass_kernel_spmd(nc, in_maps, core_ids=list(range(n_cores)))
    return np.concatenate([np.asarray(r["out"]) for r in res.results], axis=0).astype(np.float32)
```

```python
import numpy as np
import ml_dtypes
import concourse.bass as bass
import concourse.mybir as mybir
from concourse.bass_utils import run_bass_kernel_spmd
from contextlib import ExitStack

F32 = mybir.dt.float32
BF16 = mybir.dt.bfloat16
AF = mybir.ActivationFunctionType
ALU = mybir.AluOpType


class _Op:
    __slots__ = ("eng", "idx", "emit", "is_dma", "deps", "signal", "sigval", "sem", "semval", "prev_same_sem", "is_out")

    def __init__(self, eng, idx, emit, is_dma):
        self.eng = eng
        self.idx = idx
        self.emit = emit
        self.is_dma = is_dma
        self.deps = {}
        self.signal = False
        self.sigval = None
        self.sem = None
        self.semval = None
        self.prev_same_sem = None
        self.is_out = False


class Prog:
    ENGS = ("pe", "act", "dve", "pool", "sp")

    def __init__(self, nc, es, n_dma_sems=24, same_engine_raw=True):
        self.nc = nc
        self.es = es
        self.ops = {e: [] for e in self.ENGS}
        self.buf = {}
        self.same_engine_raw = same_engine_raw
        self.eng_sem = {e: es.enter_context(nc.semaphore("sem_" + e)) for e in ("pe", "act", "dve", "pool")}
        self.dma_sems = [es.enter_context(nc.semaphore("sem_dma%d" % i)) for i in range(n_dma_sems)]
        self.n_dma = 0
        self.dma_last = [None] * n_dma_sems

    def _segs(self, name, lo, hi):
        L = self.buf.get(name, [])
        out, new, cur = [], [], lo
        for s in L:
            if s[1] <= lo or s[0] >= hi:
                new.append(s)
                continue
            if s[0] < lo:
                new.append([s[0], lo, s[2], dict(s[3])])
                s = [lo, s[1], s[2], s[3]]
            tail = None
            if s[1] > hi:
                tail = [hi, s[1], s[2], dict(s[3])]
                s = [s[0], hi, s[2], s[3]]
            if cur < s[0]:
                g = [cur, s[0], None, {}]
                new.append(g)
                out.append(g)
            new.append(s)
            out.append(s)
            cur = s[1]
            if tail is not None:
                new.append(tail)
        if cur < hi:
            g = [cur, hi, None, {}]
            new.append(g)
            out.append(g)
        new.sort(key=lambda s: s[0])
        self.buf[name] = new
        return out

    def _add_dep(self, op, d, kind):
        if d is None or d is op:
            return
        if not d.is_dma and d.eng == op.eng and not op.is_dma:
            if d.eng == "pe":
                return
            if not self.same_engine_raw:
                return
        key = id(d)
        if key not in op.deps:
            op.deps[key] = d

    PSUM_NAMES = ("pj", "psc", "pv", "ptr")

    def _record(self, op, reads, writes):
        def is_psum(name):
            return (name[0] if isinstance(name, tuple) else name) in self.PSUM_NAMES
        excl = [(n, 0, 1 << 30) for (n, lo, hi) in list(reads) + list(writes) if is_psum(n)]
        reads = [r for r in reads if not is_psum(r[0])]
        writes = [w for w in writes if not is_psum(w[0])] + excl
        for (name, lo, hi) in reads:
            for s in self._segs(name, lo, hi):
                self._add_dep(op, s[2], "raw")
                rk = ("dma", id(op)) if op.is_dma else op.eng
                s[3][rk] = op
        for (name, lo, hi) in writes:
            for s in self._segs(name, lo, hi):
                self._add_dep(op, s[2], "waw")
                for r in s[3].values():
                    self._add_dep(op, r, "war")
                s[2] = op
                s[3] = {}

    def op(self, eng, emit, reads=(), writes=()):
        o = _Op(eng, len(self.ops[eng]), emit, False)
        self.ops[eng].append(o)
        self._record(o, reads, writes)
        return o

    def dma(self, out_ap, in_ap, reads=(), writes=(), out=False, **kw):
        def emit(e):
            return e.dma_start(out=out_ap, in_=in_ap, **kw)
        o = _Op("sp", len(self.ops["sp"]), emit, True)
        j = self.n_dma
        self.n_dma += 1
        n = len(self.dma_sems)
        o.sem = self.dma_sems[j % n]
        o.semval = 16 * (j // n + 1)
        o.prev_same_sem = self.dma_last[j % n]
        self.dma_last[j % n] = o
        o.is_out = out
        self.ops["sp"].append(o)
        self._record(o, reads, writes)
        return o

    def _finalize(self):
        self.waits = {}
        for e in self.ENGS:
            waited = {}
            for o in self.ops[e]:
                need = {}
                deps = list(o.deps.values())
                if o.is_dma and o.prev_same_sem is not None:
                    deps.append(o.prev_same_sem)
                for d in deps:
                    if d.is_dma:
                        k = ("dma", id(d.sem))
                        v = d.semval
                    else:
                        k = ("eng", d.eng)
                        v = d.idx
                    if waited.get(k, -1) >= v:
                        continue
                    if k not in need or need[k][0] < v:
                        need[k] = (v, d)
                for k, (v, d) in need.items():
                    waited[k] = v
                    if not d.is_dma:
                        d.signal = True
                self.waits[id(o)] = [d for (_, d) in need.values()]
        for e in ("pe", "act", "dve", "pool"):
            c = 0
            for o in self.ops[e]:
                if o.signal:
                    c += 1
                    o.sigval = c

    def emit(self):
        self._finalize()
        nc = self.nc
        handles = {"pe": nc.tensor, "act": nc.scalar, "dve": nc.vector, "pool": nc.gpsimd, "sp": nc.sync}

        def run(e, h):
            for o in self.ops[e]:
                for d in self.waits[id(o)]:
                    if d.is_dma:
                        h.wait_ge(d.sem, d.semval)
                    else:
                        h.wait_ge(self.eng_sem[d.eng], d.sigval)
                inst = o.emit(h)
                if o.is_dma:
                    inst.then_inc(o.sem, 16)
                elif o.signal:
                    inst.then_inc(self.eng_sem[e], 1)
            if e == "sp":
                for o in self.dma_last:
                    if o is not None:
                        h.wait_ge(o.sem, o.semval)

        with nc.Block() as block:
            @block.tensor
            def _(h):
                run("pe", h)

            @block.scalar
            def _(h):
                run("act", h)

            @block.vector
            def _(h):
                run("dve", h)

            @block.gpsimd
            def _(h):
                run("pool", h)

            @block.sync
            def _(h):
                run("sp", h)


D = 1024
DIN = 3072
NMETA = 16
EPS = 1e-6
SUBLN_EPS = 1e-5
LAM_INIT = 0.8 - 0.6
ROPE_THETA = 500000.0
AX = mybir.AxisListType.X

C_NW, C_CW, C_CB, C_BR, C_BI, C_L, C_LAM, C_SUB, C_FNW, NSM = 0, 8, 24, 28, 32, 36, 40, 296, 808, 1832


def R(name, lo=0, hi=1):
    return (name, lo, hi)


def build_nc(S, NCH):
    SEQ = 512 * NCH
    T = NMETA + SEQ
    NT = SEQ // 128
    nc = bass.Bass("TRN2", target_bir_lowering=False)

    def dram(name, shape, dtype=F32, kind="ExternalInput"):
        return nc.dram_tensor(name, shape, dtype, kind=kind).ap()

    x_d = dram("x", [S, SEQ, D])
    meta_d = dram("meta", [NMETA, D])
    win_d = dram("w_in", [D, DIN])
    wout_d = dram("w_out", [D, D])
    small_d = dram("small", [128, NSM])
    wgate_d = dram("wgate", [128, 1024])
    c16_d = dram("c16", [128, 256], BF16)
    cs_d = dram("cs", [128, 2, T])
    out_d = dram("out", [S, SEQ, D], kind="ExternalOutput")

    with ExitStack() as es:
        P = Prog(nc, es)

        def sb(name, shape, dtype):
            return es.enter_context(nc.sbuf_tensor("s_" + name, shape, dtype))

        def ps(name, shape, dtype):
            return es.enter_context(nc.psum_tensor("p_" + name, shape, dtype))

        win_bf = sb("win_bf", [128, 8, DIN], BF16)
        wout_bf = sb("wout_bf", [128, 8, D], BF16)
        NST = 3
        stage = [sb("stage%d" % i, [128, 1024], F32) for i in range(NST)]
        xn = [sb("xn%d" % i, [128, 1024], BF16) for i in range(2)]
        sm = sb("sm", [128, NSM], F32)
        c16 = sb("c16", [128, 256], BF16)
        wg_bf = sb("wg_bf", [128, 8, 128], BF16)
        kT = sb("kT", [128, 4, T], BF16)
        Vaug = sb("Vaug", [128, (NT + 1) * 4, 130], BF16)
        uT = sb("uT", [128, 8, 512], BF16)
        muT = sb("muT", [128, 8, NMETA], BF16)
        qT = sb("qT", [128, 4, 512], BF16)
        xl = sb("xl", [128, 4, 515], F32)
        xlm = sb("xlm", [128, 4, 3 + NMETA], F32)
        gl = sb("gl", [128, 4, 512], BF16)
        gw = sb("gw", [128, 4, 512], BF16)
        mixT = sb("mixT", [128, 8, 512], BF16)
        att = sb("att", [128, 4, 512], BF16)
        cs = sb("cs", [128, 2, 512], F32)
        csm = sb("csm", [128, 2, NMETA], F32)
        rt = [sb("rt%d" % i, [128, 512], F32) for i in range(2)]
        xc2 = [sb("xc%d" % i, [128, 512], F32) for i in range(2)]
        xcb2 = [sb("xcb%d" % i, [128, 512], BF16) for i in range(2)]
        l_tr = sb("l_tr", [128, 512], F32)
        l_ti = sb("l_ti", [128, 512], F32)
        l_a = sb("l_a", [128, 512], F32)
        l_m = sb("l_m", [128, 512], F32)
        tg = [l_a, l_m]
        TGN = ["l_a", "l_m"]
        NET = 4
        ET = [sb("ET%d" % i, [128, 512], BF16) for i in range(NET)]
        oj = sb("oj", [128, 128], BF16)
        xres = [sb("xres%d" % i, [128, 1024], F32) for i in range(3)]
        sc = sb("sc", [128, 128], F32)
        lt = sb("lt", [128, 128], F32)
        xlm_tail = sb("xlm_tail", [128, 4, 3], F32)
        sqj = sb("sqj", [128, 1024], BF16)
        rsx = [sb("rsx%d" % i, [128, 4], F32) for i in range(2)]
        rsy = [sb("rsy%d" % i, [128, 4], F32) for i in range(2)]
        rsa = [sb("rsa%d" % i, [128, 4], F32) for i in range(2)]

        ident = c16[:, 0:128]
        permT = c16[:, 128:256]
        MHALF, LAM, NLAM, LS, LE, SPE, SP, C8, C4, HB, CARRY, CARRYM = 0, 1, 2, 3, 5, 7, 11, 15, 19, 23, 31, 35
        SSA, RSA, SSE, RSE, SSD, RSD, RL = 80, 84, 44, 46, 48, 56, 64

        def col(i, n=1, rows=128):
            return sc[0:rows, i:i + n]

        pj = [ps("pj%d" % i, [128, 512], F32) for i in range(2)]
        ptr = ps("ptr", [128, 8, 128], BF16)
        psc = [ps("psc%d" % i, [128, 512], F32) for i in range(2)]
        pv = [ps("pv%d" % i, [128, 512], F32) for i in range(3)]

        PJB = [pj[0], pj[1], psc[0], psc[1], pv[0], pv[1], pv[2]]
        PJN = [("pj", 0), ("pj", 1), ("psc", 0), ("psc", 1), ("pv", 0), ("pv", 1), ("pv", 2)]
        NPJ = 7
        cnt = {"st": 0, "pj": 0, "xn": 0, "rt": 0, "tg": 0, "psc": 0, "ET": 0, "ot": 0, "xres": 0, "sl": 0, "sla": 0, "pjc": 0}

        def nxt(key, n):
            v = cnt[key] % n
            cnt[key] += 1
            return v

        P.dma(sm[:], small_d, writes=[R("sm")])
        P.dma(c16[:], c16_d, writes=[R("c16")])
        P.dma(csm[:], cs_d[:, :, 0:NMETA], writes=[R("csm")])
        P.op("pool", lambda e: e.memset(Vaug[:, :, 128:129], 1.0), writes=[R("V", 0, NT + 1)])
        P.op("pool", lambda e: e.memset(xlm[:, :, 0:3], 0.0), writes=[R(("xlm", ct), 0, 3) for ct in range(4)])
        P.op("pool", lambda e: e.memset(col(CARRY, 4), 0.0), writes=[R("carry")])
        P.op("pool", lambda e: e.memset(sc[0:64, 39:40], 0.0), writes=[R("maskb", 0, 1)])
        P.op("pool", lambda e: e.memset(sc[64:128, 39:40], -30000.0), writes=[R("maskb", 1, 2)])
        for i in range(2):
            P.op("dve", lambda e, i=i: e.tensor_tensor(out=lt[:, 64 * i:64 * i + 64], in0=sm[:, C_LAM + 128 * i:C_LAM + 128 * i + 64],
                                                       in1=sm[:, C_LAM + 128 * i + 64:C_LAM + 128 * i + 128], op=ALU.mult),
                 reads=[R("sm")], writes=[R("lt", i, i + 1)])
            P.op("dve", lambda e, i=i: e.reduce_sum(out=col(LS + i), in_=lt[:, 64 * i:64 * i + 64], axis=AX),
                 reads=[R("lt", i, i + 1)], writes=[R("ls", i, i + 1)])
        P.op("act", lambda e: e.activation(out=col(LE, 2), in_=col(LS, 2), func=AF.Exp), reads=[R("ls", 0, 2)], writes=[R("le")])
        P.op("pool", lambda e: e.tensor_tensor(out=col(LAM), in0=col(LE), in1=col(LE + 1), op=ALU.subtract),
             reads=[R("le")], writes=[R("lam")])
        P.op("pool", lambda e: e.tensor_scalar(out=col(NLAM), in0=col(LAM), scalar1=-1.0, scalar2=-LAM_INIT, op0=ALU.mult, op1=ALU.add),
             reads=[R("lam")], writes=[R("nlam")])
        P.op("act", lambda e: e.activation(out=col(SPE, 4), in_=sm[:, C_L:C_L + 4], func=AF.Exp, scale=-1.0),
             reads=[R("sm")], writes=[R("spe")])
        P.op("act", lambda e: e.activation(out=col(SP, 4), in_=col(SPE, 4), func=AF.Ln, bias=1.0, scale=1.0),
             reads=[R("spe")], writes=[R("sp")])
        P.op("pool", lambda e: e.tensor_scalar(out=col(C8, 4), in0=col(SP, 4), scalar1=-8.0, scalar2=None, op0=ALU.mult),
             reads=[R("sp")], writes=[R("c8")])
        P.op("pool", lambda e: e.tensor_scalar(out=col(C4, 4), in0=col(SP, 4), scalar1=-4.0, scalar2=None, op0=ALU.mult),
             reads=[R("sp")], writes=[R("c4")])
        P.op("pool", lambda e: e.tensor_scalar(out=col(HB, 8), in0=sm[:, C_BR:C_BR + 8], scalar1=0.5, scalar2=None, op0=ALU.mult),
             reads=[R("sm")], writes=[R("hb")])
        P.op("pool", lambda e: e.tensor_scalar(out=sm[:, C_SUB:C_SUB + 512], in0=sm[:, C_SUB:C_SUB + 512], scalar1=0.5 * (1.0 - LAM_INIT),
                                               scalar2=None, op0=ALU.mult),
             reads=[R("sm")], writes=[R("subg")])
        sublng = sm[:, C_SUB:C_SUB + 512]
        fnw = sm[:, C_FNW:C_FNW + 1024]
        b = nxt("st", NST)
        P.dma(stage[b][:], wgate_d, writes=[R(("st", b))])
        P.op("pool", lambda e, b=b: e.tensor_copy(out=wg_bf[:].rearrange("p a b -> p (a b)"), in_=stage[b][:]),
             reads=[R(("st", b))], writes=[R("wg")])

        I32 = mybir.dt.int32
        RS_MAGIC = 1597463007.0
        rs_cnt = [0]

        def rsqrt_dve(dst, src, scale, eps, src_regs, dst_regs, rows=128, n=1):
            u = rs_cnt[0] % 2
            rs_cnt[0] += 1
            X = rsx[u][0:rows, 0:n]
            Y = rsy[u][0:rows, 0:n]
            A = rsa[u][0:rows, 0:n]
            rx, ry, ra = R(("rsx", u)), R(("rsy", u)), R(("rsa", u))
            P.op("dve", lambda e: e.tensor_scalar(out=X, in0=src, scalar1=scale, scalar2=eps, op0=ALU.mult, op1=ALU.add),
                 reads=src_regs, writes=[rx])
            P.op("dve", lambda e: e.tensor_scalar(out=Y.bitcast(I32), in0=X.bitcast(I32), scalar1=-0.5, scalar2=RS_MAGIC, op0=ALU.mult, op1=ALU.add),
                 reads=[rx], writes=[ry])
            for it in range(3):
                last = it == 2
                P.op("dve", lambda e: e.tensor_tensor(out=A, in0=Y, in1=Y, op=ALU.mult), reads=[ry], writes=[ra])
                P.op("dve", lambda e: e.tensor_tensor(out=A, in0=A, in1=X, op=ALU.mult), reads=[ra, rx], writes=[ra])
                P.op("dve", lambda e: e.tensor_scalar(out=A, in0=A, scalar1=-0.5, scalar2=1.5, op0=ALU.mult, op1=ALU.add), reads=[ra], writes=[ra])
                if last:
                    P.op("dve", lambda e: e.tensor_tensor(out=dst, in0=Y, in1=A, op=ALU.mult), reads=[ry, ra], writes=dst_regs)
                else:
                    P.op("dve", lambda e: e.tensor_tensor(out=Y, in0=Y, in1=A, op=ALU.mult), reads=[ry, ra], writes=[ry])

        def stage_A1a(src_ap, rows, defer_rs=False):
            b = nxt("st", NST)
            sl = nxt("sla", 4)
            P.dma(stage[b][0:rows, :], src_ap, writes=[R(("st", b))])
            P.op("dve", lambda e: e.scalar_tensor_tensor(out=sqj[0:rows, :], in0=stage[b][0:rows, :], scalar=1.0, in1=stage[b][0:rows, :],
                                                         op0=ALU.mult, op1=ALU.mult, accum_out=col(SSA + sl, 1, rows)),
                 reads=[R(("st", b))], writes=[R("sqj"), R(("ssa", sl))])
            if not defer_rs:
                rsqrt_dve(col(RSA + sl, 1, rows), col(SSA + sl, 1, rows), 1.0 / D, EPS, [R(("ssa", sl))], [R(("rsa", sl))], rows=rows)
            return (b, sl, rows)

        def rsqrt_act(dst, src, scale, eps, src_regs, dst_regs, n=1):
            u = rs_cnt[0] % 2
            rs_cnt[0] += 1
            X = rsx[u][:, 0:n]
            P.op("act", lambda e: e.activation(out=X, in_=src, func=AF.Sqrt, bias=eps, scale=scale), reads=src_regs, writes=[R(("rsx", u))])
            P.op("dve", lambda e: e.reciprocal(out=dst, in_=X), reads=[R(("rsx", u))], writes=dst_regs)

        def stage_A1b(ctx):
            b, sl, rows = ctx
            xb = nxt("xn", 2)
            P.op("dve", lambda e: e.tensor_scalar(out=xn[xb][0:rows, :], in0=stage[b][0:rows, :], scalar1=col(RSA + sl, 1, rows),
                                                  scalar2=None, op0=ALU.mult),
                 reads=[R(("st", b)), R(("rsa", sl))], writes=[R(("xn", xb))])
            return xb

        def stage_A1(src_ap, rows):
            return stage_A1b(stage_A1a(src_ap, rows))

        def stage_A2(xb, rows, dst, dst_name, col0):
            for k in range(8):
                P.op("pe", lambda e, k=k: e.transpose(ptr[:, k, 0:rows], xn[xb][0:rows, 128 * k:128 * k + 128], ident[0:rows, 0:rows]),
                     reads=[R(("xn", xb)), R("c16")], writes=[R("ptr", k, k + 1)])
            P.op("dve", lambda e: e.tensor_copy(out=dst[:, :, col0:col0 + rows], in_=ptr[:, :, 0:rows]),
                 reads=[R("ptr", 0, 8)], writes=[R(dst_name, col0, col0 + rows)])

        def stage_A(src_ap, rows, dst, dst_name, col0):
            b = nxt("st", NST)
            sl = nxt("sla", 4)
            xb = nxt("xn", 2)
            P.dma(stage[b][0:rows, :], src_ap, writes=[R(("st", b))])
            P.op("act", lambda e: e.activation(out=sqj[0:rows, :], in_=stage[b][0:rows, :], func=AF.Square, accum_out=col(SSA + sl, 1, rows)),
                 reads=[R(("st", b))], writes=[R("sqj"), R(("ssa", sl))])
            u = rs_cnt[0] % 2
            rs_cnt[0] += 1
            X = rsx[u][0:rows, 0:1]
            P.op("act", lambda e: e.activation(out=X, in_=col(SSA + sl, 1, rows), func=AF.Sqrt, bias=EPS, scale=1.0 / D),
                 reads=[R(("ssa", sl))], writes=[R(("rsx", u))])
            P.op("dve", lambda e: e.reciprocal(out=col(RSA + sl, 1, rows), in_=X), reads=[R(("rsx", u))], writes=[R(("rsa", sl))])
            P.op("act", lambda e: e.activation(out=xn[xb][0:rows, :], in_=stage[b][0:rows, :], func=AF.Copy, scale=col(RSA + sl, 1, rows)),
                 reads=[R(("st", b)), R(("rsa", sl))], writes=[R(("xn", xb))])
            stage_A2(xb, rows, dst, dst_name, col0)

        def weight_pieces():
            L = []
            cntw = [0]

            def win_piece(cg, k):
                def f():
                    b = cntw[0] % 3
                    cntw[0] += 1
                    P.dma(xres[b][:], win_d[128 * k:128 * k + 128, 1024 * cg:1024 * cg + 1024], writes=[R(("xres", b))])
                    if cntw[0] % 2 == 0:
                        P.op("dve", lambda e: e.tensor_scalar(out=win_bf[:, k, 1024 * cg:1024 * cg + 1024], in0=xres[b][:],
                                                              scalar1=sm[:, C_NW + k:C_NW + k + 1], scalar2=None, op0=ALU.mult),
                             reads=[R(("xres", b)), R("sm")], writes=[R(("win", k), 1024 * cg, 1024 * cg + 1024)])
                    else:
                        P.op("act", lambda e: e.activation(out=win_bf[:, k, 1024 * cg:1024 * cg + 1024], in_=xres[b][:],
                                                           func=AF.Copy, scale=sm[:, C_NW + k:C_NW + k + 1]),
                             reads=[R(("xres", b)), R("sm")], writes=[R(("win", k), 1024 * cg, 1024 * cg + 1024)])
                return f

            def wout_piece(k):
                def f():
                    b = cntw[0] % 3
                    cntw[0] += 1
                    P.dma(xres[b][:], wout_d[128 * k:128 * k + 128, :], writes=[R(("xres", b))])
                    if k % 2 == 0:
                        P.op("dve", lambda e: e.tensor_scalar(out=wout_bf[:, k, :], in0=xres[b][:], scalar1=(1.0 if k < 4 else 0.5),
                                                              scalar2=None, op0=ALU.mult),
                             reads=[R(("xres", b))], writes=[R(("wout", k), 0, 1024)])
                    else:
                        P.op("act", lambda e: e.activation(out=wout_bf[:, k, :], in_=xres[b][:], func=AF.Copy, scale=(1.0 if k < 4 else 0.5)),
                             reads=[R(("xres", b))], writes=[R(("wout", k), 0, 1024)])
                return f

            for cg in range(3):
                for k in range(8):
                    L.append(win_piece(cg, k))
            for k in range(8):
                L.append(wout_piece(k))
            return L

        def proj_group(lhs_fn, rhs_fn, outrows, N, reads_fn):
            b = nxt("pj", NPJ)
            for k in range(8):
                P.op("pe", lambda e, k=k: e.matmul(PJB[b][0:outrows, 0:N], lhsT=lhs_fn(k), rhs=rhs_fn(k), start=(k == 0), stop=(k == 7)),
                     reads=reads_fn(k), writes=[R(PJN[b])])
            return b

        def stage_B(u, u_name, N, p0, meta, tiles, after_xl0=None):
            cos = csm[:, 0, :] if meta else cs[:, 0, :]
            sin = csm[:, 1, :] if meta else cs[:, 1, :]
            cs_name = "csm" if meta else "cs"

            def fm(c0):
                return proj_group(lambda k: win_bf[:, k, c0:c0 + 128], lambda k: u[:, k, 0:N], 128, N,
                                  lambda k: [R(("win", k), c0, c0 + 128), R(u_name, 0, N)])

            ropes = []
            for h in range(4):
                b = fm(512 + 128 * h)
                dest = kT[:, h, p0:p0 + N]
                dreg = R(("kT", h), p0, p0 + N)
                P.op("act", lambda e, b=b, dest=dest: e.activation(out=dest, in_=PJB[b][:, 0:N], func=AF.Copy),
                     reads=[R(PJN[b])], writes=[dreg])
                ropes.append((dest, dreg))
            if not meta:
                for h in range(4):
                    b = fm(128 * h)
                    dest = qT[:, h, 0:N]
                    dreg = R(("qT", h), 0, N)
                    P.op("act", lambda e, b=b, dest=dest: e.activation(out=dest, in_=PJB[b][:, 0:N], func=AF.Copy),
                         reads=[R(PJN[b])], writes=[dreg])
                    ropes.append((dest, dreg))

            def do_rope(dest, dreg):
                b2 = nxt("pj", NPJ)
                r = nxt("rt", 1)
                t1, t2 = rt[2 * r], rt[2 * r + 1]
                P.op("pe", lambda e: e.matmul(PJB[b2][:, 0:N], lhsT=permT, rhs=dest, start=True, stop=True),
                     reads=[dreg, R("c16")], writes=[R(PJN[b2])])
                P.op("pool", lambda e: e.tensor_tensor(out=t1[:, 0:N], in0=dest, in1=cos, op=ALU.mult),
                     reads=[dreg, R(cs_name)], writes=[R(("rt", 2 * r), 0, 512)])
                P.op("dve", lambda e: e.tensor_tensor(out=t2[:, 0:N], in0=PJB[b2][:, 0:N], in1=sin, op=ALU.mult),
                     reads=[R(PJN[b2]), R(cs_name)], writes=[R(("rt", 2 * r + 1), 0, 512)])
                P.op("pool", lambda e: e.tensor_tensor(out=dest, in0=t1[:, 0:N], in1=t2[:, 0:N], op=ALU.add),
                     reads=[R(("rt", 2 * r), 0, 512), R(("rt", 2 * r + 1), 0, 512)], writes=[dreg])

            for (rows, vt, c0l) in tiles:
                b = proj_group(lambda k, c0l=c0l, rows=rows: u[:, k, c0l:c0l + rows], lambda k: win_bf[:, k, 1024:1536], rows, 512,
                               lambda k: [R(("win", k), 1024, 1536), R(u_name, c0l, c0l + rows)])
                P.op("dve", lambda e, b=b, rows=rows, vt=vt: e.tensor_copy(out=Vaug[0:rows, 4 * vt:4 * vt + 4, 0:128],
                                                                          in_=PJB[b][0:rows, :].rearrange("p (h d) -> p h d", h=4)),
                     reads=[R(PJN[b])], writes=[R("V", vt, vt + 1)])
            for (dest, dreg) in ropes[0:4]:
                do_rope(dest, dreg)
            if not meta:
                for jl, (rows, vt, c0l) in enumerate(tiles):
                    b = proj_group(lambda k, c0l=c0l, rows=rows: u[:, k, c0l:c0l + rows], lambda k: win_bf[:, k, 1536:2048], rows, 512,
                                   lambda k: [R(("win", k), 1536, 2048), R(u_name, c0l, c0l + rows)])
                    g = nxt("tg", 2)
                    P.op("act", lambda e, b=b, g=g: e.activation(out=tg[g][:], in_=PJB[b][:, :], func=AF.Tanh, scale=0.5),
                         reads=[R(PJN[b])], writes=[R(TGN[g])])
                    P.op("dve", lambda e, b=b, g=g: e.scalar_tensor_tensor(out=tg[g][:], in0=tg[g][:], scalar=1.0, in1=PJB[b][:, :],
                                                                           op0=ALU.add, op1=ALU.mult),
                         reads=[R(PJN[b]), R(TGN[g])], writes=[R(TGN[g])])
                    P.op("pool", lambda e, g=g, jl=jl: e.tensor_tensor(out=gw[:, jl, :], in0=tg[g][:], in1=sublng, op=ALU.mult),
                         reads=[R(TGN[g]), R("subg")], writes=[R(("gw", jl), 0, 512)])
                for (dest, dreg) in ropes[4:8]:
                    do_rope(dest, dreg)
            for ct in range(4):
                if ct == 1 and after_xl0 is not None:
                    after_xl0()
                b = fm(2048 + 128 * ct)
                P.op("act", lambda e, b=b, ct=ct: e.activation(out=(xlm if meta else xl)[:, ct, 3:3 + N], in_=PJB[b][:, 0:N], func=AF.Copy),
                     reads=[R(PJN[b])], writes=[R(("xlm" if meta else "xl", ct), 3, 3 + N)])
            if not meta:
                for ct in range(4):
                    b = fm(2560 + 128 * ct)
                    g = nxt("tg", 2)
                    P.op("act", lambda e, b=b, g=g: e.activation(out=tg[g][:, 0:N], in_=PJB[b][:, 0:N], func=AF.Tanh, scale=0.5),
                         reads=[R(PJN[b])], writes=[R(TGN[g])])
                    P.op("dve", lambda e, b=b, g=g, ct=ct: e.scalar_tensor_tensor(out=gl[:, ct, 0:N], in0=tg[g][:, 0:N], scalar=1.0,
                                                                                 in1=PJB[b][:, 0:N], op0=ALU.add, op1=ALU.mult),
                         reads=[R(PJN[b]), R(TGN[g])], writes=[R(("gl", ct), 0, N)])

        def stage_C_conv(ct, N, meta, xl=xl):
            xc, xcb = xc2[ct % 2], xcb2[ct % 2]
            xcn, xcbn = ("xc", ct % 2), ("xcb", ct % 2)
            xln = "xlm" if meta else "xl"
            if meta:
                xl = xlm
            xlr = R((xln, ct), 0, 3 + N)
            P.op("pool", lambda e: e.tensor_scalar(out=xc[:, 0:N], in0=xl[:, ct, 0:N], scalar1=sm[:, C_CW + 4 * ct:C_CW + 4 * ct + 1],
                                                   scalar2=sm[:, C_CB + ct:C_CB + ct + 1], op0=ALU.mult, op1=ALU.add),
                 reads=[xlr, R("sm")], writes=[R(xcn)])
            for j in range(1, 4):
                P.op("dve", lambda e, j=j: e.scalar_tensor_tensor(out=xc[:, 0:N], in0=xl[:, ct, j:j + N],
                                                                  scalar=sm[:, C_CW + 4 * ct + j:C_CW + 4 * ct + j + 1],
                                                                  in1=xc[:, 0:N], op0=ALU.mult, op1=ALU.add),
                     reads=[xlr, R("sm"), R(xcn)], writes=[R(xcn)])
            if meta:
                P.op("pool", lambda e: e.tensor_copy(out=xlm_tail[:, ct, :], in_=xl[:, ct, N:N + 3]),
                     reads=[R((xln, ct), N, N + 3)], writes=[R("xlm_tail", ct, ct + 1)])
            else:
                P.op("pool", lambda e: e.tensor_copy(out=xl[:, ct, 0:3], in_=xl[:, ct, N:N + 3]),
                     reads=[R(("xl", ct), N, N + 3)], writes=[R(("xl", ct), 0, 3)])
            P.op("pool", lambda e: e.tensor_copy(out=xcb[:, 0:N], in_=xc[:, 0:N]), reads=[R(xcn)], writes=[R(xcbn)])

        def stage_C_rest(ct, N, meta, in_hook=False):
            xc, xcb = xc2[ct % 2], xcb2[ct % 2]
            xcn, xcbn = ("xc", ct % 2), ("xcb", ct % 2)
            for gi, dst, dname in ((0, l_tr, "l_tr"), (1, l_ti, "l_ti")):
                b = nxt("pjc", 2) if in_hook else nxt("pj", NPJ)
                P.op("pe", lambda e, b=b, gi=gi: e.matmul(PJB[b][:, 0:N], lhsT=wg_bf[:, 4 * gi + ct, :], rhs=xcb[:, 0:N],
                                                         start=True, stop=True),
                     reads=[R("wg"), R(xcbn)], writes=[R(PJN[b])])
                P.op("act", lambda e, b=b, gi=gi, dst=dst: e.activation(out=dst[:, 0:N], in_=PJB[b][:, 0:N], func=AF.Tanh,
                                                                       bias=col(HB + 4 * gi + ct), scale=0.5),
                     reads=[R(PJN[b]), R("hb")], writes=[R(dname)])
            P.op("act", lambda e: e.activation(out=l_a[:, 0:N], in_=l_tr[:, 0:N], func=AF.Exp, bias=col(C4 + ct), scale=col(C4 + ct)),
                 reads=[R("l_tr"), R("c4")], writes=[R("l_a")])
            P.op("pool", lambda e: e.tensor_tensor(out=l_m[:, 0:N], in0=l_a[:, 0:N], in1=l_a[:, 0:N], op=ALU.mult),
                 reads=[R("l_a")], writes=[R("l_m")])
            P.op("act", lambda e: e.activation(out=l_m[:, 0:N], in_=l_m[:, 0:N], func=AF.Sqrt, bias=0.25, scale=-0.25),
                 reads=[R("l_m")], writes=[R("l_m")])
            if meta:
                P.op("pool", lambda e: e.memset(l_m[:, 0:1], 0.5), reads=[], writes=[R("l_m")])
            P.op("dve", lambda e: e.scalar_tensor_tensor(out=l_ti[:, 0:N], in0=l_ti[:, 0:N], scalar=1.0, in1=xc[:, 0:N],
                                                         op0=ALU.add, op1=ALU.mult),
                 reads=[R("l_ti"), R(xcn)], writes=[R("l_ti")])
            P.op("pool", lambda e: e.tensor_tensor(out=l_ti[:, 0:N], in0=l_ti[:, 0:N], in1=l_m[:, 0:N], op=ALU.mult),
                 reads=[R("l_ti"), R("l_m")], writes=[R("l_ti")])
            P.op("dve", lambda e: e.tensor_tensor_scan(out=l_tr[:, 0:N], data0=l_a[:, 0:N], data1=l_ti[:, 0:N],
                                                       initial=col(CARRY + ct), op0=ALU.mult, op1=ALU.add),
                 reads=[R("l_a"), R("l_ti"), R("carry", ct, ct + 1)], writes=[R("l_tr")])
            P.op("pool", lambda e: e.tensor_copy(out=col(CARRY + ct), in_=l_tr[:, N - 1:N]),
                 reads=[R("l_tr")], writes=[R("carry", ct, ct + 1)])
            if meta:
                P.op("pool", lambda e: e.tensor_copy(out=col(CARRYM + ct), in_=l_tr[:, N - 1:N]),
                     reads=[R("l_tr")], writes=[R("carrym", ct, ct + 1)])
            else:
                P.op("pool", lambda e: e.tensor_tensor(out=mixT[:, 4 + ct, 0:N], in0=l_tr[:, 0:N], in1=gl[:, ct, 0:N], op=ALU.mult),
                     reads=[R("l_tr"), R(("gl", ct), 0, N)], writes=[R(("mixT", 4 + ct), 0, N)])

        def stage_C(N, meta):
            for ct in range(4):
                stage_C_conv(ct, N, meta)
                stage_C_rest(ct, N, meta)

        def c_pieces(ct, N):
            xc, xcb = xc2[ct % 2], xcb2[ct % 2]
            xcn, xcbn = ("xc", ct % 2), ("xcb", ct % 2)

            def gate(gi, dst, dname):
                def f():
                    b = nxt("pjc", 2)
                    P.op("pe", lambda e: e.matmul(PJB[b][:, 0:N], lhsT=wg_bf[:, 4 * gi + ct, :], rhs=xcb[:, 0:N], start=True, stop=True),
                         reads=[R("wg"), R(xcbn)], writes=[R(PJN[b])])
                    P.op("act", lambda e: e.activation(out=dst[:, 0:N], in_=PJB[b][:, 0:N], func=AF.Tanh, bias=col(HB + 4 * gi + ct), scale=0.5),
                         reads=[R(PJN[b]), R("hb")], writes=[R(dname)])
                return f

            def p_exp():
                P.op("act", lambda e: e.activation(out=l_a[:, 0:N], in_=l_tr[:, 0:N], func=AF.Exp, bias=col(C4 + ct), scale=col(C4 + ct)),
                     reads=[R("l_tr"), R("c4")], writes=[R("l_a")])
                P.op("pool", lambda e: e.tensor_tensor(out=l_m[:, 0:N], in0=l_a[:, 0:N], in1=l_a[:, 0:N], op=ALU.mult),
                     reads=[R("l_a")], writes=[R("l_m")])
                P.op("dve", lambda e: e.scalar_tensor_tensor(out=l_ti[:, 0:N], in0=l_ti[:, 0:N], scalar=1.0, in1=xc[:, 0:N],
                                                             op0=ALU.add, op1=ALU.mult),
                     reads=[R("l_ti"), R(xcn)], writes=[R("l_ti")])

            def p_sqrt():
                P.op("act", lambda e: e.activation(out=l_m[:, 0:N], in_=l_m[:, 0:N], func=AF.Sqrt, bias=0.25, scale=-0.25),
                     reads=[R("l_m")], writes=[R("l_m")])

            def p_tail():
                P.op("pool", lambda e: e.tensor_tensor(out=l_ti[:, 0:N], in0=l_ti[:, 0:N], in1=l_m[:, 0:N], op=ALU.mult),
                     reads=[R("l_ti"), R("l_m")], writes=[R("l_ti")])
                P.op("dve", lambda e: e.tensor_tensor_scan(out=l_tr[:, 0:N], data0=l_a[:, 0:N], data1=l_ti[:, 0:N],
                                                           initial=col(CARRY + ct), op0=ALU.mult, op1=ALU.add),
                     reads=[R("l_a"), R("l_ti"), R("carry", ct, ct + 1)], writes=[R("l_tr")])
                P.op("pool", lambda e: e.tensor_copy(out=col(CARRY + ct), in_=l_tr[:, N - 1:N]),
                     reads=[R("l_tr")], writes=[R("carry", ct, ct + 1)])
                P.op("pool", lambda e: e.tensor_tensor(out=mixT[:, 4 + ct, 0:N], in0=l_tr[:, 0:N], in1=gl[:, ct, 0:N], op=ALU.mult),
                     reads=[R("l_tr"), R(("gl", ct), 0, N)], writes=[R(("mixT", 4 + ct), 0, N)])

            return [gate(0, l_tr, "l_tr"), gate(1, l_ti, "l_ti"), p_exp, p_sqrt, p_tail]

        def att_T(h):
            for jl in range(4):
                P.op("pe", lambda e, jl=jl: e.transpose(ptr[:, jl, :], att[:, jl, 128 * h:128 * h + 128], ident),
                     reads=[R(("att", jl), 128 * h, 128 * h + 128), R("c16")], writes=[R("ptr", jl, jl + 1)])
            P.op("dve", lambda e: e.tensor_copy(out=mixT[:, h, :], in_=ptr[:, 0:4, :].rearrange("p j n -> p (j n)")),
                 reads=[R("ptr", 0, 4)], writes=[R(("mixT", h), 0, 512)])

        def stage_D(c, pieces_for=None):
            ktiles = [None] + list(range(4 * c + 4))
            def head(h, pieces):
                items = [(kt, m) for kt in ktiles for m in range(2)]
                started = set()
                info = {}

                def emit_scores(n):
                    kt, m = items[n]
                    b = nxt("psc", 2)
                    if kt is None:
                        K, kp0, qlo = NMETA, 0, 0
                    else:
                        K, kp0, qlo = 128, NMETA + 128 * kt, max(0, kt - 4 * c)
                    Nq = 512 - 128 * qlo
                    P.op("pe", lambda e: e.matmul(psc[b][0:K, 0:Nq], lhsT=kT[64 * m:64 * m + 64, h, kp0:kp0 + K],
                                                  rhs=qT[64 * m:64 * m + 64, h, 128 * qlo:512], start=True, stop=True),
                         reads=[R(("kT", h), kp0, kp0 + K), R(("qT", h), 128 * qlo, 512)], writes=[R(("psc", b))])
                    et = nxt("ET", NET)
                    if kt is not None and kt >= 4 * c:
                        P.op("act", lambda e: e.activation(out=ET[et][:, 0:64], in_=psc[b][:, 0:64], func=AF.Exp, scale=0.125, bias=sc[:, 39:40]),
                             reads=[R(("psc", b)), R("maskb", 0, 2)], writes=[R(("ET", et), 0, 64)])
                        P.op("act", lambda e: e.activation(out=ET[et][:, 64:Nq], in_=psc[b][:, 64:Nq], func=AF.Exp, scale=0.125),
                             reads=[R(("psc", b))], writes=[R(("ET", et), 64, 512)])
                    else:
                        P.op("act", lambda e: e.activation(out=ET[et][0:K, 0:Nq], in_=psc[b][0:K, 0:Nq], func=AF.Exp, scale=0.125),
                             reads=[R(("psc", b))], writes=[R(("ET", et), 0, 512)])
                    info[n] = (et, K, qlo)

                def emit_pv(n):
                    kt, m = items[n]
                    et, K, qlo = info[n]
                    vt = 0 if kt is None else kt + 1
                    for jl in range(qlo, 4):
                        a = 2 * jl + m
                        bank, off = a // 3, 130 * (a % 3)
                        first = bank not in started
                        started.add(bank)
                        last = (kt is not None) and (kt == 4 * c + jl)
                        P.op("pe", lambda e, jl=jl, bank=bank, off=off, first=first, last=last: e.matmul(
                            pv[bank][:, off:off + 129], lhsT=ET[et][0:K, 128 * (jl - qlo):128 * (jl - qlo) + 128],
                            rhs=Vaug[0:K, 4 * vt + h, 0:129], start=first, stop=last, skip_group_check=True),
                            reads=[R(("ET", et), 0, 512), R("V", vt, vt + 1)], writes=[R(("pv", bank), off, off + 129)])

                emit_scores(0)
                emit_scores(1)
                pi = 0
                pstride = max(1, (len(items) - 2) // max(1, len(pieces)))
                for n in range(len(items)):
                    emit_pv(n)
                    if n + 2 < len(items):
                        emit_scores(n + 2)
                    if n >= 1 and (n - 1) % pstride == 0 and pi < len(pieces):
                        pieces[pi]()
                        pi += 1
                sl = nxt("sl", 2)
                accs = []
                for a in range(8):
                    bank, off = a // 3, 130 * (a % 3)
                    raw = rt[a // 4][:, 128 * (a % 4):128 * (a % 4) + 128]
                    rawr = R(("rt", a // 4), 128 * (a % 4), 128 * (a % 4) + 128)
                    accs.append((None, None, raw, rawr, R(("rl", sl), a, a + 1), RL + 8 * sl + a))
                for bank, na in ((0, 3), (1, 3), (2, 2)):
                    a0 = 3 * bank
                    P.op("dve", lambda e, bank=bank, na=na, a0=a0: e.reciprocal(
                        out=col(RL + 8 * sl + a0, na), in_=pv[bank][:, 0:130 * na].rearrange("p (a c) -> p a c", c=130)[:, :, 128]),
                        reads=[R(("pv", bank), 0, 130 * na)], writes=[R(("rl", sl), a0, a0 + na)])
                def rawslots(lo, n):
                    t = rt[lo // 4]
                    return t[:, 128 * (lo % 4):128 * (lo % 4) + 128 * n].rearrange("p (a c) -> p a c", c=128)

                def accslots(bank, j0, n):
                    return pv[bank][:, 130 * j0:130 * (j0 + n)].rearrange("p (a c) -> p a c", c=130)[:, :, 0:128]

                for eng, bank, j0, n, lo in (("act", 0, 0, 3, 0), ("dve", 1, 0, 1, 3), ("dve", 1, 1, 2, 4), ("act", 2, 0, 2, 6)):
                    rd = [R(("pv", bank), 130 * j0, 130 * (j0 + n))]
                    wr = [R(("rt", lo // 4), 128 * (lo % 4), 128 * (lo % 4) + 128 * n)]
                    if eng == "act":
                        P.op("act", lambda e, bank=bank, j0=j0, n=n, lo=lo: e.activation(out=rawslots(lo, n), in_=accslots(bank, j0, n), func=AF.Copy),
                             reads=rd, writes=wr)
                    else:
                        P.op("dve", lambda e, bank=bank, j0=j0, n=n, lo=lo: e.tensor_copy(out=rawslots(lo, n), in_=accslots(bank, j0, n)),
                             reads=rd, writes=wr)
                while pi < len(pieces):
                    pieces[pi]()
                    pi += 1

                def p_norm():
                    for jl in range(4):
                        (acc0, rr0, raw0, rawr0, rl0, c0) = accs[2 * jl]
                        (acc1, rr1, raw1, rawr1, rl1, c1) = accs[2 * jl + 1]
                        P.op("dve", lambda e, c1=c1: e.tensor_tensor(out=col(c1), in0=col(c1), in1=col(NLAM), op=ALU.mult),
                             reads=[rl1, R("nlam")], writes=[rl1])
                        P.op("dve", lambda e, raw0=raw0, c0=c0: e.tensor_scalar(out=raw0, in0=raw0, scalar1=col(c0), scalar2=None, op0=ALU.mult),
                             reads=[rawr0, rl0], writes=[rawr0])
                        P.op("dve", lambda e, raw0=raw0, raw1=raw1, c1=c1: e.scalar_tensor_tensor(out=raw0, in0=raw1, scalar=col(c1), in1=raw0,
                                                                                                op0=ALU.mult, op1=ALU.add),
                             reads=[rawr1, rl1, rawr0], writes=[rawr0])

                o_ap = [accs[2 * jl][2] for jl in range(4)]
                o_rg = [accs[2 * jl][3] for jl in range(4)]

                class Post:
                    pass
                post = Post()
                rs_args = (col(RSD + 4 * sl, 4), col(SSD + 4 * sl, 4), 1.0 / 128, SUBLN_EPS, [R(("ssd", sl), 0, 4)], [R(("rsd", sl), 0, 4)])

                def p_sq():
                    p_norm()
                    for jl in range(4):
                        P.op("act", lambda e, jl=jl: e.activation(out=oj[:], in_=o_ap[jl], func=AF.Square, accum_out=col(SSD + 4 * sl + jl)),
                             reads=[o_rg[jl]], writes=[R("oj"), R(("ssd", sl), jl, jl + 1)])

                def p_stt():
                    for jl in range(4):
                        P.op("dve", lambda e, jl=jl: e.scalar_tensor_tensor(out=att[:, jl, 128 * h:128 * h + 128], in0=o_ap[jl],
                                                                            scalar=col(RSD + 4 * sl + jl), in1=gw[:, jl, 128 * h:128 * h + 128],
                                                                            op0=ALU.mult, op1=ALU.mult),
                             reads=[o_rg[jl], R(("rsd", sl), jl, jl + 1), R(("gw", jl), 128 * h, 128 * h + 128)],
                             writes=[R(("att", jl), 128 * h, 128 * h + 128)])

                def p_full():
                    p_sq()
                    rsqrt_dve(*rs_args, n=4)
                    p_stt()

                post.sq, post.rs_args, post.stt, post.full = p_sq, rs_args, p_stt, p_full
                return post

            pend = None
            for h in range(4):
                pc = pieces_for(h)
                L = []
                if pend is not None:
                    L.append(pend.sq)
                L += [pc["g0"], pc["g1"], pc["p_exp"]]
                if pc.get("A2"):
                    L.append(pc["A2"])

                def sqrt_window(pc=pc, pend=pend):
                    pc["p_sqrt"]()
                    if pend is not None:
                        rsqrt_act(*pend.rs_args, n=4)
                    if pc.get("a1a_rs"):
                        rsqrt_act(*pc["a1a_rs"]())
                L.append(sqrt_window)
                if pend is not None:
                    L.append(pend.stt)
                if h >= 1:
                    L.append(lambda h=h: att_T(h - 1))
                if pc.get("A1b"):
                    L.append(pc["A1b"])
                if pc.get("A1a"):
                    L.append(pc["A1a"])
                if pc.get("conv"):
                    L.append(pc["conv"])
                L.append(pc["p_tail"])
                pend = head(h, L)
            pend.full()


        xr = {}

        def e_load(s, c, jl):
            r0 = 512 * c + 128 * jl
            xb = nxt("xres", 3)
            P.dma(xres[xb][:], x_d[s, r0:r0 + 128, :], writes=[R(("xres", xb))])
            xr[jl] = xb

        KORD = (4, 5, 6, 0, 1, 2, 7, 3)

        def stage_E(s, c, mid=None):
            hbs = {}
            for jl in range(3):
                hbs[jl] = [nxt("pj", NPJ), nxt("pj", NPJ)]
                e_mm(jl, hbs[jl], range(0, 7))
            if mid is not None:
                mid()
            for jl in range(3):
                e_mm(jl, hbs[jl], range(7, 8))
                e_adds(jl, hbs[jl])
            for jl in range(3):
                e_norm_store(s, c, jl)
            hbs[3] = [nxt("pj", NPJ), nxt("pj", NPJ)]
            e_mm(3, hbs[3], range(0, 8))
            e_adds(3, hbs[3])
            e_norm_store(s, c, 3)

        def e_mm(jl, hb, kis):
            for half in range(2):
                for ki in kis:
                    k = KORD[ki]
                    P.op("pe", lambda e, half=half, k=k, ki=ki: e.matmul(PJB[hb[half]][:, :], lhsT=mixT[:, k, 128 * jl:128 * jl + 128],
                                                                        rhs=wout_bf[:, k, 512 * half:512 * half + 512], start=(ki == 0), stop=(ki == 7)),
                         reads=[R(("mixT", k), 128 * jl, 128 * jl + 128), R(("wout", k), 512 * half, 512 * half + 512)],
                         writes=[R(PJN[hb[half]])])

        def e_adds(jl, hb):
            xb = xr[jl]
            for half in range(2):
                P.op("dve", lambda e, half=half: e.tensor_tensor(out=xres[xb][:, 512 * half:512 * half + 512], in0=PJB[hb[half]][:, :],
                                                                 in1=xres[xb][:, 512 * half:512 * half + 512], op=ALU.add),
                     reads=[R(PJN[hb[half]]), R(("xres", xb))], writes=[R(("xres", xb))])

        def e_norm_store(s, c, jl):
            r0 = 512 * c + 128 * jl
            xb = xr[jl]
            sl = nxt("sl", 2)
            P.op("act", lambda e: e.activation(out=sqj[:], in_=xres[xb][:], func=AF.Square, accum_out=col(SSE + sl)),
                 reads=[R(("xres", xb))], writes=[R("sqj"), R(("sse", sl))])
            rsqrt_dve(col(RSE + sl), col(SSE + sl), 1.0 / D, EPS, [R(("sse", sl))], [R(("rse", sl))])
            P.op("act", lambda e: e.activation(out=xres[xb][:], in_=xres[xb][:], func=AF.Copy, scale=col(RSE + sl)),
                 reads=[R(("xres", xb)), R(("rse", sl))], writes=[R(("xres", xb))])
            P.op("pool", lambda e: e.tensor_tensor(out=xres[xb][:], in0=xres[xb][:], in1=fnw, op=ALU.mult),
                 reads=[R(("xres", xb)), R("sm")], writes=[R(("xres", xb))])
            P.dma(out_d[s, r0:r0 + 128, :], xres[xb][:], reads=[R(("xres", xb))], out=True)
            if jl == 0:
                e_load(s, c, 3)

        chunks = [(s, c) for s in range(S) for c in range(NCH)]

        def x_tile(i, jl):
            s, c = chunks[i]
            r0 = 512 * c + 128 * jl
            return x_d[s, r0:r0 + 128, :]

        wp = weight_pieces()
        stage_A(meta_d, NMETA, muT, "muT", 0)
        for jl in range(4):
            for f in wp[4 * jl:4 * jl + 4]:
                f()
            stage_A(x_tile(0, jl), 128, uT, "uT", 128 * jl)
        for f in wp[16:]:
            f()
        def restore_state():
            P.op("pool", lambda e: e.tensor_copy(out=xl[:, :, 0:3], in_=xlm_tail[:, :, :]),
                 reads=[R("xlm_tail", 0, 4)], writes=[R(("xl", ct), 0, 3) for ct in range(4)])
            P.op("pool", lambda e: e.tensor_copy(out=col(CARRY, 4), in_=col(CARRYM, 4)),
                 reads=[R("carrym", 0, 4)], writes=[R("carry", 0, 4)])

        for i, (s, c) in enumerate(chunks):
            P.dma(cs[:], cs_d[:, :, NMETA + 512 * c:NMETA + 512 * c + 512], writes=[R("cs")])
            if i == 0:
                def meta_pass():
                    stage_B(muT, "muT", NMETA, 0, True, [(NMETA, 0, 0)])
                    stage_C(NMETA, True)
                    restore_state()
                    stage_C_conv(0, 512, False)

                stage_B(uT, "uT", 512, NMETA + 512 * c, False, [(128, 1 + 4 * c + jl, 128 * jl) for jl in range(4)], after_xl0=meta_pass)
            if c == 0 and i > 0:
                restore_state()
            if i > 0:
                stage_B(uT, "uT", 512, NMETA + 512 * c, False, [(128, 1 + 4 * c + jl, 128 * jl) for jl in range(4)],
                        after_xl0=lambda: stage_C_conv(0, 512, False))
            nxt_i = i + 1 if i + 1 < len(chunks) else None
            st = {}

            def a_pieces(k, nxt_i=nxt_i, st=st):
                Dd = {}
                if nxt_i is None:
                    return Dd
                if 2 <= k <= 5:
                    Dd["A2"] = lambda: stage_A2(st[("xb", k - 2)], 128, uT, "uT", 128 * (k - 2))
                if 1 <= k <= 4:
                    def f1():
                        st[("xb", k - 1)] = stage_A1b(st[("a", k - 1)])
                    Dd["A1b"] = f1
                    if k <= 3:
                        def rs():
                            b, sl, rows = st[("a", k - 1)]
                            return (col(RSA + sl), col(SSA + sl), 1.0 / D, EPS, [R(("ssa", sl))], [R(("rsa", sl))])
                        Dd["a1a_rs"] = rs
                if k <= 3:
                    def f2():
                        st[("a", k)] = stage_A1a(x_tile(nxt_i, k), 128, defer_rs=(k <= 2))
                    Dd["A1a"] = f2
                return Dd

            def pieces_for(h):
                g0, g1, p_exp, p_sqrt, p_tail = c_pieces(h, 512)
                pc = {"g0": g0, "g1": g1, "p_exp": p_exp, "p_sqrt": p_sqrt, "p_tail": p_tail}
                if h < 3:
                    pc["conv"] = lambda: stage_C_conv(h + 1, 512, False)
                pc.update(a_pieces(h))
                return pc

            for jl in range(3):
                e_load(s, c, jl)
            stage_D(c, pieces_for)
            ap4 = a_pieces(4)
            for key in ("A2", "A1b"):
                if key in ap4:
                    ap4[key]()

            def mid():
                att_T(3)
                ap5 = a_pieces(5)
                if "A2" in ap5:
                    ap5["A2"]()

            stage_E(s, c, mid=mid)
        P.emit()
    return nc


def _host_layout(inputs, n_cores):
    f32 = np.float32
    x = np.asarray(inputs["x"], dtype=f32)
    B, SEQ, _ = x.shape
    S = B // n_cores
    T = NMETA + SEQ
    small = np.zeros((128, NSM), f32)
    small[:, C_NW:C_NW + 8] = np.asarray(inputs["norm_w"], f32).reshape(8, 128).T
    cw = np.asarray(inputs["conv_w"], f32).reshape(4, 4, 128)
    small[:, C_CW:C_CW + 16] = cw.transpose(2, 1, 0).reshape(128, 16)
    small[:, C_CB:C_CB + 4] = np.asarray(inputs["conv_b"], f32).reshape(4, 128).T
    small[:, C_BR:C_BR + 4] = np.asarray(inputs["b_rgate"], f32).reshape(4, 128).T
    small[:, C_BI:C_BI + 4] = np.asarray(inputs["b_igate"], f32).reshape(4, 128).T
    small[:, C_L:C_L + 4] = np.asarray(inputs["lru_L"], f32).reshape(4, 128).T
    lamv = np.concatenate([np.asarray(inputs[k], f32).reshape(64) for k in ("lambda_q1", "lambda_k1", "lambda_q2", "lambda_k2")])
    small[:, C_LAM:C_LAM + 256] = np.broadcast_to(lamv[None, :], (128, 256))
    small[:, C_SUB:C_SUB + 512] = np.broadcast_to(np.tile(np.asarray(inputs["subln_w"], f32).reshape(128), 4)[None, :], (128, 512))
    small[:, C_FNW:C_FNW + 1024] = np.broadcast_to(np.asarray(inputs["final_norm_w"], f32).reshape(1024)[None, :], (128, 1024))
    wg = np.zeros((128, 2, 4, 128), f32)
    for gi, key in enumerate(("w_rgate", "w_igate")):
        w = np.asarray(inputs[key], f32).reshape(8, 64, 64)
        for ct in range(4):
            wg[0:64, gi, ct, 0:64] = w[2 * ct]
            wg[64:128, gi, ct, 64:128] = w[2 * ct + 1]
    wg = wg.reshape(128, 1024)
    c16 = np.zeros((128, 256), f32)
    c16[:, 0:128] = np.eye(128, dtype=f32)
    for p in range(128):
        d = p % 64
        if d < 8:
            c16[p + 8, 128 + p] = -1.0
        elif d < 16:
            c16[p - 8, 128 + p] = 1.0
    c16 = c16.astype(ml_dtypes.bfloat16)
    pos = np.arange(T, dtype=f32)
    inv = np.power(f32(ROPE_THETA), -np.arange(0, 16, 2, dtype=f32) / f32(16)).astype(f32)
    ang = (pos[:, None] * inv[None, :]).astype(f32)
    cs = np.zeros((128, 2, T), f32)
    cs[:, 0, :] = 1.0
    for p in range(128):
        d = p % 64
        if d < 16:
            cs[p, 0, :] = np.cos(ang[:, d % 8])
            cs[p, 1, :] = np.sin(ang[:, d % 8])
    shared = {
        "meta": np.ascontiguousarray(np.asarray(inputs["meta_tokens"], f32)),
        "w_in": np.ascontiguousarray(np.asarray(inputs["w_in"], f32).reshape(D, DIN)),
        "w_out": np.ascontiguousarray(np.asarray(inputs["w_out"], f32).reshape(D, D)),
        "small": small, "wgate": wg, "c16": c16, "cs": cs,
    }
    in_maps = []
    for i in range(n_cores):
        m = dict(shared)
        m["x"] = np.ascontiguousarray(x[i * S:(i + 1) * S])
        in_maps.append(m)
    return in_maps, S, SEQ


def kernel(**inputs):
    n_cores = 8
    in_maps, S, SEQ = _host_layout(inputs, n_cores)
    nc = build_nc(S, SEQ // 512)
    res = run_bass_kernel_spmd(nc, in_maps, core_ids=list(range(n_cores)))
    return np.concatenate([np.asarray(r["out"]) for r in res.results], axis=0).astype(np.float32)
```

```python
import numpy as np
import ml_dtypes
import concourse.bass as bass
import concourse.mybir as mybir
from concourse.bass_utils import run_bass_kernel_spmd
from contextlib import ExitStack

F32 = mybir.dt.float32
BF16 = mybir.dt.bfloat16
AF = mybir.ActivationFunctionType
ALU = mybir.AluOpType


class _Op:
    __slots__ = ("eng", "idx", "emit", "is_dma", "deps", "signal", "sigval", "sem", "semval", "prev_same_sem", "is_out")

    def __init__(self, eng, idx, emit, is_dma):
        self.eng = eng
        self.idx = idx
        self.emit = emit
        self.is_dma = is_dma
        self.deps = {}
        self.signal = False
        self.sigval = None
        self.sem = None
        self.semval = None
        self.prev_same_sem = None
        self.is_out = False


class Prog:
    ENGS = ("pe", "act", "dve", "pool", "sp")

    def __init__(self, nc, es, n_dma_sems=24, same_engine_raw=True):
        self.nc = nc
        self.es = es
        self.ops = {e: [] for e in self.ENGS}
        self.buf = {}
        self.same_engine_raw = same_engine_raw
        self.eng_sem = {e: es.enter_context(nc.semaphore("sem_" + e)) for e in ("pe", "act", "dve", "pool")}
        self.dma_sems = [es.enter_context(nc.semaphore("sem_dma%d" % i)) for i in range(n_dma_sems)]
        self.n_dma = 0
        self.dma_last = [None] * n_dma_sems

    def _segs(self, name, lo, hi):
        L = self.buf.get(name, [])
        out, new, cur = [], [], lo
        for s in L:
            if s[1] <= lo or s[0] >= hi:
                new.append(s)
                continue
            if s[0] < lo:
                new.append([s[0], lo, s[2], dict(s[3])])
                s = [lo, s[1], s[2], s[3]]
            tail = None
            if s[1] > hi:
                tail = [hi, s[1], s[2], dict(s[3])]
                s = [s[0], hi, s[2], s[3]]
            if cur < s[0]:
                g = [cur, s[0], None, {}]
                new.append(g)
                out.append(g)
            new.append(s)
            out.append(s)
            cur = s[1]
            if tail is not None:
                new.append(tail)
        if cur < hi:
            g = [cur, hi, None, {}]
            new.append(g)
            out.append(g)
        new.sort(key=lambda s: s[0])
        self.buf[name] = new
        return out

    def _add_dep(self, op, d, kind):
        if d is None or d is op:
            return
        if not d.is_dma and d.eng == op.eng and not op.is_dma:
            if d.eng == "pe":
                return
            if not self.same_engine_raw:
                return
        key = id(d)
        if key not in op.deps:
            op.deps[key] = d

    PSUM_NAMES = ("pj", "psc", "pv", "ptr")

    def _record(self, op, reads, writes):
        def is_psum(name):
            return (name[0] if isinstance(name, tuple) else name) in self.PSUM_NAMES
        excl = [(n, 0, 1 << 30) for (n, lo, hi) in list(reads) + list(writes) if is_psum(n)]
        reads = [r for r in reads if not is_psum(r[0])]
        writes = [w for w in writes if not is_psum(w[0])] + excl
        for (name, lo, hi) in reads:
            for s in self._segs(name, lo, hi):
                self._add_dep(op, s[2], "raw")
                rk = ("dma", id(op)) if op.is_dma else op.eng
                s[3][rk] = op
        for (name, lo, hi) in writes:
            for s in self._segs(name, lo, hi):
                self._add_dep(op, s[2], "waw")
                for r in s[3].values():
                    self._add_dep(op, r, "war")
                s[2] = op
                s[3] = {}

    def op(self, eng, emit, reads=(), writes=()):
        o = _Op(eng, len(self.ops[eng]), emit, False)
        self.ops[eng].append(o)
        self._record(o, reads, writes)
        return o

    def dma(self, out_ap, in_ap, reads=(), writes=(), out=False, **kw):
        def emit(e):
            return e.dma_start(out=out_ap, in_=in_ap, **kw)
        o = _Op("sp", len(self.ops["sp"]), emit, True)
        j = self.n_dma
        self.n_dma += 1
        n = len(self.dma_sems)
        o.sem = self.dma_sems[j % n]
        o.semval = 16 * (j // n + 1)
        o.prev_same_sem = self.dma_last[j % n]
        self.dma_last[j % n] = o
        o.is_out = out
        self.ops["sp"].append(o)
        self._record(o, reads, writes)
        return o

    def _finalize(self):
        self.waits = {}
        for e in self.ENGS:
            waited = {}
            for o in self.ops[e]:
                need = {}
                deps = list(o.deps.values())
                if o.is_dma and o.prev_same_sem is not None:
                    deps.append(o.prev_same_sem)
                for d in deps:
                    if d.is_dma:
                        k = ("dma", id(d.sem))
                        v = d.semval
                    else:
                        k = ("eng", d.eng)
                        v = d.idx
                    if waited.get(k, -1) >= v:
                        continue
                    if k not in need or need[k][0] < v:
                        need[k] = (v, d)
                for k, (v, d) in need.items():
                    waited[k] = v
                    if not d.is_dma:
                        d.signal = True
                self.waits[id(o)] = [d for (_, d) in need.values()]
        for e in ("pe", "act", "dve", "pool"):
            c = 0
            for o in self.ops[e]:
                if o.signal:
                    c += 1
                    o.sigval = c

    def emit(self):
        self._finalize()
        nc = self.nc
        handles = {"pe": nc.tensor, "act": nc.scalar, "dve": nc.vector, "pool": nc.gpsimd, "sp": nc.sync}

        def run(e, h):
            for o in self.ops[e]:
                for d in self.waits[id(o)]:
                    if d.is_dma:
                        h.wait_ge(d.sem, d.semval)
                    else:
                        h.wait_ge(self.eng_sem[d.eng], d.sigval)
                inst = o.emit(h)
                if o.is_dma:
                    inst.then_inc(o.sem, 16)
                elif o.signal:
                    inst.then_inc(self.eng_sem[e], 1)
            if e == "sp":
                for o in self.dma_last:
                    if o is not None:
                        h.wait_ge(o.sem, o.semval)

        with nc.Block() as block:
            @block.tensor
            def _(h):
                run("pe", h)

            @block.scalar
            def _(h):
                run("act", h)

            @block.vector
            def _(h):
                run("dve", h)

            @block.gpsimd
            def _(h):
                run("pool", h)

            @block.sync
            def _(h):
                run("sp", h)


D = 1024
DIN = 3072
NMETA = 16
EPS = 1e-6
SUBLN_EPS = 1e-5
LAM_INIT = 0.8 - 0.6
ROPE_THETA = 500000.0
AX = mybir.AxisListType.X

C_NW, C_CW, C_CB, C_BR, C_BI, C_L, C_LAM, C_SUB, C_FNW, NSM = 0, 8, 24, 28, 32, 36, 40, 296, 808, 1832


def R(name, lo=0, hi=1):
    return (name, lo, hi)


def build_nc(S, NCH):
    SEQ = 512 * NCH
    T = NMETA + SEQ
    NT = SEQ // 128
    nc = bass.Bass("TRN2", target_bir_lowering=False)

    def dram(name, shape, dtype=F32, kind="ExternalInput"):
        return nc.dram_tensor(name, shape, dtype, kind=kind).ap()

    x_d = dram("x", [S, SEQ, D])
    meta_d = dram("meta", [NMETA, D])
    win_d = dram("w_in", [D, DIN])
    wout_d = dram("w_out", [D, D])
    small_d = dram("small", [128, NSM])
    wgate_d = dram("wgate", [128, 1024])
    c16_d = dram("c16", [128, 256], BF16)
    cs_d = dram("cs", [128, 2, T])
    out_d = dram("out", [S, SEQ, D], kind="ExternalOutput")

    with ExitStack() as es:
        P = Prog(nc, es)

        def sb(name, shape, dtype):
            return es.enter_context(nc.sbuf_tensor("s_" + name, shape, dtype))

        def ps(name, shape, dtype):
            return es.enter_context(nc.psum_tensor("p_" + name, shape, dtype))

        win_bf = sb("win_bf", [128, 8, DIN], BF16)
        wout_bf = sb("wout_bf", [128, 8, D], BF16)
        NST = 3
        stage = [sb("stage%d" % i, [128, 1024], F32) for i in range(NST)]
        xn = [sb("xn%d" % i, [128, 1024], BF16) for i in range(2)]
        sm = sb("sm", [128, NSM], F32)
        c16 = sb("c16", [128, 256], BF16)
        wg_bf = sb("wg_bf", [128, 8, 128], BF16)
        kT = sb("kT", [128, 4, T], BF16)
        Vaug = sb("Vaug", [128, (NT + 1) * 4, 130], BF16)
        uT = sb("uT", [128, 8, 512], BF16)
        muT = sb("muT", [128, 8, NMETA], BF16)
        qT = sb("qT", [128, 4, 512], BF16)
        xl = sb("xl", [128, 4, 515], F32)
        xlm = sb("xlm", [128, 4, 3 + NMETA], F32)
        gl = sb("gl", [128, 4, 512], BF16)
        gw = sb("gw", [128, 4, 512], BF16)
        mixT = sb("mixT", [128, 8, 512], BF16)
        att = sb("att", [128, 4, 512], BF16)
        cs = sb("cs", [128, 2, 512], F32)
        csm = sb("csm", [128, 2, NMETA], F32)
        rt = [sb("rt%d" % i, [128, 512], F32) for i in range(2)]
        xc2 = [sb("xc%d" % i, [128, 512], F32) for i in range(2)]
        xcb2 = [sb("xcb%d" % i, [128, 512], BF16) for i in range(2)]
        l_tr = sb("l_tr", [128, 512], F32)
        l_ti = sb("l_ti", [128, 512], F32)
        l_a = sb("l_a", [128, 512], F32)
        l_m = sb("l_m", [128, 512], F32)
        tg = [l_a, l_m]
        TGN = ["l_a", "l_m"]
        NET = 4
        ET = [sb("ET%d" % i, [128, 512], BF16) for i in range(NET)]
        oj = sb("oj", [128, 128], BF16)
        xres = [sb("xres%d" % i, [128, 1024], F32) for i in range(3)]
        sc = sb("sc", [128, 128], F32)
        lt = sb("lt", [128, 128], F32)
        xlm_tail = sb("xlm_tail", [128, 4, 3], F32)
        sqj = sb("sqj", [128, 1024], BF16)
        rsx = [sb("rsx%d" % i, [128, 4], F32) for i in range(2)]
        rsy = [sb("rsy%d" % i, [128, 4], F32) for i in range(2)]
        rsa = [sb("rsa%d" % i, [128, 4], F32) for i in range(2)]

        ident = c16[:, 0:128]
        permT = c16[:, 128:256]
        MHALF, LAM, NLAM, LS, LE, SPE, SP, C8, C4, HB, CARRY, CARRYM = 0, 1, 2, 3, 5, 7, 11, 15, 19, 23, 31, 35
        SSA, RSA, SSE, RSE, SSD, RSD, RL = 80, 84, 44, 46, 48, 56, 64

        def col(i, n=1, rows=128):
            return sc[0:rows, i:i + n]

        pj = [ps("pj%d" % i, [128, 512], F32) for i in range(2)]
        ptr = ps("ptr", [128, 8, 128], BF16)
        psc = [ps("psc%d" % i, [128, 512], F32) for i in range(2)]
        pv = [ps("pv%d" % i, [128, 512], F32) for i in range(3)]

        PJB = [pj[0], pj[1], psc[0], psc[1], pv[0], pv[1], pv[2]]
        PJN = [("pj", 0), ("pj", 1), ("psc", 0), ("psc", 1), ("pv", 0), ("pv", 1), ("pv", 2)]
        NPJ = 7
        cnt = {"st": 0, "pj": 0, "xn": 0, "rt": 0, "tg": 0, "psc": 0, "ET": 0, "ot": 0, "xres": 0, "sl": 0, "sla": 0, "pjc": 0}

        def nxt(key, n):
            v = cnt[key] % n
            cnt[key] += 1
            return v

        P.dma(sm[:], small_d, writes=[R("sm")])
        P.dma(c16[:], c16_d, writes=[R("c16")])
        P.dma(csm[:], cs_d[:, :, 0:NMETA], writes=[R("csm")])
        P.op("pool", lambda e: e.memset(Vaug[:, :, 128:129], 1.0), writes=[R("V", 0, NT + 1)])
        P.op("pool", lambda e: e.memset(xlm[:, :, 0:3], 0.0), writes=[R(("xlm", ct), 0, 3) for ct in range(4)])
        P.op("pool", lambda e: e.memset(col(CARRY, 4), 0.0), writes=[R("carry")])
        P.op("pool", lambda e: e.memset(sc[0:64, 39:40], 0.0), writes=[R("maskb", 0, 1)])
        P.op("pool", lambda e: e.memset(sc[64:128, 39:40], -30000.0), writes=[R("maskb", 1, 2)])
        for i in range(2):
            P.op("dve", lambda e, i=i: e.tensor_tensor(out=lt[:, 64 * i:64 * i + 64], in0=sm[:, C_LAM + 128 * i:C_LAM + 128 * i + 64],
                                                       in1=sm[:, C_LAM + 128 * i + 64:C_LAM + 128 * i + 128], op=ALU.mult),
                 reads=[R("sm")], writes=[R("lt", i, i + 1)])
            P.op("dve", lambda e, i=i: e.reduce_sum(out=col(LS + i), in_=lt[:, 64 * i:64 * i + 64], axis=AX),
                 reads=[R("lt", i, i + 1)], writes=[R("ls", i, i + 1)])
        P.op("act", lambda e: e.activation(out=col(LE, 2), in_=col(LS, 2), func=AF.Exp), reads=[R("ls", 0, 2)], writes=[R("le")])
        P.op("pool", lambda e: e.tensor_tensor(out=col(LAM), in0=col(LE), in1=col(LE + 1), op=ALU.subtract),
             reads=[R("le")], writes=[R("lam")])
        P.op("pool", lambda e: e.tensor_scalar(out=col(NLAM), in0=col(LAM), scalar1=-1.0, scalar2=-LAM_INIT, op0=ALU.mult, op1=ALU.add),
             reads=[R("lam")], writes=[R("nlam")])
        P.op("act", lambda e: e.activation(out=col(SPE, 4), in_=sm[:, C_L:C_L + 4], func=AF.Exp, scale=-1.0),
             reads=[R("sm")], writes=[R("spe")])
        P.op("act", lambda e: e.activation(out=col(SP, 4), in_=col(SPE, 4), func=AF.Ln, bias=1.0, scale=1.0),
             reads=[R("spe")], writes=[R("sp")])
        P.op("pool", lambda e: e.tensor_scalar(out=col(C8, 4), in0=col(SP, 4), scalar1=-8.0, scalar2=None, op0=ALU.mult),
             reads=[R("sp")], writes=[R("c8")])
        P.op("pool", lambda e: e.tensor_scalar(out=col(C4, 4), in0=col(SP, 4), scalar1=-4.0, scalar2=None, op0=ALU.mult),
             reads=[R("sp")], writes=[R("c4")])
        P.op("pool", lambda e: e.tensor_scalar(out=col(HB, 8), in0=sm[:, C_BR:C_BR + 8], scalar1=0.5, scalar2=None, op0=ALU.mult),
             reads=[R("sm")], writes=[R("hb")])
        P.op("pool", lambda e: e.tensor_scalar(out=sm[:, C_SUB:C_SUB + 512], in0=sm[:, C_SUB:C_SUB + 512], scalar1=0.5 * (1.0 - LAM_INIT),
                                               scalar2=None, op0=ALU.mult),
             reads=[R("sm")], writes=[R("subg")])
        sublng = sm[:, C_SUB:C_SUB + 512]
        fnw = sm[:, C_FNW:C_FNW + 1024]
        b = nxt("st", NST)
        P.dma(stage[b][:], wgate_d, writes=[R(("st", b))])
        P.op("pool", lambda e, b=b: e.tensor_copy(out=wg_bf[:].rearrange("p a b -> p (a b)"), in_=stage[b][:]),
             reads=[R(("st", b))], writes=[R("wg")])

        I32 = mybir.dt.int32
        RS_MAGIC = 1597463007.0
        rs_cnt = [0]

        def rsqrt_dve(dst, src, scale, eps, src_regs, dst_regs, rows=128, n=1):
            u = rs_cnt[0] % 2
            rs_cnt[0] += 1
            X = rsx[u][0:rows, 0:n]
            Y = rsy[u][0:rows, 0:n]
            A = rsa[u][0:rows, 0:n]
            rx, ry, ra = R(("rsx", u)), R(("rsy", u)), R(("rsa", u))
            P.op("dve", lambda e: e.tensor_scalar(out=X, in0=src, scalar1=scale, scalar2=eps, op0=ALU.mult, op1=ALU.add),
                 reads=src_regs, writes=[rx])
            P.op("dve", lambda e: e.tensor_scalar(out=Y.bitcast(I32), in0=X.bitcast(I32), scalar1=-0.5, scalar2=RS_MAGIC, op0=ALU.mult, op1=ALU.add),
                 reads=[rx], writes=[ry])
            for it in range(3):
                last = it == 2
                P.op("dve", lambda e: e.tensor_tensor(out=A, in0=Y, in1=Y, op=ALU.mult), reads=[ry], writes=[ra])
                P.op("dve", lambda e: e.tensor_tensor(out=A, in0=A, in1=X, op=ALU.mult), reads=[ra, rx], writes=[ra])
                P.op("dve", lambda e: e.tensor_scalar(out=A, in0=A, scalar1=-0.5, scalar2=1.5, op0=ALU.mult, op1=ALU.add), reads=[ra], writes=[ra])
                if last:
                    P.op("dve", lambda e: e.tensor_tensor(out=dst, in0=Y, in1=A, op=ALU.mult), reads=[ry, ra], writes=dst_regs)
                else:
                    P.op("dve", lambda e: e.tensor_tensor(out=Y, in0=Y, in1=A, op=ALU.mult), reads=[ry, ra], writes=[ry])

        def stage_A1a(src_ap, rows, defer_rs=False):
            b = nxt("st", NST)
            sl = nxt("sla", 4)
            P.dma(stage[b][0:rows, :], src_ap, writes=[R(("st", b))])
            P.op("dve", lambda e: e.scalar_tensor_tensor(out=sqj[0:rows, :], in0=stage[b][0:rows, :], scalar=1.0, in1=stage[b][0:rows, :],
                                                         op0=ALU.mult, op1=ALU.mult, accum_out=col(SSA + sl, 1, rows)),
                 reads=[R(("st", b))], writes=[R("sqj"), R(("ssa", sl))])
            if not defer_rs:
                rsqrt_dve(col(RSA + sl, 1, rows), col(SSA + sl, 1, rows), 1.0 / D, EPS, [R(("ssa", sl))], [R(("rsa", sl))], rows=rows)
            return (b, sl, rows)

        def rsqrt_act(dst, src, scale, eps, src_regs, dst_regs, n=1):
            u = rs_cnt[0] % 2
            rs_cnt[0] += 1
            X = rsx[u][:, 0:n]
            P.op("act", lambda e: e.activation(out=X, in_=src, func=AF.Sqrt, bias=eps, scale=scale), reads=src_regs, writes=[R(("rsx", u))])
            P.op("dve", lambda e: e.reciprocal(out=dst, in_=X), reads=[R(("rsx", u))], writes=dst_regs)

        def stage_A1b(ctx):
            b, sl, rows = ctx
            xb = nxt("xn", 2)
            P.op("dve", lambda e: e.tensor_scalar(out=xn[xb][0:rows, :], in0=stage[b][0:rows, :], scalar1=col(RSA + sl, 1, rows),
                                                  scalar2=None, op0=ALU.mult),
                 reads=[R(("st", b)), R(("rsa", sl))], writes=[R(("xn", xb))])
            return xb

        def stage_A1(src_ap, rows):
            return stage_A1b(stage_A1a(src_ap, rows))

        def stage_A2(xb, rows, dst, dst_name, col0):
            for k in range(8):
                P.op("pe", lambda e, k=k: e.transpose(ptr[:, k, 0:rows], xn[xb][0:rows, 128 * k:128 * k + 128], ident[0:rows, 0:rows]),
                     reads=[R(("xn", xb)), R("c16")], writes=[R("ptr", k, k + 1)])
            P.op("dve", lambda e: e.tensor_copy(out=dst[:, :, col0:col0 + rows], in_=ptr[:, :, 0:rows]),
                 reads=[R("ptr", 0, 8)], writes=[R(dst_name, col0, col0 + rows)])

        def stage_A(src_ap, rows, dst, dst_name, col0):
            b = nxt("st", NST)
            sl = nxt("sla", 4)
            xb = nxt("xn", 2)
            P.dma(stage[b][0:rows, :], src_ap, writes=[R(("st", b))])
            P.op("act", lambda e: e.activation(out=sqj[0:rows, :], in_=stage[b][0:rows, :], func=AF.Square, accum_out=col(SSA + sl, 1, rows)),
                 reads=[R(("st", b))], writes=[R("sqj"), R(("ssa", sl))])
            u = rs_cnt[0] % 2
            rs_cnt[0] += 1
            X = rsx[u][0:rows, 0:1]
            P.op("act", lambda e: e.activation(out=X, in_=col(SSA + sl, 1, rows), func=AF.Sqrt, bias=EPS, scale=1.0 / D),
                 reads=[R(("ssa", sl))], writes=[R(("rsx", u))])
            P.op("dve", lambda e: e.reciprocal(out=col(RSA + sl, 1, rows), in_=X), reads=[R(("rsx", u))], writes=[R(("rsa", sl))])
            P.op("act", lambda e: e.activation(out=xn[xb][0:rows, :], in_=stage[b][0:rows, :], func=AF.Copy, scale=col(RSA + sl, 1, rows)),
                 reads=[R(("st", b)), R(("rsa", sl))], writes=[R(("xn", xb))])
            stage_A2(xb, rows, dst, dst_name, col0)

        def weight_pieces():
            L = []
            cntw = [0]

            def win_piece(cg, k):
                def f():
                    b = cntw[0] % 3
                    cntw[0] += 1
                    P.dma(xres[b][:], win_d[128 * k:128 * k + 128, 1024 * cg:1024 * cg + 1024], writes=[R(("xres", b))])
                    if cntw[0] % 2 == 0:
                        P.op("dve", lambda e: e.tensor_scalar(out=win_bf[:, k, 1024 * cg:1024 * cg + 1024], in0=xres[b][:],
                                                              scalar1=sm[:, C_NW + k:C_NW + k + 1], scalar2=None, op0=ALU.mult),
                             reads=[R(("xres", b)), R("sm")], writes=[R(("win", k), 1024 * cg, 1024 * cg + 1024)])
                    else:
                        P.op("act", lambda e: e.activation(out=win_bf[:, k, 1024 * cg:1024 * cg + 1024], in_=xres[b][:],
                                                           func=AF.Copy, scale=sm[:, C_NW + k:C_NW + k + 1]),
                             reads=[R(("xres", b)), R("sm")], writes=[R(("win", k), 1024 * cg, 1024 * cg + 1024)])
                return f

            def wout_piece(k):
                def f():
                    b = cntw[0] % 3
                    cntw[0] += 1
                    P.dma(xres[b][:], wout_d[128 * k:128 * k + 128, :], writes=[R(("xres", b))])
                    if k % 2 == 0:
                        P.op("dve", lambda e: e.tensor_scalar(out=wout_bf[:, k, :], in0=xres[b][:], scalar1=(1.0 if k < 4 else 0.5),
                                                              scalar2=None, op0=ALU.mult),
                             reads=[R(("xres", b))], writes=[R(("wout", k), 0, 1024)])
                    else:
                        P.op("act", lambda e: e.activation(out=wout_bf[:, k, :], in_=xres[b][:], func=AF.Copy, scale=(1.0 if k < 4 else 0.5)),
                             reads=[R(("xres", b))], writes=[R(("wout", k), 0, 1024)])
                return f

            for cg in range(3):
                for k in range(8):
                    L.append(win_piece(cg, k))
            for k in range(8):
                L.append(wout_piece(k))
            return L

        def proj_group(lhs_fn, rhs_fn, outrows, N, reads_fn):
            b = nxt("pj", NPJ)
            for k in range(8):
                P.op("pe", lambda e, k=k: e.matmul(PJB[b][0:outrows, 0:N], lhsT=lhs_fn(k), rhs=rhs_fn(k), start=(k == 0), stop=(k == 7)),
                     reads=reads_fn(k), writes=[R(PJN[b])])
            return b

        def stage_B(u, u_name, N, p0, meta, tiles, after_xl0=None):
            cos = csm[:, 0, :] if meta else cs[:, 0, :]
            sin = csm[:, 1, :] if meta else cs[:, 1, :]
            cs_name = "csm" if meta else "cs"

            def fm(c0):
                return proj_group(lambda k: win_bf[:, k, c0:c0 + 128], lambda k: u[:, k, 0:N], 128, N,
                                  lambda k: [R(("win", k), c0, c0 + 128), R(u_name, 0, N)])

            ropes = []
            for h in range(4):
                b = fm(512 + 128 * h)
                dest = kT[:, h, p0:p0 + N]
                dreg = R(("kT", h), p0, p0 + N)
                P.op("act", lambda e, b=b, dest=dest: e.activation(out=dest, in_=PJB[b][:, 0:N], func=AF.Copy),
                     reads=[R(PJN[b])], writes=[dreg])
                ropes.append((dest, dreg))
            if not meta:
                for h in range(4):
                    b = fm(128 * h)
                    dest = qT[:, h, 0:N]
                    dreg = R(("qT", h), 0, N)
                    P.op("act", lambda e, b=b, dest=dest: e.activation(out=dest, in_=PJB[b][:, 0:N], func=AF.Copy),
                         reads=[R(PJN[b])], writes=[dreg])
                    ropes.append((dest, dreg))

            def do_rope(dest, dreg):
                b2 = nxt("pj", NPJ)
                r = nxt("rt", 1)
                t1, t2 = rt[2 * r], rt[2 * r + 1]
                P.op("pe", lambda e: e.matmul(PJB[b2][:, 0:N], lhsT=permT, rhs=dest, start=True, stop=True),
                     reads=[dreg, R("c16")], writes=[R(PJN[b2])])
                P.op("pool", lambda e: e.tensor_tensor(out=t1[:, 0:N], in0=dest, in1=cos, op=ALU.mult),
                     reads=[dreg, R(cs_name)], writes=[R(("rt", 2 * r), 0, 512)])
                P.op("dve", lambda e: e.tensor_tensor(out=t2[:, 0:N], in0=PJB[b2][:, 0:N], in1=sin, op=ALU.mult),
                     reads=[R(PJN[b2]), R(cs_name)], writes=[R(("rt", 2 * r + 1), 0, 512)])
                P.op("pool", lambda e: e.tensor_tensor(out=dest, in0=t1[:, 0:N], in1=t2[:, 0:N], op=ALU.add),
                     reads=[R(("rt", 2 * r), 0, 512), R(("rt", 2 * r + 1), 0, 512)], writes=[dreg])

            for (rows, vt, c0l) in tiles:
                b = proj_group(lambda k, c0l=c0l, rows=rows: u[:, k, c0l:c0l + rows], lambda k: win_bf[:, k, 1024:1536], rows, 512,
                               lambda k: [R(("win", k), 1024, 1536), R(u_name, c0l, c0l + rows)])
                P.op("dve", lambda e, b=b, rows=rows, vt=vt: e.tensor_copy(out=Vaug[0:rows, 4 * vt:4 * vt + 4, 0:128],
                                                                          in_=PJB[b][0:rows, :].rearrange("p (h d) -> p h d", h=4)),
                     reads=[R(PJN[b])], writes=[R("V", vt, vt + 1)])
            for (dest, dreg) in ropes[0:4]:
                do_rope(dest, dreg)
            if not meta:
                for jl, (rows, vt, c0l) in enumerate(tiles):
                    b = proj_group(lambda k, c0l=c0l, rows=rows: u[:, k, c0l:c0l + rows], lambda k: win_bf[:, k, 1536:2048], rows, 512,
                                   lambda k: [R(("win", k), 1536, 2048), R(u_name, c0l, c0l + rows)])
                    g = nxt("tg", 2)
                    P.op("act", lambda e, b=b, g=g: e.activation(out=tg[g][:], in_=PJB[b][:, :], func=AF.Tanh, scale=0.5),
                         reads=[R(PJN[b])], writes=[R(TGN[g])])
                    P.op("dve", lambda e, b=b, g=g: e.scalar_tensor_tensor(out=tg[g][:], in0=tg[g][:], scalar=1.0, in1=PJB[b][:, :],
                                                                           op0=ALU.add, op1=ALU.mult),
                         reads=[R(PJN[b]), R(TGN[g])], writes=[R(TGN[g])])
                    P.op("pool", lambda e, g=g, jl=jl: e.tensor_tensor(out=gw[:, jl, :], in0=tg[g][:], in1=sublng, op=ALU.mult),
                         reads=[R(TGN[g]), R("subg")], writes=[R(("gw", jl), 0, 512)])
                for (dest, dreg) in ropes[4:8]:
                    do_rope(dest, dreg)
            for ct in range(4):
                if ct == 1 and after_xl0 is not None:
                    after_xl0()
                b = fm(2048 + 128 * ct)
                P.op("act", lambda e, b=b, ct=ct: e.activation(out=(xlm if meta else xl)[:, ct, 3:3 + N], in_=PJB[b][:, 0:N], func=AF.Copy),
                     reads=[R(PJN[b])], writes=[R(("xlm" if meta else "xl", ct), 3, 3 + N)])
            if not meta:
                for ct in range(4):
                    b = fm(2560 + 128 * ct)
                    g = nxt("tg", 2)
                    P.op("act", lambda e, b=b, g=g: e.activation(out=tg[g][:, 0:N], in_=PJB[b][:, 0:N], func=AF.Tanh, scale=0.5),
                         reads=[R(PJN[b])], writes=[R(TGN[g])])
                    P.op("dve", lambda e, b=b, g=g, ct=ct: e.scalar_tensor_tensor(out=gl[:, ct, 0:N], in0=tg[g][:, 0:N], scalar=1.0,
                                                                                 in1=PJB[b][:, 0:N], op0=ALU.add, op1=ALU.mult),
                         reads=[R(PJN[b]), R(TGN[g])], writes=[R(("gl", ct), 0, N)])

        def stage_C_conv(ct, N, meta, xl=xl):
            xc, xcb = xc2[ct % 2], xcb2[ct % 2]
            xcn, xcbn = ("xc", ct % 2), ("xcb", ct % 2)
            xln = "xlm" if meta else "xl"
            if meta:
                xl = xlm
            xlr = R((xln, ct), 0, 3 + N)
            P.op("pool", lambda e: e.tensor_scalar(out=xc[:, 0:N], in0=xl[:, ct, 0:N], scalar1=sm[:, C_CW + 4 * ct:C_CW + 4 * ct + 1],
                                                   scalar2=sm[:, C_CB + ct:C_CB + ct + 1], op0=ALU.mult, op1=ALU.add),
                 reads=[xlr, R("sm")], writes=[R(xcn)])
            for j in range(1, 4):
                P.op("dve", lambda e, j=j: e.scalar_tensor_tensor(out=xc[:, 0:N], in0=xl[:, ct, j:j + N],
                                                                  scalar=sm[:, C_CW + 4 * ct + j:C_CW + 4 * ct + j + 1],
                                                                  in1=xc[:, 0:N], op0=ALU.mult, op1=ALU.add),
                     reads=[xlr, R("sm"), R(xcn)], writes=[R(xcn)])
            if meta:
                P.op("pool", lambda e: e.tensor_copy(out=xlm_tail[:, ct, :], in_=xl[:, ct, N:N + 3]),
                     reads=[R((xln, ct), N, N + 3)], writes=[R("xlm_tail", ct, ct + 1)])
            else:
                P.op("pool", lambda e: e.tensor_copy(out=xl[:, ct, 0:3], in_=xl[:, ct, N:N + 3]),
                     reads=[R(("xl", ct), N, N + 3)], writes=[R(("xl", ct), 0, 3)])
            P.op("pool", lambda e: e.tensor_copy(out=xcb[:, 0:N], in_=xc[:, 0:N]), reads=[R(xcn)], writes=[R(xcbn)])

        def stage_C_rest(ct, N, meta, in_hook=False):
            xc, xcb = xc2[ct % 2], xcb2[ct % 2]
            xcn, xcbn = ("xc", ct % 2), ("xcb", ct % 2)
            for gi, dst, dname in ((0, l_tr, "l_tr"), (1, l_ti, "l_ti")):
                b = nxt("pjc", 2) if in_hook else nxt("pj", NPJ)
                P.op("pe", lambda e, b=b, gi=gi: e.matmul(PJB[b][:, 0:N], lhsT=wg_bf[:, 4 * gi + ct, :], rhs=xcb[:, 0:N],
                                                         start=True, stop=True),
                     reads=[R("wg"), R(xcbn)], writes=[R(PJN[b])])
                P.op("act", lambda e, b=b, gi=gi, dst=dst: e.activation(out=dst[:, 0:N], in_=PJB[b][:, 0:N], func=AF.Tanh,
                                                                       bias=col(HB + 4 * gi + ct), scale=0.5),
                     reads=[R(PJN[b]), R("hb")], writes=[R(dname)])
            P.op("act", lambda e: e.activation(out=l_a[:, 0:N], in_=l_tr[:, 0:N], func=AF.Exp, bias=col(C4 + ct), scale=col(C4 + ct)),
                 reads=[R("l_tr"), R("c4")], writes=[R("l_a")])
            P.op("pool", lambda e: e.tensor_tensor(out=l_m[:, 0:N], in0=l_a[:, 0:N], in1=l_a[:, 0:N], op=ALU.mult),
                 reads=[R("l_a")], writes=[R("l_m")])
            P.op("act", lambda e: e.activation(out=l_m[:, 0:N], in_=l_m[:, 0:N], func=AF.Sqrt, bias=0.25, scale=-0.25),
                 reads=[R("l_m")], writes=[R("l_m")])
            if meta:
                P.op("pool", lambda e: e.memset(l_m[:, 0:1], 0.5), reads=[], writes=[R("l_m")])
            P.op("dve", lambda e: e.scalar_tensor_tensor(out=l_ti[:, 0:N], in0=l_ti[:, 0:N], scalar=1.0, in1=xc[:, 0:N],
                                                         op0=ALU.add, op1=ALU.mult),
                 reads=[R("l_ti"), R(xcn)], writes=[R("l_ti")])
            P.op("pool", lambda e: e.tensor_tensor(out=l_ti[:, 0:N], in0=l_ti[:, 0:N], in1=l_m[:, 0:N], op=ALU.mult),
                 reads=[R("l_ti"), R("l_m")], writes=[R("l_ti")])
            P.op("dve", lambda e: e.tensor_tensor_scan(out=l_tr[:, 0:N], data0=l_a[:, 0:N], data1=l_ti[:, 0:N],
                                                       initial=col(CARRY + ct), op0=ALU.mult, op1=ALU.add),
                 reads=[R("l_a"), R("l_ti"), R("carry", ct, ct + 1)], writes=[R("l_tr")])
            P.op("pool", lambda e: e.tensor_copy(out=col(CARRY + ct), in_=l_tr[:, N - 1:N]),
                 reads=[R("l_tr")], writes=[R("carry", ct, ct + 1)])
            if meta:
                P.op("pool", lambda e: e.tensor_copy(out=col(CARRYM + ct), in_=l_tr[:, N - 1:N]),
                     reads=[R("l_tr")], writes=[R("carrym", ct, ct + 1)])
            else:
                P.op("pool", lambda e: e.tensor_tensor(out=mixT[:, 4 + ct, 0:N], in0=l_tr[:, 0:N], in1=gl[:, ct, 0:N], op=ALU.mult),
                     reads=[R("l_tr"), R(("gl", ct), 0, N)], writes=[R(("mixT", 4 + ct), 0, N)])

        def stage_C(N, meta):
            for ct in range(4):
                stage_C_conv(ct, N, meta)
                stage_C_rest(ct, N, meta)

        def c_pieces(ct, N):
            xc, xcb = xc2[ct % 2], xcb2[ct % 2]
            xcn, xcbn = ("xc", ct % 2), ("xcb", ct % 2)

            def gate(gi, dst, dname):
                def f():
                    b = nxt("pjc", 2)
                    P.op("pe", lambda e: e.matmul(PJB[b][:, 0:N], lhsT=wg_bf[:, 4 * gi + ct, :], rhs=xcb[:, 0:N], start=True, stop=True),
                         reads=[R("wg"), R(xcbn)], writes=[R(PJN[b])])
                    P.op("act", lambda e: e.activation(out=dst[:, 0:N], in_=PJB[b][:, 0:N], func=AF.Tanh, bias=col(HB + 4 * gi + ct), scale=0.5),
                         reads=[R(PJN[b]), R("hb")], writes=[R(dname)])
                return f

            def p_exp():
                P.op("act", lambda e: e.activation(out=l_a[:, 0:N], in_=l_tr[:, 0:N], func=AF.Exp, bias=col(C4 + ct), scale=col(C4 + ct)),
                     reads=[R("l_tr"), R("c4")], writes=[R("l_a")])
                P.op("pool", lambda e: e.tensor_tensor(out=l_m[:, 0:N], in0=l_a[:, 0:N], in1=l_a[:, 0:N], op=ALU.mult),
                     reads=[R("l_a")], writes=[R("l_m")])
                P.op("dve", lambda e: e.scalar_tensor_tensor(out=l_ti[:, 0:N], in0=l_ti[:, 0:N], scalar=1.0, in1=xc[:, 0:N],
                                                             op0=ALU.add, op1=ALU.mult),
                     reads=[R("l_ti"), R(xcn)], writes=[R("l_ti")])

            def p_sqrt():
                P.op("act", lambda e: e.activation(out=l_m[:, 0:N], in_=l_m[:, 0:N], func=AF.Sqrt, bias=0.25, scale=-0.25),
                     reads=[R("l_m")], writes=[R("l_m")])

            def p_tail():
                P.op("pool", lambda e: e.tensor_tensor(out=l_ti[:, 0:N], in0=l_ti[:, 0:N], in1=l_m[:, 0:N], op=ALU.mult),
                     reads=[R("l_ti"), R("l_m")], writes=[R("l_ti")])
                P.op("dve", lambda e: e.tensor_tensor_scan(out=l_tr[:, 0:N], data0=l_a[:, 0:N], data1=l_ti[:, 0:N],
                                                           initial=col(CARRY + ct), op0=ALU.mult, op1=ALU.add),
                     reads=[R("l_a"), R("l_ti"), R("carry", ct, ct + 1)], writes=[R("l_tr")])
                P.op("pool", lambda e: e.tensor_copy(out=col(CARRY + ct), in_=l_tr[:, N - 1:N]),
                     reads=[R("l_tr")], writes=[R("carry", ct, ct + 1)])
                P.op("pool", lambda e: e.tensor_tensor(out=mixT[:, 4 + ct, 0:N], in0=l_tr[:, 0:N], in1=gl[:, ct, 0:N], op=ALU.mult),
                     reads=[R("l_tr"), R(("gl", ct), 0, N)], writes=[R(("mixT", 4 + ct), 0, N)])

            return [gate(0, l_tr, "l_tr"), gate(1, l_ti, "l_ti"), p_exp, p_sqrt, p_tail]

        def att_T(h):
            for jl in range(4):
                P.op("pe", lambda e, jl=jl: e.transpose(ptr[:, jl, :], att[:, jl, 128 * h:128 * h + 128], ident),
                     reads=[R(("att", jl), 128 * h, 128 * h + 128), R("c16")], writes=[R("ptr", jl, jl + 1)])
            P.op("dve", lambda e: e.tensor_copy(out=mixT[:, h, :], in_=ptr[:, 0:4, :].rearrange("p j n -> p (j n)")),
                 reads=[R("ptr", 0, 4)], writes=[R(("mixT", h), 0, 512)])

        def stage_D(c, pieces_for=None):
            ktiles = [None] + list(range(4 * c + 4))
            def head(h, pieces):
                items = [(kt, m) for kt in ktiles for m in range(2)]
                started = set()
                info = {}

                def emit_scores(n):
                    kt, m = items[n]
                    b = nxt("psc", 2)
                    if kt is None:
                        K, kp0, qlo = NMETA, 0, 0
                    else:
                        K, kp0, qlo = 128, NMETA + 128 * kt, max(0, kt - 4 * c)
                    Nq = 512 - 128 * qlo
                    P.op("pe", lambda e: e.matmul(psc[b][0:K, 0:Nq], lhsT=kT[64 * m:64 * m + 64, h, kp0:kp0 + K],
                                                  rhs=qT[64 * m:64 * m + 64, h, 128 * qlo:512], start=True, stop=True),
                         reads=[R(("kT", h), kp0, kp0 + K), R(("qT", h), 128 * qlo, 512)], writes=[R(("psc", b))])
                    et = nxt("ET", NET)
                    if kt is not None and kt >= 4 * c:
                        P.op("act", lambda e: e.activation(out=ET[et][:, 0:64], in_=psc[b][:, 0:64], func=AF.Exp, scale=0.125, bias=sc[:, 39:40]),
                             reads=[R(("psc", b)), R("maskb", 0, 2)], writes=[R(("ET", et), 0, 64)])
                        P.op("act", lambda e: e.activation(out=ET[et][:, 64:Nq], in_=psc[b][:, 64:Nq], func=AF.Exp, scale=0.125),
                             reads=[R(("psc", b))], writes=[R(("ET", et), 64, 512)])
                    else:
                        P.op("act", lambda e: e.activation(out=ET[et][0:K, 0:Nq], in_=psc[b][0:K, 0:Nq], func=AF.Exp, scale=0.125),
                             reads=[R(("psc", b))], writes=[R(("ET", et), 0, 512)])
                    info[n] = (et, K, qlo)

                def emit_pv(n):
                    kt, m = items[n]
                    et, K, qlo = info[n]
                    vt = 0 if kt is None else kt + 1
                    for jl in range(qlo, 4):
                        a = 2 * jl + m
                        bank, off = a // 3, 130 * (a % 3)
                        first = bank not in started
                        started.add(bank)
                        last = (kt is not None) and (kt == 4 * c + jl)
                        P.op("pe", lambda e, jl=jl, bank=bank, off=off, first=first, last=last: e.matmul(
                            pv[bank][:, off:off + 129], lhsT=ET[et][0:K, 128 * (jl - qlo):128 * (jl - qlo) + 128],
                            rhs=Vaug[0:K, 4 * vt + h, 0:129], start=first, stop=last, skip_group_check=True),
                            reads=[R(("ET", et), 0, 512), R("V", vt, vt + 1)], writes=[R(("pv", bank), off, off + 129)])

                emit_scores(0)
                emit_scores(1)
                pi = 0
                pstride = max(1, (len(items) - 2) // max(1, len(pieces)))
                for n in range(len(items)):
                    emit_pv(n)
                    if n + 2 < len(items):
                        emit_scores(n + 2)
                    if n >= 1 and (n - 1) % pstride == 0 and pi < len(pieces):
                        pieces[pi]()
                        pi += 1
                sl = nxt("sl", 2)
                accs = []
                for a in range(8):
                    bank, off = a // 3, 130 * (a % 3)
                    raw = rt[a // 4][:, 128 * (a % 4):128 * (a % 4) + 128]
                    rawr = R(("rt", a // 4), 128 * (a % 4), 128 * (a % 4) + 128)
                    accs.append((None, None, raw, rawr, R(("rl", sl), a, a + 1), RL + 8 * sl + a))
                for bank, na in ((0, 3), (1, 3), (2, 2)):
                    a0 = 3 * bank
                    P.op("dve", lambda e, bank=bank, na=na, a0=a0: e.reciprocal(
                        out=col(RL + 8 * sl + a0, na), in_=pv[bank][:, 0:130 * na].rearrange("p (a c) -> p a c", c=130)[:, :, 128]),
                        reads=[R(("pv", bank), 0, 130 * na)], writes=[R(("rl", sl), a0, a0 + na)])
                def rawslots(lo, n):
                    t = rt[lo // 4]
                    return t[:, 128 * (lo % 4):128 * (lo % 4) + 128 * n].rearrange("p (a c) -> p a c", c=128)

                def accslots(bank, j0, n):
                    return pv[bank][:, 130 * j0:130 * (j0 + n)].rearrange("p (a c) -> p a c", c=130)[:, :, 0:128]

                for eng, bank, j0, n, lo in (("act", 0, 0, 3, 0), ("dve", 1, 0, 1, 3), ("dve", 1, 1, 2, 4), ("act", 2, 0, 2, 6)):
                    rd = [R(("pv", bank), 130 * j0, 130 * (j0 + n))]
                    wr = [R(("rt", lo // 4), 128 * (lo % 4), 128 * (lo % 4) + 128 * n)]
                    if eng == "act":
                        P.op("act", lambda e, bank=bank, j0=j0, n=n, lo=lo: e.activation(out=rawslots(lo, n), in_=accslots(bank, j0, n), func=AF.Copy),
                             reads=rd, writes=wr)
                    else:
                        P.op("dve", lambda e, bank=bank, j0=j0, n=n, lo=lo: e.tensor_copy(out=rawslots(lo, n), in_=accslots(bank, j0, n)),
                             reads=rd, writes=wr)
                while pi < len(pieces):
                    pieces[pi]()
                    pi += 1

                def p_norm():
                    for jl in range(4):
                        (acc0, rr0, raw0, rawr0, rl0, c0) = accs[2 * jl]
                        (acc1, rr1, raw1, rawr1, rl1, c1) = accs[2 * jl + 1]
                        P.op("dve", lambda e, c1=c1: e.tensor_tensor(out=col(c1), in0=col(c1), in1=col(NLAM), op=ALU.mult),
                             reads=[rl1, R("nlam")], writes=[rl1])
                        P.op("dve", lambda e, raw0=raw0, c0=c0: e.tensor_scalar(out=raw0, in0=raw0, scalar1=col(c0), scalar2=None, op0=ALU.mult),
                             reads=[rawr0, rl0], writes=[rawr0])
                        P.op("dve", lambda e, raw0=raw0, raw1=raw1, c1=c1: e.scalar_tensor_tensor(out=raw0, in0=raw1, scalar=col(c1), in1=raw0,
                                                                                                op0=ALU.mult, op1=ALU.add),
                             reads=[rawr1, rl1, rawr0], writes=[rawr0])

                o_ap = [accs[2 * jl][2] for jl in range(4)]
                o_rg = [accs[2 * jl][3] for jl in range(4)]

                class Post:
                    pass
                post = Post()
                rs_args = (col(RSD + 4 * sl, 4), col(SSD + 4 * sl, 4), 1.0 / 128, SUBLN_EPS, [R(("ssd", sl), 0, 4)], [R(("rsd", sl), 0, 4)])

                def p_sq():
                    p_norm()
                    for jl in range(4):
                        P.op("dve", lambda e, jl=jl: e.scalar_tensor_tensor(out=oj[:], in0=o_ap[jl], scalar=1.0, in1=o_ap[jl], op0=ALU.mult, op1=ALU.mult,
                                                                            accum_out=col(SSD + 4 * sl + jl)),
                             reads=[o_rg[jl]], writes=[R("oj"), R(("ssd", sl), jl, jl + 1)])

                def p_stt():
                    for jl in range(4):
                        P.op("dve", lambda e, jl=jl: e.scalar_tensor_tensor(out=att[:, jl, 128 * h:128 * h + 128], in0=o_ap[jl],
                                                                            scalar=col(RSD + 4 * sl + jl), in1=gw[:, jl, 128 * h:128 * h + 128],
                                                                            op0=ALU.mult, op1=ALU.mult),
                             reads=[o_rg[jl], R(("rsd", sl), jl, jl + 1), R(("gw", jl), 128 * h, 128 * h + 128)],
                             writes=[R(("att", jl), 128 * h, 128 * h + 128)])

                def p_full():
                    p_sq()
                    rsqrt_dve(*rs_args, n=4)
                    p_stt()

                post.sq, post.rs_args, post.stt, post.full = p_sq, rs_args, p_stt, p_full
                return post

            pend = None
            for h in range(4):
                pc = pieces_for(h)
                L = []
                if pend is not None:
                    L.append(pend.sq)
                L += [pc["g0"], pc["g1"], pc["p_exp"]]
                if pc.get("A2"):
                    L.append(pc["A2"])

                def sqrt_window(pc=pc, pend=pend):
                    pc["p_sqrt"]()
                    if pend is not None:
                        rsqrt_act(*pend.rs_args, n=4)
                    if pc.get("a1a_rs"):
                        rsqrt_act(*pc["a1a_rs"]())
                L.append(sqrt_window)
                if pend is not None:
                    L.append(pend.stt)
                if h >= 1:
                    L.append(lambda h=h: att_T(h - 1))
                if pc.get("A1b"):
                    L.append(pc["A1b"])
                if pc.get("A1a"):
                    L.append(pc["A1a"])
                if pc.get("conv"):
                    L.append(pc["conv"])
                L.append(pc["p_tail"])
                pend = head(h, L)
            pend.full()


        xr = {}

        def e_load(s, c, jl):
            r0 = 512 * c + 128 * jl
            xb = nxt("xres", 3)
            P.dma(xres[xb][:], x_d[s, r0:r0 + 128, :], writes=[R(("xres", xb))])
            xr[jl] = xb

        KORD = (4, 5, 6, 0, 1, 2, 7, 3)

        def stage_E(s, c, mid=None):
            hbs = {}
            for jl in range(3):
                hbs[jl] = [nxt("pj", NPJ), nxt("pj", NPJ)]
                e_mm(jl, hbs[jl], range(0, 7))
            if mid is not None:
                mid()
            for jl in range(3):
                e_mm(jl, hbs[jl], range(7, 8))
                e_epilogue(s, c, jl, hbs[jl])
            hbs[3] = [nxt("pj", NPJ), nxt("pj", NPJ)]
            e_mm(3, hbs[3], range(0, 8))
            e_epilogue(s, c, 3, hbs[3])

        def e_mm(jl, hb, kis):
            for half in range(2):
                for ki in kis:
                    k = KORD[ki]
                    P.op("pe", lambda e, half=half, k=k, ki=ki: e.matmul(PJB[hb[half]][:, :], lhsT=mixT[:, k, 128 * jl:128 * jl + 128],
                                                                        rhs=wout_bf[:, k, 512 * half:512 * half + 512], start=(ki == 0), stop=(ki == 7)),
                         reads=[R(("mixT", k), 128 * jl, 128 * jl + 128), R(("wout", k), 512 * half, 512 * half + 512)],
                         writes=[R(PJN[hb[half]])])

        def e_epilogue(s, c, jl, hb):
            r0 = 512 * c + 128 * jl
            xb = xr[jl]
            for half in range(2):
                P.op("dve", lambda e, half=half: e.tensor_tensor(out=xres[xb][:, 512 * half:512 * half + 512], in0=PJB[hb[half]][:, :],
                                                                 in1=xres[xb][:, 512 * half:512 * half + 512], op=ALU.add),
                     reads=[R(PJN[hb[half]]), R(("xres", xb))], writes=[R(("xres", xb))])
            sl = nxt("sl", 2)
            P.op("act", lambda e: e.activation(out=sqj[:], in_=xres[xb][:], func=AF.Square, accum_out=col(SSE + sl)),
                 reads=[R(("xres", xb))], writes=[R("sqj"), R(("sse", sl))])
            rsqrt_dve(col(RSE + sl), col(SSE + sl), 1.0 / D, EPS, [R(("sse", sl))], [R(("rse", sl))])
            P.op("act", lambda e: e.activation(out=xres[xb][:], in_=xres[xb][:], func=AF.Copy, scale=col(RSE + sl)),
                 reads=[R(("xres", xb)), R(("rse", sl))], writes=[R(("xres", xb))])
            P.op("pool", lambda e: e.tensor_tensor(out=xres[xb][:], in0=xres[xb][:], in1=fnw, op=ALU.mult),
                 reads=[R(("xres", xb)), R("sm")], writes=[R(("xres", xb))])
            P.dma(out_d[s, r0:r0 + 128, :], xres[xb][:], reads=[R(("xres", xb))], out=True)
            if jl == 0:
                e_load(s, c, 3)

        chunks = [(s, c) for s in range(S) for c in range(NCH)]

        def x_tile(i, jl):
            s, c = chunks[i]
            r0 = 512 * c + 128 * jl
            return x_d[s, r0:r0 + 128, :]

        wp = weight_pieces()
        stage_A(meta_d, NMETA, muT, "muT", 0)
        for jl in range(4):
            for f in wp[4 * jl:4 * jl + 4]:
                f()
            stage_A(x_tile(0, jl), 128, uT, "uT", 128 * jl)
        for f in wp[16:]:
            f()
        def restore_state():
            P.op("pool", lambda e: e.tensor_copy(out=xl[:, :, 0:3], in_=xlm_tail[:, :, :]),
                 reads=[R("xlm_tail", 0, 4)], writes=[R(("xl", ct), 0, 3) for ct in range(4)])
            P.op("pool", lambda e: e.tensor_copy(out=col(CARRY, 4), in_=col(CARRYM, 4)),
                 reads=[R("carrym", 0, 4)], writes=[R("carry", 0, 4)])

        for i, (s, c) in enumerate(chunks):
            P.dma(cs[:], cs_d[:, :, NMETA + 512 * c:NMETA + 512 * c + 512], writes=[R("cs")])
            if i == 0:
                def meta_pass():
                    stage_B(muT, "muT", NMETA, 0, True, [(NMETA, 0, 0)])
                    stage_C(NMETA, True)
                    restore_state()
                    stage_C_conv(0, 512, False)

                stage_B(uT, "uT", 512, NMETA + 512 * c, False, [(128, 1 + 4 * c + jl, 128 * jl) for jl in range(4)], after_xl0=meta_pass)
            if c == 0 and i > 0:
                restore_state()
            if i > 0:
                stage_B(uT, "uT", 512, NMETA + 512 * c, False, [(128, 1 + 4 * c + jl, 128 * jl) for jl in range(4)],
                        after_xl0=lambda: stage_C_conv(0, 512, False))
            nxt_i = i + 1 if i + 1 < len(chunks) else None
            st = {}

            def a_pieces(k, nxt_i=nxt_i, st=st):
                Dd = {}
                if nxt_i is None:
                    return Dd
                if 2 <= k <= 5:
                    Dd["A2"] = lambda: stage_A2(st[("xb", k - 2)], 128, uT, "uT", 128 * (k - 2))
                if 1 <= k <= 4:
                    def f1():
                        st[("xb", k - 1)] = stage_A1b(st[("a", k - 1)])
                    Dd["A1b"] = f1
                    if k <= 3:
                        def rs():
                            b, sl, rows = st[("a", k - 1)]
                            return (col(RSA + sl), col(SSA + sl), 1.0 / D, EPS, [R(("ssa", sl))], [R(("rsa", sl))])
                        Dd["a1a_rs"] = rs
                if k <= 3:
                    def f2():
                        st[("a", k)] = stage_A1a(x_tile(nxt_i, k), 128, defer_rs=(k <= 2))
                    Dd["A1a"] = f2
                return Dd

            def pieces_for(h):
                g0, g1, p_exp, p_sqrt, p_tail = c_pieces(h, 512)
                pc = {"g0": g0, "g1": g1, "p_exp": p_exp, "p_sqrt": p_sqrt, "p_tail": p_tail}
                if h < 3:
                    pc["conv"] = lambda: stage_C_conv(h + 1, 512, False)
                pc.update(a_pieces(h))
                return pc

            for jl in range(3):
                e_load(s, c, jl)
            stage_D(c, pieces_for)
            ap4 = a_pieces(4)
            for key in ("A2", "A1b"):
                if key in ap4:
                    ap4[key]()

            def mid():
                att_T(3)
                ap5 = a_pieces(5)
                if "A2" in ap5:
                    ap5["A2"]()

            stage_E(s, c, mid=mid)
        P.emit()
    return nc


def _host_layout(inputs, n_cores):
    f32 = np.float32
    x = np.asarray(inputs["x"], dtype=f32)
    B, SEQ, _ = x.shape
    S = B // n_cores
    T = NMETA + SEQ
    small = np.zeros((128, NSM), f32)
    small[:, C_NW:C_NW + 8] = np.asarray(inputs["norm_w"], f32).reshape(8, 128).T
    cw = np.asarray(inputs["conv_w"], f32).reshape(4, 4, 128)
    small[:, C_CW:C_CW + 16] = cw.transpose(2, 1, 0).reshape(128, 16)
    small[:, C_CB:C_CB + 4] = np.asarray(inputs["conv_b"], f32).reshape(4, 128).T
    small[:, C_BR:C_BR + 4] = np.asarray(inputs["b_rgate"], f32).reshape(4, 128).T
    small[:, C_BI:C_BI + 4] = np.asarray(inputs["b_igate"], f32).reshape(4, 128).T
    small[:, C_L:C_L + 4] = np.asarray(inputs["lru_L"], f32).reshape(4, 128).T
    lamv = np.concatenate([np.asarray(inputs[k], f32).reshape(64) for k in ("lambda_q1", "lambda_k1", "lambda_q2", "lambda_k2")])
    small[:, C_LAM:C_LAM + 256] = np.broadcast_to(lamv[None, :], (128, 256))
    small[:, C_SUB:C_SUB + 512] = np.broadcast_to(np.tile(np.asarray(inputs["subln_w"], f32).reshape(128), 4)[None, :], (128, 512))
    small[:, C_FNW:C_FNW + 1024] = np.broadcast_to(np.asarray(inputs["final_norm_w"], f32).reshape(1024)[None, :], (128, 1024))
    wg = np.zeros((128, 2, 4, 128), f32)
    for gi, key in enumerate(("w_rgate", "w_igate")):
        w = np.asarray(inputs[key], f32).reshape(8, 64, 64)
        for ct in range(4):
            wg[0:64, gi, ct, 0:64] = w[2 * ct]
            wg[64:128, gi, ct, 64:128] = w[2 * ct + 1]
    wg = wg.reshape(128, 1024)
    c16 = np.zeros((128, 256), f32)
    c16[:, 0:128] = np.eye(128, dtype=f32)
    for p in range(128):
        d = p % 64
        if d < 8:
            c16[p + 8, 128 + p] = -1.0
        elif d < 16:
            c16[p - 8, 128 + p] = 1.0
    c16 = c16.astype(ml_dtypes.bfloat16)
    pos = np.arange(T, dtype=f32)
    inv = np.power(f32(ROPE_THETA), -np.arange(0, 16, 2, dtype=f32) / f32(16)).astype(f32)
    ang = (pos[:, None] * inv[None, :]).astype(f32)
    cs = np.zeros((128, 2, T), f32)
    cs[:, 0, :] = 1.0
    for p in range(128):
        d = p % 64
        if d < 16:
            cs[p, 0, :] = np.cos(ang[:, d % 8])
            cs[p, 1, :] = np.sin(ang[:, d % 8])
    shared = {
        "meta": np.ascontiguousarray(np.asarray(inputs["meta_tokens"], f32)),
        "w_in": np.ascontiguousarray(np.asarray(inputs["w_in"], f32).reshape(D, DIN)),
        "w_out": np.ascontiguousarray(np.asarray(inputs["w_out"], f32).reshape(D, D)),
        "small": small, "wgate": wg, "c16": c16, "cs": cs,
    }
    in_maps = []
    for i in range(n_cores):
        m = dict(shared)
        m["x"] = np.ascontiguousarray(x[i * S:(i + 1) * S])
        in_maps.append(m)
    return in_maps, S, SEQ


def kernel(**inputs):
    n_cores = 8
    in_maps, S, SEQ = _host_layout(inputs, n_cores)
    nc = build_nc(S, SEQ // 512)
    res = run_bass_kernel_spmd(nc, in_maps, core_ids=list(range(n_cores)))
    return np.concatenate([np.asarray(r["out"]) for r in res.results], axis=0).astype(np.float32)
```

```python
import numpy as np
import ml_dtypes
import concourse.bass as bass
import concourse.mybir as mybir
from concourse.bass_utils import run_bass_kernel_spmd
from contextlib import ExitStack

F32 = mybir.dt.float32
BF16 = mybir.dt.bfloat16
AF = mybir.ActivationFunctionType
ALU = mybir.AluOpType


class _Op:
    __slots__ = ("eng", "idx", "emit", "is_dma", "deps", "signal", "sigval", "sem", "semval", "prev_same_sem", "is_out")

    def __init__(self, eng, idx, emit, is_dma):
        self.eng = eng
        self.idx = idx
        self.emit = emit
        self.is_dma = is_dma
        self.deps = {}
        self.signal = False
        self.sigval = None
        self.sem = None
        self.semval = None
        self.prev_same_sem = None
        self.is_out = False


class Prog:
    ENGS = ("pe", "act", "dve", "pool", "sp")

    def __init__(self, nc, es, n_dma_sems=24, same_engine_raw=True):
        self.nc = nc
        self.es = es
        self.ops = {e: [] for e in self.ENGS}
        self.buf = {}
        self.same_engine_raw = same_engine_raw
        self.eng_sem = {e: es.enter_context(nc.semaphore("sem_" + e)) for e in ("pe", "act", "dve", "pool")}
        self.dma_sems = [es.enter_context(nc.semaphore("sem_dma%d" % i)) for i in range(n_dma_sems)]
        self.n_dma = 0
        self.dma_last = [None] * n_dma_sems

    def _segs(self, name, lo, hi):
        L = self.buf.get(name, [])
        out, new, cur = [], [], lo
        for s in L:
            if s[1] <= lo or s[0] >= hi:
                new.append(s)
                continue
            if s[0] < lo:
                new.append([s[0], lo, s[2], dict(s[3])])
                s = [lo, s[1], s[2], s[3]]
            tail = None
            if s[1] > hi:
                tail = [hi, s[1], s[2], dict(s[3])]
                s = [s[0], hi, s[2], s[3]]
            if cur < s[0]:
                g = [cur, s[0], None, {}]
                new.append(g)
                out.append(g)
            new.append(s)
            out.append(s)
            cur = s[1]
            if tail is not None:
                new.append(tail)
        if cur < hi:
            g = [cur, hi, None, {}]
            new.append(g)
            out.append(g)
        new.sort(key=lambda s: s[0])
        self.buf[name] = new
        return out

    def _add_dep(self, op, d, kind):
        if d is None or d is op:
            return
        if not d.is_dma and d.eng == op.eng and not op.is_dma:
            if d.eng == "pe":
                return
            if not self.same_engine_raw:
                return
        key = id(d)
        if key not in op.deps:
            op.deps[key] = d

    PSUM_NAMES = ("pj", "psc", "pv", "ptr")

    def _record(self, op, reads, writes):
        def is_psum(name):
            return (name[0] if isinstance(name, tuple) else name) in self.PSUM_NAMES
        excl = [(n, 0, 1 << 30) for (n, lo, hi) in list(reads) + list(writes) if is_psum(n)]
        reads = [r for r in reads if not is_psum(r[0])]
        writes = [w for w in writes if not is_psum(w[0])] + excl
        for (name, lo, hi) in reads:
            for s in self._segs(name, lo, hi):
                self._add_dep(op, s[2], "raw")
                rk = ("dma", id(op)) if op.is_dma else op.eng
                s[3][rk] = op
        for (name, lo, hi) in writes:
            for s in self._segs(name, lo, hi):
                self._add_dep(op, s[2], "waw")
                for r in s[3].values():
                    self._add_dep(op, r, "war")
                s[2] = op
                s[3] = {}

    def op(self, eng, emit, reads=(), writes=()):
        o = _Op(eng, len(self.ops[eng]), emit, False)
        self.ops[eng].append(o)
        self._record(o, reads, writes)
        return o

    def dma(self, out_ap, in_ap, reads=(), writes=(), out=False, **kw):
        def emit(e):
            return e.dma_start(out=out_ap, in_=in_ap, **kw)
        o = _Op("sp", len(self.ops["sp"]), emit, True)
        j = self.n_dma
        self.n_dma += 1
        n = len(self.dma_sems)
        o.sem = self.dma_sems[j % n]
        o.semval = 16 * (j // n + 1)
        o.prev_same_sem = self.dma_last[j % n]
        self.dma_last[j % n] = o
        o.is_out = out
        self.ops["sp"].append(o)
        self._record(o, reads, writes)
        return o

    def _finalize(self):
        self.waits = {}
        for e in self.ENGS:
            waited = {}
            for o in self.ops[e]:
                need = {}
                deps = list(o.deps.values())
                if o.is_dma and o.prev_same_sem is not None:
                    deps.append(o.prev_same_sem)
                for d in deps:
                    if d.is_dma:
                        k = ("dma", id(d.sem))
                        v = d.semval
                    else:
                        k = ("eng", d.eng)
                        v = d.idx
                    if waited.get(k, -1) >= v:
                        continue
                    if k not in need or need[k][0] < v:
                        need[k] = (v, d)
                for k, (v, d) in need.items():
                    waited[k] = v
                    if not d.is_dma:
                        d.signal = True
                self.waits[id(o)] = [d for (_, d) in need.values()]
        for e in ("pe", "act", "dve", "pool"):
            c = 0
            for o in self.ops[e]:
                if o.signal:
                    c += 1
                    o.sigval = c

    def emit(self):
        self._finalize()
        nc = self.nc
        handles = {"pe": nc.tensor, "act": nc.scalar, "dve": nc.vector, "pool": nc.gpsimd, "sp": nc.sync}

        def run(e, h):
            for o in self.ops[e]:
                for d in self.waits[id(o)]:
                    if d.is_dma:
                        h.wait_ge(d.sem, d.semval)
                    else:
                        h.wait_ge(self.eng_sem[d.eng], d.sigval)
                inst = o.emit(h)
                if o.is_dma:
                    inst.then_inc(o.sem, 16)
                elif o.signal:
                    inst.then_inc(self.eng_sem[e], 1)
            if e == "sp":
                for o in self.dma_last:
                    if o is not None:
                        h.wait_ge(o.sem, o.semval)

        with nc.Block() as block:
            @block.tensor
            def _(h):
                run("pe", h)

            @block.scalar
            def _(h):
                run("act", h)

            @block.vector
            def _(h):
                run("dve", h)

            @block.gpsimd
            def _(h):
                run("pool", h)

            @block.sync
            def _(h):
                run("sp", h)


D = 1024
DIN = 3072
NMETA = 16
EPS = 1e-6
SUBLN_EPS = 1e-5
LAM_INIT = 0.8 - 0.6
ROPE_THETA = 500000.0
AX = mybir.AxisListType.X

C_NW, C_CW, C_CB, C_BR, C_BI, C_L, C_LAM, C_SUB, C_FNW, NSM = 0, 8, 24, 28, 32, 36, 40, 296, 808, 1832


def R(name, lo=0, hi=1):
    return (name, lo, hi)


def build_nc(S, NCH):
    SEQ = 512 * NCH
    T = NMETA + SEQ
    NT = SEQ // 128
    nc = bass.Bass("TRN2", target_bir_lowering=False)

    def dram(name, shape, dtype=F32, kind="ExternalInput"):
        return nc.dram_tensor(name, shape, dtype, kind=kind).ap()

    x_d = dram("x", [S, SEQ, D])
    meta_d = dram("meta", [NMETA, D])
    win_d = dram("w_in", [D, DIN])
    wout_d = dram("w_out", [D, D])
    small_d = dram("small", [128, NSM])
    wgate_d = dram("wgate", [128, 1024])
    c16_d = dram("c16", [128, 256], BF16)
    cs_d = dram("cs", [128, 2, T])
    out_d = dram("out", [S, SEQ, D], kind="ExternalOutput")

    with ExitStack() as es:
        P = Prog(nc, es)

        def sb(name, shape, dtype):
            return es.enter_context(nc.sbuf_tensor("s_" + name, shape, dtype))

        def ps(name, shape, dtype):
            return es.enter_context(nc.psum_tensor("p_" + name, shape, dtype))

        win_bf = sb("win_bf", [128, 8, DIN], BF16)
        wout_bf = sb("wout_bf", [128, 8, D], BF16)
        NST = 3
        stage = [sb("stage%d" % i, [128, 1024], F32) for i in range(NST)]
        xn = [sb("xn%d" % i, [128, 1024], BF16) for i in range(2)]
        sm = sb("sm", [128, NSM], F32)
        c16 = sb("c16", [128, 256], BF16)
        wg_bf = sb("wg_bf", [128, 8, 128], BF16)
        kT = sb("kT", [128, 4, T], BF16)
        Vaug = sb("Vaug", [128, (NT + 1) * 4, 130], BF16)
        uT = sb("uT", [128, 8, 512], BF16)
        muT = sb("muT", [128, 8, NMETA], BF16)
        qT = sb("qT", [128, 4, 512], BF16)
        xl = sb("xl", [128, 4, 515], F32)
        xlm = sb("xlm", [128, 4, 3 + NMETA], F32)
        gl = sb("gl", [128, 4, 512], BF16)
        gw = sb("gw", [128, 4, 512], BF16)
        mixT = sb("mixT", [128, 8, 512], BF16)
        att = sb("att", [128, 4, 512], BF16)
        cs = sb("cs", [128, 2, 512], F32)
        csm = sb("csm", [128, 2, NMETA], F32)
        rt = [sb("rt%d" % i, [128, 512], F32) for i in range(2)]
        xc2 = [sb("xc%d" % i, [128, 512], F32) for i in range(2)]
        xcb2 = [sb("xcb%d" % i, [128, 512], BF16) for i in range(2)]
        l_tr = sb("l_tr", [128, 512], F32)
        l_ti = sb("l_ti", [128, 512], F32)
        l_a = sb("l_a", [128, 512], F32)
        l_m = sb("l_m", [128, 512], F32)
        tg = [l_a, l_m]
        TGN = ["l_a", "l_m"]
        NET = 4
        ET = [sb("ET%d" % i, [128, 512], BF16) for i in range(NET)]
        oj = sb("oj", [128, 128], BF16)
        xres = [sb("xres%d" % i, [128, 1024], F32) for i in range(3)]
        sc = sb("sc", [128, 128], F32)
        lt = sb("lt", [128, 128], F32)
        xlm_tail = sb("xlm_tail", [128, 4, 3], F32)
        sqj = sb("sqj", [128, 1024], BF16)
        rsx = [sb("rsx%d" % i, [128, 4], F32) for i in range(2)]
        rsy = [sb("rsy%d" % i, [128, 4], F32) for i in range(2)]
        rsa = [sb("rsa%d" % i, [128, 4], F32) for i in range(2)]

        ident = c16[:, 0:128]
        permT = c16[:, 128:256]
        MHALF, LAM, NLAM, LS, LE, SPE, SP, C8, C4, HB, CARRY, CARRYM = 0, 1, 2, 3, 5, 7, 11, 15, 19, 23, 31, 35
        SSA, RSA, SSE, RSE, SSD, RSD, RL = 80, 84, 44, 46, 48, 56, 64

        def col(i, n=1, rows=128):
            return sc[0:rows, i:i + n]

        pj = [ps("pj%d" % i, [128, 512], F32) for i in range(2)]
        ptr = ps("ptr", [128, 8, 128], BF16)
        psc = [ps("psc%d" % i, [128, 512], F32) for i in range(2)]
        pv = [ps("pv%d" % i, [128, 512], F32) for i in range(3)]

        PJB = [pj[0], pj[1], psc[0], psc[1], pv[0], pv[1], pv[2]]
        PJN = [("pj", 0), ("pj", 1), ("psc", 0), ("psc", 1), ("pv", 0), ("pv", 1), ("pv", 2)]
        NPJ = 7
        cnt = {"st": 0, "pj": 0, "xn": 0, "rt": 0, "tg": 0, "psc": 0, "ET": 0, "ot": 0, "xres": 0, "sl": 0, "sla": 0, "pjc": 0}

        def nxt(key, n):
            v = cnt[key] % n
            cnt[key] += 1
            return v

        P.dma(sm[:], small_d, writes=[R("sm")])
        P.dma(c16[:], c16_d, writes=[R("c16")])
        P.dma(csm[:], cs_d[:, :, 0:NMETA], writes=[R("csm")])
        P.op("pool", lambda e: e.memset(Vaug[:, :, 128:129], 1.0), writes=[R("V", 0, NT + 1)])
        P.op("pool", lambda e: e.memset(xlm[:, :, 0:3], 0.0), writes=[R(("xlm", ct), 0, 3) for ct in range(4)])
        P.op("pool", lambda e: e.memset(col(CARRY, 4), 0.0), writes=[R("carry")])
        P.op("pool", lambda e: e.memset(sc[0:64, 39:40], 0.0), writes=[R("maskb", 0, 1)])
        P.op("pool", lambda e: e.memset(sc[64:128, 39:40], -30000.0), writes=[R("maskb", 1, 2)])
        for i in range(2):
            P.op("dve", lambda e, i=i: e.tensor_tensor(out=lt[:, 64 * i:64 * i + 64], in0=sm[:, C_LAM + 128 * i:C_LAM + 128 * i + 64],
                                                       in1=sm[:, C_LAM + 128 * i + 64:C_LAM + 128 * i + 128], op=ALU.mult),
                 reads=[R("sm")], writes=[R("lt", i, i + 1)])
            P.op("dve", lambda e, i=i: e.reduce_sum(out=col(LS + i), in_=lt[:, 64 * i:64 * i + 64], axis=AX),
                 reads=[R("lt", i, i + 1)], writes=[R("ls", i, i + 1)])
        P.op("act", lambda e: e.activation(out=col(LE, 2), in_=col(LS, 2), func=AF.Exp), reads=[R("ls", 0, 2)], writes=[R("le")])
        P.op("pool", lambda e: e.tensor_tensor(out=col(LAM), in0=col(LE), in1=col(LE + 1), op=ALU.subtract),
             reads=[R("le")], writes=[R("lam")])
        P.op("pool", lambda e: e.tensor_scalar(out=col(NLAM), in0=col(LAM), scalar1=-1.0, scalar2=-LAM_INIT, op0=ALU.mult, op1=ALU.add),
             reads=[R("lam")], writes=[R("nlam")])
        P.op("act", lambda e: e.activation(out=col(SPE, 4), in_=sm[:, C_L:C_L + 4], func=AF.Exp, scale=-1.0),
             reads=[R("sm")], writes=[R("spe")])
        P.op("act", lambda e: e.activation(out=col(SP, 4), in_=col(SPE, 4), func=AF.Ln, bias=1.0, scale=1.0),
             reads=[R("spe")], writes=[R("sp")])
        P.op("pool", lambda e: e.tensor_scalar(out=col(C8, 4), in0=col(SP, 4), scalar1=-8.0, scalar2=None, op0=ALU.mult),
             reads=[R("sp")], writes=[R("c8")])
        P.op("pool", lambda e: e.tensor_scalar(out=col(C4, 4), in0=col(SP, 4), scalar1=-4.0, scalar2=None, op0=ALU.mult),
             reads=[R("sp")], writes=[R("c4")])
        P.op("pool", lambda e: e.tensor_scalar(out=col(HB, 8), in0=sm[:, C_BR:C_BR + 8], scalar1=0.5, scalar2=None, op0=ALU.mult),
             reads=[R("sm")], writes=[R("hb")])
        P.op("pool", lambda e: e.tensor_scalar(out=sm[:, C_SUB:C_SUB + 512], in0=sm[:, C_SUB:C_SUB + 512], scalar1=0.5 * (1.0 - LAM_INIT),
                                               scalar2=None, op0=ALU.mult),
             reads=[R("sm")], writes=[R("subg")])
        sublng = sm[:, C_SUB:C_SUB + 512]
        fnw = sm[:, C_FNW:C_FNW + 1024]
        b = nxt("st", NST)
        P.dma(stage[b][:], wgate_d, writes=[R(("st", b))])
        P.op("pool", lambda e, b=b: e.tensor_copy(out=wg_bf[:].rearrange("p a b -> p (a b)"), in_=stage[b][:]),
             reads=[R(("st", b))], writes=[R("wg")])

        I32 = mybir.dt.int32
        RS_MAGIC = 1597463007.0
        rs_cnt = [0]

        def rsqrt_dve(dst, src, scale, eps, src_regs, dst_regs, rows=128, n=1):
            u = rs_cnt[0] % 2
            rs_cnt[0] += 1
            X = rsx[u][0:rows, 0:n]
            Y = rsy[u][0:rows, 0:n]
            A = rsa[u][0:rows, 0:n]
            rx, ry, ra = R(("rsx", u)), R(("rsy", u)), R(("rsa", u))
            P.op("dve", lambda e: e.tensor_scalar(out=X, in0=src, scalar1=scale, scalar2=eps, op0=ALU.mult, op1=ALU.add),
                 reads=src_regs, writes=[rx])
            P.op("dve", lambda e: e.tensor_scalar(out=Y.bitcast(I32), in0=X.bitcast(I32), scalar1=-0.5, scalar2=RS_MAGIC, op0=ALU.mult, op1=ALU.add),
                 reads=[rx], writes=[ry])
            for it in range(3):
                last = it == 2
                if n == 1:
                    P.op("dve", lambda e: e.scalar_tensor_tensor(out=A, in0=Y, scalar=Y, in1=X, op0=ALU.mult, op1=ALU.mult),
                         reads=[ry, rx], writes=[ra])
                else:
                    P.op("dve", lambda e: e.tensor_tensor(out=A, in0=Y, in1=Y, op=ALU.mult), reads=[ry], writes=[ra])
                    P.op("dve", lambda e: e.tensor_tensor(out=A, in0=A, in1=X, op=ALU.mult), reads=[ra, rx], writes=[ra])
                P.op("dve", lambda e: e.tensor_scalar(out=A, in0=A, scalar1=-0.5, scalar2=1.5, op0=ALU.mult, op1=ALU.add), reads=[ra], writes=[ra])
                if last:
                    P.op("dve", lambda e: e.tensor_tensor(out=dst, in0=Y, in1=A, op=ALU.mult), reads=[ry, ra], writes=dst_regs)
                else:
                    P.op("dve", lambda e: e.tensor_tensor(out=Y, in0=Y, in1=A, op=ALU.mult), reads=[ry, ra], writes=[ry])

        def stage_A1a(src_ap, rows, defer_rs=False):
            b = nxt("st", NST)
            sl = nxt("sla", 4)
            P.dma(stage[b][0:rows, :], src_ap, writes=[R(("st", b))])
            P.op("dve", lambda e: e.scalar_tensor_tensor(out=sqj[0:rows, :], in0=stage[b][0:rows, :], scalar=1.0, in1=stage[b][0:rows, :],
                                                         op0=ALU.mult, op1=ALU.mult, accum_out=col(SSA + sl, 1, rows)),
                 reads=[R(("st", b))], writes=[R("sqj"), R(("ssa", sl))])
            if not defer_rs:
                rsqrt_dve(col(RSA + sl, 1, rows), col(SSA + sl, 1, rows), 1.0 / D, EPS, [R(("ssa", sl))], [R(("rsa", sl))], rows=rows)
            return (b, sl, rows)

        def rsqrt_act(dst, src, scale, eps, src_regs, dst_regs, n=1):
            u = rs_cnt[0] % 2
            rs_cnt[0] += 1
            X = rsx[u][:, 0:n]
            P.op("act", lambda e: e.activation(out=X, in_=src, func=AF.Sqrt, bias=eps, scale=scale), reads=src_regs, writes=[R(("rsx", u))])
            P.op("dve", lambda e: e.reciprocal(out=dst, in_=X), reads=[R(("rsx", u))], writes=dst_regs)

        def stage_A1b(ctx):
            b, sl, rows = ctx
            xb = nxt("xn", 2)
            P.op("dve", lambda e: e.tensor_scalar(out=xn[xb][0:rows, :], in0=stage[b][0:rows, :], scalar1=col(RSA + sl, 1, rows),
                                                  scalar2=None, op0=ALU.mult),
                 reads=[R(("st", b)), R(("rsa", sl))], writes=[R(("xn", xb))])
            return xb

        def stage_A1(src_ap, rows):
            return stage_A1b(stage_A1a(src_ap, rows))

        def stage_A2(xb, rows, dst, dst_name, col0):
            for k in range(8):
                P.op("pe", lambda e, k=k: e.transpose(ptr[:, k, 0:rows], xn[xb][0:rows, 128 * k:128 * k + 128], ident[0:rows, 0:rows]),
                     reads=[R(("xn", xb)), R("c16")], writes=[R("ptr", k, k + 1)])
            P.op("dve", lambda e: e.tensor_copy(out=dst[:, :, col0:col0 + rows], in_=ptr[:, :, 0:rows]),
                 reads=[R("ptr", 0, 8)], writes=[R(dst_name, col0, col0 + rows)])

        def stage_A(src_ap, rows, dst, dst_name, col0):
            b = nxt("st", NST)
            sl = nxt("sla", 4)
            xb = nxt("xn", 2)
            P.dma(stage[b][0:rows, :], src_ap, writes=[R(("st", b))])
            P.op("act", lambda e: e.activation(out=sqj[0:rows, :], in_=stage[b][0:rows, :], func=AF.Square, accum_out=col(SSA + sl, 1, rows)),
                 reads=[R(("st", b))], writes=[R("sqj"), R(("ssa", sl))])
            u = rs_cnt[0] % 2
            rs_cnt[0] += 1
            X = rsx[u][0:rows, 0:1]
            P.op("act", lambda e: e.activation(out=X, in_=col(SSA + sl, 1, rows), func=AF.Sqrt, bias=EPS, scale=1.0 / D),
                 reads=[R(("ssa", sl))], writes=[R(("rsx", u))])
            P.op("dve", lambda e: e.reciprocal(out=col(RSA + sl, 1, rows), in_=X), reads=[R(("rsx", u))], writes=[R(("rsa", sl))])
            P.op("act", lambda e: e.activation(out=xn[xb][0:rows, :], in_=stage[b][0:rows, :], func=AF.Copy, scale=col(RSA + sl, 1, rows)),
                 reads=[R(("st", b)), R(("rsa", sl))], writes=[R(("xn", xb))])
            stage_A2(xb, rows, dst, dst_name, col0)

        def weight_pieces():
            L = []
            cntw = [0]

            def win_piece(cg, k):
                def f():
                    b = cntw[0] % 3
                    cntw[0] += 1
                    P.dma(xres[b][:], win_d[128 * k:128 * k + 128, 1024 * cg:1024 * cg + 1024], writes=[R(("xres", b))])
                    if cntw[0] % 2 == 0:
                        P.op("dve", lambda e: e.tensor_scalar(out=win_bf[:, k, 1024 * cg:1024 * cg + 1024], in0=xres[b][:],
                                                              scalar1=sm[:, C_NW + k:C_NW + k + 1], scalar2=None, op0=ALU.mult),
                             reads=[R(("xres", b)), R("sm")], writes=[R(("win", k), 1024 * cg, 1024 * cg + 1024)])
                    else:
                        P.op("act", lambda e: e.activation(out=win_bf[:, k, 1024 * cg:1024 * cg + 1024], in_=xres[b][:],
                                                           func=AF.Copy, scale=sm[:, C_NW + k:C_NW + k + 1]),
                             reads=[R(("xres", b)), R("sm")], writes=[R(("win", k), 1024 * cg, 1024 * cg + 1024)])
                return f

            def wout_piece(k):
                def f():
                    b = cntw[0] % 3
                    cntw[0] += 1
                    P.dma(xres[b][:], wout_d[128 * k:128 * k + 128, :], writes=[R(("xres", b))])
                    if k % 2 == 0:
                        P.op("dve", lambda e: e.tensor_scalar(out=wout_bf[:, k, :], in0=xres[b][:], scalar1=(1.0 if k < 4 else 0.5),
                                                              scalar2=None, op0=ALU.mult),
                             reads=[R(("xres", b))], writes=[R(("wout", k), 0, 1024)])
                    else:
                        P.op("act", lambda e: e.activation(out=wout_bf[:, k, :], in_=xres[b][:], func=AF.Copy, scale=(1.0 if k < 4 else 0.5)),
                             reads=[R(("xres", b))], writes=[R(("wout", k), 0, 1024)])
                return f

            for cg in range(3):
                for k in range(8):
                    L.append(win_piece(cg, k))
            for k in range(8):
                L.append(wout_piece(k))
            return L

        def proj_group(lhs_fn, rhs_fn, outrows, N, reads_fn):
            b = nxt("pj", NPJ)
            for k in range(8):
                P.op("pe", lambda e, k=k: e.matmul(PJB[b][0:outrows, 0:N], lhsT=lhs_fn(k), rhs=rhs_fn(k), start=(k == 0), stop=(k == 7)),
                     reads=reads_fn(k), writes=[R(PJN[b])])
            return b

        def stage_B(u, u_name, N, p0, meta, tiles, after_xl0=None):
            cos = csm[:, 0, :] if meta else cs[:, 0, :]
            sin = csm[:, 1, :] if meta else cs[:, 1, :]
            cs_name = "csm" if meta else "cs"

            def fm(c0):
                return proj_group(lambda k: win_bf[:, k, c0:c0 + 128], lambda k: u[:, k, 0:N], 128, N,
                                  lambda k: [R(("win", k), c0, c0 + 128), R(u_name, 0, N)])

            ropes = []
            for h in range(4):
                b = fm(512 + 128 * h)
                dest = kT[:, h, p0:p0 + N]
                dreg = R(("kT", h), p0, p0 + N)
                P.op("act", lambda e, b=b, dest=dest: e.activation(out=dest, in_=PJB[b][:, 0:N], func=AF.Copy),
                     reads=[R(PJN[b])], writes=[dreg])
                ropes.append((dest, dreg))
            if not meta:
                for h in range(4):
                    b = fm(128 * h)
                    dest = qT[:, h, 0:N]
                    dreg = R(("qT", h), 0, N)
                    P.op("act", lambda e, b=b, dest=dest: e.activation(out=dest, in_=PJB[b][:, 0:N], func=AF.Copy),
                         reads=[R(PJN[b])], writes=[dreg])
                    ropes.append((dest, dreg))

            def do_rope(dest, dreg):
                b2 = nxt("pj", NPJ)
                r = nxt("rt", 1)
                t1, t2 = rt[2 * r], rt[2 * r + 1]
                P.op("pe", lambda e: e.matmul(PJB[b2][:, 0:N], lhsT=permT, rhs=dest, start=True, stop=True),
                     reads=[dreg, R("c16")], writes=[R(PJN[b2])])
                P.op("pool", lambda e: e.tensor_tensor(out=t1[:, 0:N], in0=dest, in1=cos, op=ALU.mult),
                     reads=[dreg, R(cs_name)], writes=[R(("rt", 2 * r), 0, 512)])
                P.op("dve", lambda e: e.tensor_tensor(out=t2[:, 0:N], in0=PJB[b2][:, 0:N], in1=sin, op=ALU.mult),
                     reads=[R(PJN[b2]), R(cs_name)], writes=[R(("rt", 2 * r + 1), 0, 512)])
                P.op("pool", lambda e: e.tensor_tensor(out=dest, in0=t1[:, 0:N], in1=t2[:, 0:N], op=ALU.add),
                     reads=[R(("rt", 2 * r), 0, 512), R(("rt", 2 * r + 1), 0, 512)], writes=[dreg])

            for (rows, vt, c0l) in tiles:
                b = proj_group(lambda k, c0l=c0l, rows=rows: u[:, k, c0l:c0l + rows], lambda k: win_bf[:, k, 1024:1536], rows, 512,
                               lambda k: [R(("win", k), 1024, 1536), R(u_name, c0l, c0l + rows)])
                P.op("dve", lambda e, b=b, rows=rows, vt=vt: e.tensor_copy(out=Vaug[0:rows, 4 * vt:4 * vt + 4, 0:128],
                                                                          in_=PJB[b][0:rows, :].rearrange("p (h d) -> p h d", h=4)),
                     reads=[R(PJN[b])], writes=[R("V", vt, vt + 1)])
            for (dest, dreg) in ropes[0:4]:
                do_rope(dest, dreg)
            if not meta:
                for jl, (rows, vt, c0l) in enumerate(tiles):
                    b = proj_group(lambda k, c0l=c0l, rows=rows: u[:, k, c0l:c0l + rows], lambda k: win_bf[:, k, 1536:2048], rows, 512,
                                   lambda k: [R(("win", k), 1536, 2048), R(u_name, c0l, c0l + rows)])
                    g = nxt("tg", 2)
                    P.op("act", lambda e, b=b, g=g: e.activation(out=tg[g][:], in_=PJB[b][:, :], func=AF.Tanh, scale=0.5),
                         reads=[R(PJN[b])], writes=[R(TGN[g])])
                    P.op("dve", lambda e, b=b, g=g: e.scalar_tensor_tensor(out=tg[g][:], in0=tg[g][:], scalar=1.0, in1=PJB[b][:, :],
                                                                           op0=ALU.add, op1=ALU.mult),
                         reads=[R(PJN[b]), R(TGN[g])], writes=[R(TGN[g])])
                    P.op("pool", lambda e, g=g, jl=jl: e.tensor_tensor(out=gw[:, jl, :], in0=tg[g][:], in1=sublng, op=ALU.mult),
                         reads=[R(TGN[g]), R("subg")], writes=[R(("gw", jl), 0, 512)])
                for (dest, dreg) in ropes[4:8]:
                    do_rope(dest, dreg)
            for ct in range(4):
                if ct == 1 and after_xl0 is not None:
                    after_xl0()
                b = fm(2048 + 128 * ct)
                P.op("act", lambda e, b=b, ct=ct: e.activation(out=(xlm if meta else xl)[:, ct, 3:3 + N], in_=PJB[b][:, 0:N], func=AF.Copy),
                     reads=[R(PJN[b])], writes=[R(("xlm" if meta else "xl", ct), 3, 3 + N)])
            if not meta:
                for ct in range(4):
                    b = fm(2560 + 128 * ct)
                    g = nxt("tg", 2)
                    P.op("act", lambda e, b=b, g=g: e.activation(out=tg[g][:, 0:N], in_=PJB[b][:, 0:N], func=AF.Tanh, scale=0.5),
                         reads=[R(PJN[b])], writes=[R(TGN[g])])
                    P.op("dve", lambda e, b=b, g=g, ct=ct: e.scalar_tensor_tensor(out=gl[:, ct, 0:N], in0=tg[g][:, 0:N], scalar=1.0,
                                                                                 in1=PJB[b][:, 0:N], op0=ALU.add, op1=ALU.mult),
                         reads=[R(PJN[b]), R(TGN[g])], writes=[R(("gl", ct), 0, N)])

        def stage_C_conv(ct, N, meta, xl=xl):
            xc, xcb = xc2[ct % 2], xcb2[ct % 2]
            xcn, xcbn = ("xc", ct % 2), ("xcb", ct % 2)
            xln = "xlm" if meta else "xl"
            if meta:
                xl = xlm
            xlr = R((xln, ct), 0, 3 + N)
            P.op("pool", lambda e: e.tensor_scalar(out=xc[:, 0:N], in0=xl[:, ct, 0:N], scalar1=sm[:, C_CW + 4 * ct:C_CW + 4 * ct + 1],
                                                   scalar2=sm[:, C_CB + ct:C_CB + ct + 1], op0=ALU.mult, op1=ALU.add),
                 reads=[xlr, R("sm")], writes=[R(xcn)])
            for j in range(1, 4):
                P.op("dve", lambda e, j=j: e.scalar_tensor_tensor(out=xc[:, 0:N], in0=xl[:, ct, j:j + N],
                                                                  scalar=sm[:, C_CW + 4 * ct + j:C_CW + 4 * ct + j + 1],
                                                                  in1=xc[:, 0:N], op0=ALU.mult, op1=ALU.add),
                     reads=[xlr, R("sm"), R(xcn)], writes=[R(xcn)])
            if meta:
                P.op("pool", lambda e: e.tensor_copy(out=xlm_tail[:, ct, :], in_=xl[:, ct, N:N + 3]),
                     reads=[R((xln, ct), N, N + 3)], writes=[R("xlm_tail", ct, ct + 1)])
            else:
                P.op("pool", lambda e: e.tensor_copy(out=xl[:, ct, 0:3], in_=xl[:, ct, N:N + 3]),
                     reads=[R(("xl", ct), N, N + 3)], writes=[R(("xl", ct), 0, 3)])
            P.op("pool", lambda e: e.tensor_copy(out=xcb[:, 0:N], in_=xc[:, 0:N]), reads=[R(xcn)], writes=[R(xcbn)])

        def stage_C_rest(ct, N, meta, in_hook=False):
            xc, xcb = xc2[ct % 2], xcb2[ct % 2]
            xcn, xcbn = ("xc", ct % 2), ("xcb", ct % 2)
            for gi, dst, dname in ((0, l_tr, "l_tr"), (1, l_ti, "l_ti")):
                b = nxt("pjc", 2) if in_hook else nxt("pj", NPJ)
                P.op("pe", lambda e, b=b, gi=gi: e.matmul(PJB[b][:, 0:N], lhsT=wg_bf[:, 4 * gi + ct, :], rhs=xcb[:, 0:N],
                                                         start=True, stop=True),
                     reads=[R("wg"), R(xcbn)], writes=[R(PJN[b])])
                P.op("act", lambda e, b=b, gi=gi, dst=dst: e.activation(out=dst[:, 0:N], in_=PJB[b][:, 0:N], func=AF.Tanh,
                                                                       bias=col(HB + 4 * gi + ct), scale=0.5),
                     reads=[R(PJN[b]), R("hb")], writes=[R(dname)])
            P.op("act", lambda e: e.activation(out=l_a[:, 0:N], in_=l_tr[:, 0:N], func=AF.Exp, bias=col(C4 + ct), scale=col(C4 + ct)),
                 reads=[R("l_tr"), R("c4")], writes=[R("l_a")])
            P.op("pool", lambda e: e.tensor_tensor(out=l_m[:, 0:N], in0=l_a[:, 0:N], in1=l_a[:, 0:N], op=ALU.mult),
                 reads=[R("l_a")], writes=[R("l_m")])
            P.op("act", lambda e: e.activation(out=l_m[:, 0:N], in_=l_m[:, 0:N], func=AF.Sqrt, bias=0.25, scale=-0.25),
                 reads=[R("l_m")], writes=[R("l_m")])
            if meta:
                P.op("pool", lambda e: e.memset(l_m[:, 0:1], 0.5), reads=[], writes=[R("l_m")])
            P.op("dve", lambda e: e.scalar_tensor_tensor(out=l_ti[:, 0:N], in0=l_ti[:, 0:N], scalar=1.0, in1=xc[:, 0:N],
                                                         op0=ALU.add, op1=ALU.mult),
                 reads=[R("l_ti"), R(xcn)], writes=[R("l_ti")])
            P.op("pool", lambda e: e.tensor_tensor(out=l_ti[:, 0:N], in0=l_ti[:, 0:N], in1=l_m[:, 0:N], op=ALU.mult),
                 reads=[R("l_ti"), R("l_m")], writes=[R("l_ti")])
            P.op("dve", lambda e: e.tensor_tensor_scan(out=l_tr[:, 0:N], data0=l_a[:, 0:N], data1=l_ti[:, 0:N],
                                                       initial=col(CARRY + ct), op0=ALU.mult, op1=ALU.add),
                 reads=[R("l_a"), R("l_ti"), R("carry", ct, ct + 1)], writes=[R("l_tr")])
            P.op("pool", lambda e: e.tensor_copy(out=col(CARRY + ct), in_=l_tr[:, N - 1:N]),
                 reads=[R("l_tr")], writes=[R("carry", ct, ct + 1)])
            if meta:
                P.op("pool", lambda e: e.tensor_copy(out=col(CARRYM + ct), in_=l_tr[:, N - 1:N]),
                     reads=[R("l_tr")], writes=[R("carrym", ct, ct + 1)])
            else:
                P.op("pool", lambda e: e.tensor_tensor(out=mixT[:, 4 + ct, 0:N], in0=l_tr[:, 0:N], in1=gl[:, ct, 0:N], op=ALU.mult),
                     reads=[R("l_tr"), R(("gl", ct), 0, N)], writes=[R(("mixT", 4 + ct), 0, N)])

        def stage_C(N, meta):
            for ct in range(4):
                stage_C_conv(ct, N, meta)
                stage_C_rest(ct, N, meta)

        def c_pieces(ct, N):
            xc, xcb = xc2[ct % 2], xcb2[ct % 2]
            xcn, xcbn = ("xc", ct % 2), ("xcb", ct % 2)

            def gate(gi, dst, dname):
                def f():
                    b = nxt("pjc", 2)
                    P.op("pe", lambda e: e.matmul(PJB[b][:, 0:N], lhsT=wg_bf[:, 4 * gi + ct, :], rhs=xcb[:, 0:N], start=True, stop=True),
                         reads=[R("wg"), R(xcbn)], writes=[R(PJN[b])])
                    P.op("act", lambda e: e.activation(out=dst[:, 0:N], in_=PJB[b][:, 0:N], func=AF.Tanh, bias=col(HB + 4 * gi + ct), scale=0.5),
                         reads=[R(PJN[b]), R("hb")], writes=[R(dname)])
                return f

            def p_exp():
                P.op("act", lambda e: e.activation(out=l_a[:, 0:N], in_=l_tr[:, 0:N], func=AF.Exp, bias=col(C4 + ct), scale=col(C4 + ct)),
                     reads=[R("l_tr"), R("c4")], writes=[R("l_a")])
                P.op("pool", lambda e: e.tensor_tensor(out=l_m[:, 0:N], in0=l_a[:, 0:N], in1=l_a[:, 0:N], op=ALU.mult),
                     reads=[R("l_a")], writes=[R("l_m")])
                P.op("dve", lambda e: e.scalar_tensor_tensor(out=l_ti[:, 0:N], in0=l_ti[:, 0:N], scalar=1.0, in1=xc[:, 0:N],
                                                             op0=ALU.add, op1=ALU.mult),
                     reads=[R("l_ti"), R(xcn)], writes=[R("l_ti")])

            def p_sqrt():
                P.op("act", lambda e: e.activation(out=l_m[:, 0:N], in_=l_m[:, 0:N], func=AF.Sqrt, bias=0.25, scale=-0.25),
                     reads=[R("l_m")], writes=[R("l_m")])

            def p_tail():
                P.op("pool", lambda e: e.tensor_tensor(out=l_ti[:, 0:N], in0=l_ti[:, 0:N], in1=l_m[:, 0:N], op=ALU.mult),
                     reads=[R("l_ti"), R("l_m")], writes=[R("l_ti")])
                P.op("dve", lambda e: e.tensor_tensor_scan(out=l_tr[:, 0:N], data0=l_a[:, 0:N], data1=l_ti[:, 0:N],
                                                           initial=col(CARRY + ct), op0=ALU.mult, op1=ALU.add),
                     reads=[R("l_a"), R("l_ti"), R("carry", ct, ct + 1)], writes=[R("l_tr")])
                P.op("pool", lambda e: e.tensor_copy(out=col(CARRY + ct), in_=l_tr[:, N - 1:N]),
                     reads=[R("l_tr")], writes=[R("carry", ct, ct + 1)])
                P.op("pool", lambda e: e.tensor_tensor(out=mixT[:, 4 + ct, 0:N], in0=l_tr[:, 0:N], in1=gl[:, ct, 0:N], op=ALU.mult),
                     reads=[R("l_tr"), R(("gl", ct), 0, N)], writes=[R(("mixT", 4 + ct), 0, N)])

            return [gate(0, l_tr, "l_tr"), gate(1, l_ti, "l_ti"), p_exp, p_sqrt, p_tail]

        def att_T(h):
            for jl in range(4):
                P.op("pe", lambda e, jl=jl: e.transpose(ptr[:, jl, :], att[:, jl, 128 * h:128 * h + 128], ident),
                     reads=[R(("att", jl), 128 * h, 128 * h + 128), R("c16")], writes=[R("ptr", jl, jl + 1)])
            P.op("dve", lambda e: e.tensor_copy(out=mixT[:, h, :], in_=ptr[:, 0:4, :].rearrange("p j n -> p (j n)")),
                 reads=[R("ptr", 0, 4)], writes=[R(("mixT", h), 0, 512)])

        def stage_D(c, pieces_for=None):
            ktiles = [None] + list(range(4 * c + 4))
            def head(h, pieces):
                items = [(kt, m) for kt in ktiles for m in range(2)]
                started = set()
                info = {}

                def emit_scores(n):
                    kt, m = items[n]
                    b = nxt("psc", 2)
                    if kt is None:
                        K, kp0, qlo = NMETA, 0, 0
                    else:
                        K, kp0, qlo = 128, NMETA + 128 * kt, max(0, kt - 4 * c)
                    Nq = 512 - 128 * qlo
                    P.op("pe", lambda e: e.matmul(psc[b][0:K, 0:Nq], lhsT=kT[64 * m:64 * m + 64, h, kp0:kp0 + K],
                                                  rhs=qT[64 * m:64 * m + 64, h, 128 * qlo:512], start=True, stop=True),
                         reads=[R(("kT", h), kp0, kp0 + K), R(("qT", h), 128 * qlo, 512)], writes=[R(("psc", b))])
                    et = nxt("ET", NET)
                    if kt is not None and kt >= 4 * c:
                        P.op("act", lambda e: e.activation(out=ET[et][:, 0:64], in_=psc[b][:, 0:64], func=AF.Exp, scale=0.125, bias=sc[:, 39:40]),
                             reads=[R(("psc", b)), R("maskb", 0, 2)], writes=[R(("ET", et), 0, 64)])
                        P.op("act", lambda e: e.activation(out=ET[et][:, 64:Nq], in_=psc[b][:, 64:Nq], func=AF.Exp, scale=0.125),
                             reads=[R(("psc", b))], writes=[R(("ET", et), 64, 512)])
                    else:
                        P.op("act", lambda e: e.activation(out=ET[et][0:K, 0:Nq], in_=psc[b][0:K, 0:Nq], func=AF.Exp, scale=0.125),
                             reads=[R(("psc", b))], writes=[R(("ET", et), 0, 512)])
                    info[n] = (et, K, qlo)

                def emit_pv(n):
                    kt, m = items[n]
                    et, K, qlo = info[n]
                    vt = 0 if kt is None else kt + 1
                    for jl in range(qlo, 4):
                        a = 2 * jl + m
                        bank, off = a // 3, 130 * (a % 3)
                        first = bank not in started
                        started.add(bank)
                        last = (kt is not None) and (kt == 4 * c + jl)
                        P.op("pe", lambda e, jl=jl, bank=bank, off=off, first=first, last=last: e.matmul(
                            pv[bank][:, off:off + 129], lhsT=ET[et][0:K, 128 * (jl - qlo):128 * (jl - qlo) + 128],
                            rhs=Vaug[0:K, 4 * vt + h, 0:129], start=first, stop=last, skip_group_check=True),
                            reads=[R(("ET", et), 0, 512), R("V", vt, vt + 1)], writes=[R(("pv", bank), off, off + 129)])

                emit_scores(0)
                emit_scores(1)
                pi = 0
                pstride = max(1, (len(items) - 2) // max(1, len(pieces)))
                for n in range(len(items)):
                    emit_pv(n)
                    if n + 2 < len(items):
                        emit_scores(n + 2)
                    if n >= 1 and (n - 1) % pstride == 0 and pi < len(pieces):
                        pieces[pi]()
                        pi += 1
                sl = nxt("sl", 2)
                accs = []
                for a in range(8):
                    bank, off = a // 3, 130 * (a % 3)
                    raw = rt[a // 4][:, 128 * (a % 4):128 * (a % 4) + 128]
                    rawr = R(("rt", a // 4), 128 * (a % 4), 128 * (a % 4) + 128)
                    accs.append((None, None, raw, rawr, R(("rl", sl), a, a + 1), RL + 8 * sl + a))
                for bank, na in ((0, 3), (1, 3), (2, 2)):
                    a0 = 3 * bank
                    P.op("dve", lambda e, bank=bank, na=na, a0=a0: e.reciprocal(
                        out=col(RL + 8 * sl + a0, na), in_=pv[bank][:, 0:130 * na].rearrange("p (a c) -> p a c", c=130)[:, :, 128]),
                        reads=[R(("pv", bank), 0, 130 * na)], writes=[R(("rl", sl), a0, a0 + na)])
                def rawslots(lo, n):
                    t = rt[lo // 4]
                    return t[:, 128 * (lo % 4):128 * (lo % 4) + 128 * n].rearrange("p (a c) -> p a c", c=128)

                def accslots(bank, j0, n):
                    return pv[bank][:, 130 * j0:130 * (j0 + n)].rearrange("p (a c) -> p a c", c=130)[:, :, 0:128]

                for eng, bank, j0, n, lo in (("act", 0, 0, 3, 0), ("dve", 1, 0, 1, 3), ("dve", 1, 1, 2, 4), ("act", 2, 0, 2, 6)):
                    rd = [R(("pv", bank), 130 * j0, 130 * (j0 + n))]
                    wr = [R(("rt", lo // 4), 128 * (lo % 4), 128 * (lo % 4) + 128 * n)]
                    if eng == "act":
                        P.op("act", lambda e, bank=bank, j0=j0, n=n, lo=lo: e.activation(out=rawslots(lo, n), in_=accslots(bank, j0, n), func=AF.Copy),
                             reads=rd, writes=wr)
                    else:
                        P.op("dve", lambda e, bank=bank, j0=j0, n=n, lo=lo: e.tensor_copy(out=rawslots(lo, n), in_=accslots(bank, j0, n)),
                             reads=rd, writes=wr)
                while pi < len(pieces):
                    pieces[pi]()
                    pi += 1

                def p_norm():
                    for jl in range(4):
                        (acc0, rr0, raw0, rawr0, rl0, c0) = accs[2 * jl]
                        (acc1, rr1, raw1, rawr1, rl1, c1) = accs[2 * jl + 1]
                        P.op("dve", lambda e, c1=c1: e.tensor_tensor(out=col(c1), in0=col(c1), in1=col(NLAM), op=ALU.mult),
                             reads=[rl1, R("nlam")], writes=[rl1])
                        P.op("dve", lambda e, raw0=raw0, c0=c0: e.tensor_scalar(out=raw0, in0=raw0, scalar1=col(c0), scalar2=None, op0=ALU.mult),
                             reads=[rawr0, rl0], writes=[rawr0])
                        P.op("dve", lambda e, raw0=raw0, raw1=raw1, c1=c1: e.scalar_tensor_tensor(out=raw0, in0=raw1, scalar=col(c1), in1=raw0,
                                                                                                op0=ALU.mult, op1=ALU.add),
                             reads=[rawr1, rl1, rawr0], writes=[rawr0])

                o_ap = [accs[2 * jl][2] for jl in range(4)]
                o_rg = [accs[2 * jl][3] for jl in range(4)]

                class Post:
                    pass
                post = Post()
                rs_args = (col(RSD + 4 * sl, 4), col(SSD + 4 * sl, 4), 1.0 / 128, SUBLN_EPS, [R(("ssd", sl), 0, 4)], [R(("rsd", sl), 0, 4)])

                def p_sq():
                    p_norm()
                    for jl in range(4):
                        P.op("dve", lambda e, jl=jl: e.scalar_tensor_tensor(out=oj[:], in0=o_ap[jl], scalar=1.0, in1=o_ap[jl], op0=ALU.mult, op1=ALU.mult,
                                                                            accum_out=col(SSD + 4 * sl + jl)),
                             reads=[o_rg[jl]], writes=[R("oj"), R(("ssd", sl), jl, jl + 1)])

                def p_stt():
                    for jl in range(4):
                        P.op("dve", lambda e, jl=jl: e.scalar_tensor_tensor(out=att[:, jl, 128 * h:128 * h + 128], in0=o_ap[jl],
                                                                            scalar=col(RSD + 4 * sl + jl), in1=gw[:, jl, 128 * h:128 * h + 128],
                                                                            op0=ALU.mult, op1=ALU.mult),
                             reads=[o_rg[jl], R(("rsd", sl), jl, jl + 1), R(("gw", jl), 128 * h, 128 * h + 128)],
                             writes=[R(("att", jl), 128 * h, 128 * h + 128)])

                def p_full():
                    p_sq()
                    rsqrt_dve(*rs_args, n=4)
                    p_stt()

                post.sq, post.rs_args, post.stt, post.full = p_sq, rs_args, p_stt, p_full
                return post

            pend = None
            for h in range(4):
                pc = pieces_for(h)
                L = []
                if pend is not None:
                    L.append(pend.sq)
                L += [pc["g0"], pc["g1"], pc["p_exp"]]
                if pc.get("A2"):
                    L.append(pc["A2"])

                def sqrt_window(pc=pc, pend=pend):
                    pc["p_sqrt"]()
                    if pend is not None:
                        rsqrt_act(*pend.rs_args, n=4)
                    if pc.get("a1a_rs"):
                        rsqrt_act(*pc["a1a_rs"]())
                L.append(sqrt_window)
                if pend is not None:
                    L.append(pend.stt)
                if h >= 1:
                    L.append(lambda h=h: att_T(h - 1))
                if pc.get("A1b"):
                    L.append(pc["A1b"])
                if pc.get("A1a"):
                    L.append(pc["A1a"])
                if pc.get("conv"):
                    L.append(pc["conv"])
                L.append(pc["p_tail"])
                pend = head(h, L)
            pend.full()


        xr = {}

        def e_load(s, c, jl):
            r0 = 512 * c + 128 * jl
            xb = nxt("xres", 3)
            P.dma(xres[xb][:], x_d[s, r0:r0 + 128, :], writes=[R(("xres", xb))])
            xr[jl] = xb

        KORD = (4, 5, 6, 0, 1, 2, 7, 3)

        def stage_E(s, c, mid=None):
            hbs = {}
            for jl in range(3):
                hbs[jl] = [nxt("pj", NPJ), nxt("pj", NPJ)]
                e_mm(jl, hbs[jl], range(0, 7))
            if mid is not None:
                mid()
            for jl in range(3):
                e_mm(jl, hbs[jl], range(7, 8))
                e_epilogue(s, c, jl, hbs[jl])
            hbs[3] = [nxt("pj", NPJ), nxt("pj", NPJ)]
            e_mm(3, hbs[3], range(0, 8))
            e_epilogue(s, c, 3, hbs[3])

        def e_mm(jl, hb, kis):
            for half in range(2):
                for ki in kis:
                    k = KORD[ki]
                    P.op("pe", lambda e, half=half, k=k, ki=ki: e.matmul(PJB[hb[half]][:, :], lhsT=mixT[:, k, 128 * jl:128 * jl + 128],
                                                                        rhs=wout_bf[:, k, 512 * half:512 * half + 512], start=(ki == 0), stop=(ki == 7)),
                         reads=[R(("mixT", k), 128 * jl, 128 * jl + 128), R(("wout", k), 512 * half, 512 * half + 512)],
                         writes=[R(PJN[hb[half]])])

        def e_epilogue(s, c, jl, hb):
            r0 = 512 * c + 128 * jl
            xb = xr[jl]
            for half in range(2):
                P.op("dve", lambda e, half=half: e.tensor_tensor(out=xres[xb][:, 512 * half:512 * half + 512], in0=PJB[hb[half]][:, :],
                                                                 in1=xres[xb][:, 512 * half:512 * half + 512], op=ALU.add),
                     reads=[R(PJN[hb[half]]), R(("xres", xb))], writes=[R(("xres", xb))])
            sl = nxt("sl", 2)
            P.op("act", lambda e: e.activation(out=sqj[:], in_=xres[xb][:], func=AF.Square, accum_out=col(SSE + sl)),
                 reads=[R(("xres", xb))], writes=[R("sqj"), R(("sse", sl))])
            rsqrt_dve(col(RSE + sl), col(SSE + sl), 1.0 / D, EPS, [R(("sse", sl))], [R(("rse", sl))])
            P.op("act", lambda e: e.activation(out=xres[xb][:], in_=xres[xb][:], func=AF.Copy, scale=col(RSE + sl)),
                 reads=[R(("xres", xb)), R(("rse", sl))], writes=[R(("xres", xb))])
            P.op("pool", lambda e: e.tensor_tensor(out=xres[xb][:], in0=xres[xb][:], in1=fnw, op=ALU.mult),
                 reads=[R(("xres", xb)), R("sm")], writes=[R(("xres", xb))])
            P.dma(out_d[s, r0:r0 + 128, :], xres[xb][:], reads=[R(("xres", xb))], out=True)
            if jl == 0:
                e_load(s, c, 3)

        chunks = [(s, c) for s in range(S) for c in range(NCH)]

        def x_tile(i, jl):
            s, c = chunks[i]
            r0 = 512 * c + 128 * jl
            return x_d[s, r0:r0 + 128, :]

        wp = weight_pieces()
        stage_A(meta_d, NMETA, muT, "muT", 0)
        for jl in range(4):
            for f in wp[4 * jl:4 * jl + 4]:
                f()
            stage_A(x_tile(0, jl), 128, uT, "uT", 128 * jl)
        for f in wp[16:]:
            f()
        def restore_state():
            P.op("pool", lambda e: e.tensor_copy(out=xl[:, :, 0:3], in_=xlm_tail[:, :, :]),
                 reads=[R("xlm_tail", 0, 4)], writes=[R(("xl", ct), 0, 3) for ct in range(4)])
            P.op("pool", lambda e: e.tensor_copy(out=col(CARRY, 4), in_=col(CARRYM, 4)),
                 reads=[R("carrym", 0, 4)], writes=[R("carry", 0, 4)])

        for i, (s, c) in enumerate(chunks):
            P.dma(cs[:], cs_d[:, :, NMETA + 512 * c:NMETA + 512 * c + 512], writes=[R("cs")])
            if i == 0:
                def meta_pass():
                    stage_B(muT, "muT", NMETA, 0, True, [(NMETA, 0, 0)])
                    stage_C(NMETA, True)
                    restore_state()
                    stage_C_conv(0, 512, False)

                stage_B(uT, "uT", 512, NMETA + 512 * c, False, [(128, 1 + 4 * c + jl, 128 * jl) for jl in range(4)], after_xl0=meta_pass)
            if c == 0 and i > 0:
                restore_state()
            if i > 0:
                stage_B(uT, "uT", 512, NMETA + 512 * c, False, [(128, 1 + 4 * c + jl, 128 * jl) for jl in range(4)],
                        after_xl0=lambda: stage_C_conv(0, 512, False))
            nxt_i = i + 1 if i + 1 < len(chunks) else None
            st = {}

            def a_pieces(k, nxt_i=nxt_i, st=st):
                Dd = {}
                if nxt_i is None:
                    return Dd
                if 2 <= k <= 5:
                    Dd["A2"] = lambda: stage_A2(st[("xb", k - 2)], 128, uT, "uT", 128 * (k - 2))
                if 1 <= k <= 4:
                    def f1():
                        st[("xb", k - 1)] = stage_A1b(st[("a", k - 1)])
                    Dd["A1b"] = f1
                    if k <= 3:
                        def rs():
                            b, sl, rows = st[("a", k - 1)]
                            return (col(RSA + sl), col(SSA + sl), 1.0 / D, EPS, [R(("ssa", sl))], [R(("rsa", sl))])
                        Dd["a1a_rs"] = rs
                if k <= 3:
                    def f2():
                        st[("a", k)] = stage_A1a(x_tile(nxt_i, k), 128, defer_rs=(k <= 2))
                    Dd["A1a"] = f2
                return Dd

            def pieces_for(h):
                g0, g1, p_exp, p_sqrt, p_tail = c_pieces(h, 512)
                pc = {"g0": g0, "g1": g1, "p_exp": p_exp, "p_sqrt": p_sqrt, "p_tail": p_tail}
                if h < 3:
                    pc["conv"] = lambda: stage_C_conv(h + 1, 512, False)
                pc.update(a_pieces(h))
                return pc

            for jl in range(3):
                e_load(s, c, jl)
            stage_D(c, pieces_for)
            ap4 = a_pieces(4)
            for key in ("A2", "A1b"):
                if key in ap4:
                    ap4[key]()

            def mid():
                att_T(3)
                ap5 = a_pieces(5)
                if "A2" in ap5:
                    ap5["A2"]()

            stage_E(s, c, mid=mid)
        P.emit()
    return nc


def _host_layout(inputs, n_cores):
    f32 = np.float32
    x = np.asarray(inputs["x"], dtype=f32)
    B, SEQ, _ = x.shape
    S = B // n_cores
    T = NMETA + SEQ
    small = np.zeros((128, NSM), f32)
    small[:, C_NW:C_NW + 8] = np.asarray(inputs["norm_w"], f32).reshape(8, 128).T
    cw = np.asarray(inputs["conv_w"], f32).reshape(4, 4, 128)
    small[:, C_CW:C_CW + 16] = cw.transpose(2, 1, 0).reshape(128, 16)
    small[:, C_CB:C_CB + 4] = np.asarray(inputs["conv_b"], f32).reshape(4, 128).T
    small[:, C_BR:C_BR + 4] = np.asarray(inputs["b_rgate"], f32).reshape(4, 128).T
    small[:, C_BI:C_BI + 4] = np.asarray(inputs["b_igate"], f32).reshape(4, 128).T
    small[:, C_L:C_L + 4] = np.asarray(inputs["lru_L"], f32).reshape(4, 128).T
    lamv = np.concatenate([np.asarray(inputs[k], f32).reshape(64) for k in ("lambda_q1", "lambda_k1", "lambda_q2", "lambda_k2")])
    small[:, C_LAM:C_LAM + 256] = np.broadcast_to(lamv[None, :], (128, 256))
    small[:, C_SUB:C_SUB + 512] = np.broadcast_to(np.tile(np.asarray(inputs["subln_w"], f32).reshape(128), 4)[None, :], (128, 512))
    small[:, C_FNW:C_FNW + 1024] = np.broadcast_to(np.asarray(inputs["final_norm_w"], f32).reshape(1024)[None, :], (128, 1024))
    wg = np.zeros((128, 2, 4, 128), f32)
    for gi, key in enumerate(("w_rgate", "w_igate")):
        w = np.asarray(inputs[key], f32).reshape(8, 64, 64)
        for ct in range(4):
            wg[0:64, gi, ct, 0:64] = w[2 * ct]
            wg[64:128, gi, ct, 64:128] = w[2 * ct + 1]
    wg = wg.reshape(128, 1024)
    c16 = np.zeros((128, 256), f32)
    c16[:, 0:128] = np.eye(128, dtype=f32)
    for p in range(128):
        d = p % 64
        if d < 8:
            c16[p + 8, 128 + p] = -1.0
        elif d < 16:
            c16[p - 8, 128 + p] = 1.0
    c16 = c16.astype(ml_dtypes.bfloat16)
    pos = np.arange(T, dtype=f32)
    inv = np.power(f32(ROPE_THETA), -np.arange(0, 16, 2, dtype=f32) / f32(16)).astype(f32)
    ang = (pos[:, None] * inv[None, :]).astype(f32)
    cs = np.zeros((128, 2, T), f32)
    cs[:, 0, :] = 1.0
    for p in range(128):
        d = p % 64
        if d < 16:
            cs[p, 0, :] = np.cos(ang[:, d % 8])
            cs[p, 1, :] = np.sin(ang[:, d % 8])
    shared = {
        "meta": np.ascontiguousarray(np.asarray(inputs["meta_tokens"], f32)),
        "w_in": np.ascontiguousarray(np.asarray(inputs["w_in"], f32).reshape(D, DIN)),
        "w_out": np.ascontiguousarray(np.asarray(inputs["w_out"], f32).reshape(D, D)),
        "small": small, "wgate": wg, "c16": c16, "cs": cs,
    }
    in_maps = []
    for i in range(n_cores):
        m = dict(shared)
        m["x"] = np.ascontiguousarray(x[i * S:(i + 1) * S])
        in_maps.append(m)
    return in_maps, S, SEQ


def kernel(**inputs):
    n_cores = 8
    in_maps, S, SEQ = _host_layout(inputs, n_cores)
    nc = build_nc(S, SEQ // 512)
    res = run_bass_kernel_spmd(nc, in_maps, core_ids=list(range(n_cores)))
    return np.concatenate([np.asarray(r["out"]) for r in res.results], axis=0).astype(np.float32)
```

```python
import numpy as np
import ml_dtypes
import concourse.bass as bass
import concourse.mybir as mybir
from concourse.bass_utils import run_bass_kernel_spmd
from contextlib import ExitStack

F32 = mybir.dt.float32
BF16 = mybir.dt.bfloat16
AF = mybir.ActivationFunctionType
ALU = mybir.AluOpType


class _Op:
    __slots__ = ("eng", "idx", "emit", "is_dma", "deps", "signal", "sigval", "sem", "semval", "prev_same_sem", "is_out")

    def __init__(self, eng, idx, emit, is_dma):
        self.eng = eng
        self.idx = idx
        self.emit = emit
        self.is_dma = is_dma
        self.deps = {}
        self.signal = False
        self.sigval = None
        self.sem = None
        self.semval = None
        self.prev_same_sem = None
        self.is_out = False


class Prog:
    ENGS = ("pe", "act", "dve", "pool", "sp")

    def __init__(self, nc, es, n_dma_sems=24, same_engine_raw=True):
        self.nc = nc
        self.es = es
        self.ops = {e: [] for e in self.ENGS}
        self.buf = {}
        self.same_engine_raw = same_engine_raw
        self.eng_sem = {e: es.enter_context(nc.semaphore("sem_" + e)) for e in ("pe", "act", "dve", "pool")}
        self.dma_sems = [es.enter_context(nc.semaphore("sem_dma%d" % i)) for i in range(n_dma_sems)]
        self.n_dma = 0
        self.dma_last = [None] * n_dma_sems

    def _segs(self, name, lo, hi):
        L = self.buf.get(name, [])
        out, new, cur = [], [], lo
        for s in L:
            if s[1] <= lo or s[0] >= hi:
                new.append(s)
                continue
            if s[0] < lo:
                new.append([s[0], lo, s[2], dict(s[3])])
                s = [lo, s[1], s[2], s[3]]
            tail = None
            if s[1] > hi:
                tail = [hi, s[1], s[2], dict(s[3])]
                s = [s[0], hi, s[2], s[3]]
            if cur < s[0]:
                g = [cur, s[0], None, {}]
                new.append(g)
                out.append(g)
            new.append(s)
            out.append(s)
            cur = s[1]
            if tail is not None:
                new.append(tail)
        if cur < hi:
            g = [cur, hi, None, {}]
            new.append(g)
            out.append(g)
        new.sort(key=lambda s: s[0])
        self.buf[name] = new
        return out

    def _add_dep(self, op, d, kind):
        if d is None or d is op:
            return
        if not d.is_dma and d.eng == op.eng and not op.is_dma:
            if d.eng == "pe":
                return
            if not self.same_engine_raw:
                return
        key = id(d)
        if key not in op.deps:
            op.deps[key] = d

    PSUM_NAMES = ("pj", "psc", "pv", "ptr")

    def _record(self, op, reads, writes):
        def is_psum(name):
            return (name[0] if isinstance(name, tuple) else name) in self.PSUM_NAMES
        excl = [(n, 0, 1 << 30) for (n, lo, hi) in list(reads) + list(writes) if is_psum(n)]
        reads = [r for r in reads if not is_psum(r[0])]
        writes = [w for w in writes if not is_psum(w[0])] + excl
        for (name, lo, hi) in reads:
            for s in self._segs(name, lo, hi):
                self._add_dep(op, s[2], "raw")
                rk = ("dma", id(op)) if op.is_dma else op.eng
                s[3][rk] = op
        for (name, lo, hi) in writes:
            for s in self._segs(name, lo, hi):
                self._add_dep(op, s[2], "waw")
                for r in s[3].values():
                    self._add_dep(op, r, "war")
                s[2] = op
                s[3] = {}

    def op(self, eng, emit, reads=(), writes=()):
        o = _Op(eng, len(self.ops[eng]), emit, False)
        self.ops[eng].append(o)
        self._record(o, reads, writes)
        return o

    def dma(self, out_ap, in_ap, reads=(), writes=(), out=False, **kw):
        def emit(e):
            return e.dma_start(out=out_ap, in_=in_ap, **kw)
        o = _Op("sp", len(self.ops["sp"]), emit, True)
        j = self.n_dma
        self.n_dma += 1
        n = len(self.dma_sems)
        o.sem = self.dma_sems[j % n]
        o.semval = 16 * (j // n + 1)
        o.prev_same_sem = self.dma_last[j % n]
        self.dma_last[j % n] = o
        o.is_out = out
        self.ops["sp"].append(o)
        self._record(o, reads, writes)
        return o

    def _finalize(self):
        self.waits = {}
        for e in self.ENGS:
            waited = {}
            for o in self.ops[e]:
                need = {}
                deps = list(o.deps.values())
                if o.is_dma and o.prev_same_sem is not None:
                    deps.append(o.prev_same_sem)
                for d in deps:
                    if d.is_dma:
                        k = ("dma", id(d.sem))
                        v = d.semval
                    else:
                        k = ("eng", d.eng)
                        v = d.idx
                    if waited.get(k, -1) >= v:
                        continue
                    if k not in need or need[k][0] < v:
                        need[k] = (v, d)
                for k, (v, d) in need.items():
                    waited[k] = v
                    if not d.is_dma:
                        d.signal = True
                self.waits[id(o)] = [d for (_, d) in need.values()]
        for e in ("pe", "act", "dve", "pool"):
            c = 0
            for o in self.ops[e]:
                if o.signal:
                    c += 1
                    o.sigval = c

    def emit(self):
        self._finalize()
        nc = self.nc
        handles = {"pe": nc.tensor, "act": nc.scalar, "dve": nc.vector, "pool": nc.gpsimd, "sp": nc.sync}

        def run(e, h):
            for o in self.ops[e]:
                for d in self.waits[id(o)]:
                    if d.is_dma:
                        h.wait_ge(d.sem, d.semval)
                    else:
                        h.wait_ge(self.eng_sem[d.eng], d.sigval)
                inst = o.emit(h)
                if o.is_dma:
                    inst.then_inc(o.sem, 16)
                elif o.signal:
                    inst.then_inc(self.eng_sem[e], 1)
            if e == "sp":
                for o in self.dma_last:
                    if o is not None:
                        h.wait_ge(o.sem, o.semval)

        with nc.Block() as block:
            @block.tensor
            def _(h):
                run("pe", h)

            @block.scalar
            def _(h):
                run("act", h)

            @block.vector
            def _(h):
                run("dve", h)

            @block.gpsimd
            def _(h):
                run("pool", h)

            @block.sync
            def _(h):
                run("sp", h)


D = 1024
DIN = 3072
NMETA = 16
EPS = 1e-6
SUBLN_EPS = 1e-5
LAM_INIT = 0.8 - 0.6
ROPE_THETA = 500000.0
AX = mybir.AxisListType.X

C_NW, C_CW, C_CB, C_BR, C_BI, C_L, C_LAM, C_SUB, C_FNW, NSM = 0, 8, 24, 28, 32, 36, 40, 296, 808, 1832


def R(name, lo=0, hi=1):
    return (name, lo, hi)


def build_nc(S, NCH):
    SEQ = 512 * NCH
    T = NMETA + SEQ
    NT = SEQ // 128
    nc = bass.Bass("TRN2", target_bir_lowering=False)

    def dram(name, shape, dtype=F32, kind="ExternalInput"):
        return nc.dram_tensor(name, shape, dtype, kind=kind).ap()

    x_d = dram("x", [S, SEQ, D])
    meta_d = dram("meta", [NMETA, D])
    win_d = dram("w_in", [D, DIN])
    wout_d = dram("w_out", [D, D])
    small_d = dram("small", [128, NSM])
    wgate_d = dram("wgate", [128, 1024])
    c16_d = dram("c16", [128, 256], BF16)
    cs_d = dram("cs", [128, 2, T])
    out_d = dram("out", [S, SEQ, D], kind="ExternalOutput")

    with ExitStack() as es:
        P = Prog(nc, es)

        def sb(name, shape, dtype):
            return es.enter_context(nc.sbuf_tensor("s_" + name, shape, dtype))

        def ps(name, shape, dtype):
            return es.enter_context(nc.psum_tensor("p_" + name, shape, dtype))

        win_bf = sb("win_bf", [128, 8, DIN], BF16)
        wout_bf = sb("wout_bf", [128, 8, D], BF16)
        NST = 3
        stage = [sb("stage%d" % i, [128, 1024], F32) for i in range(NST)]
        xn = [sb("xn%d" % i, [128, 1024], BF16) for i in range(2)]
        sm = sb("sm", [128, NSM], F32)
        c16 = sb("c16", [128, 256], BF16)
        wg_bf = sb("wg_bf", [128, 8, 128], BF16)
        kT = sb("kT", [128, 4, T], BF16)
        Vaug = sb("Vaug", [128, (NT + 1) * 4, 130], BF16)
        uT = sb("uT", [128, 8, 512], BF16)
        muT = sb("muT", [128, 8, NMETA], BF16)
        qT = sb("qT", [128, 4, 512], BF16)
        xl = sb("xl", [128, 4, 515], F32)
        xlm = sb("xlm", [128, 4, 3 + NMETA], F32)
        gl = sb("gl", [128, 4, 512], BF16)
        gw = sb("gw", [128, 4, 512], BF16)
        mixT = sb("mixT", [128, 8, 512], BF16)
        att = sb("att", [128, 4, 512], BF16)
        cs = sb("cs", [128, 2, 512], F32)
        csm = sb("csm", [128, 2, NMETA], F32)
        rt = [sb("rt%d" % i, [128, 512], F32) for i in range(2)]
        xc2 = [sb("xc%d" % i, [128, 512], F32) for i in range(2)]
        xcb2 = [sb("xcb%d" % i, [128, 512], BF16) for i in range(2)]
        l_tr = sb("l_tr", [128, 512], F32)
        l_ti = sb("l_ti", [128, 512], F32)
        l_a = sb("l_a", [128, 512], F32)
        l_m = sb("l_m", [128, 512], F32)
        tg = [l_a, l_m]
        TGN = ["l_a", "l_m"]
        NET = 4
        ET = [sb("ET%d" % i, [128, 512], BF16) for i in range(NET)]
        oj = sb("oj", [128, 128], BF16)
        xres = [sb("xres%d" % i, [128, 1024], F32) for i in range(3)]
        sc = sb("sc", [128, 128], F32)
        lt = sb("lt", [128, 128], F32)
        xlm_tail = sb("xlm_tail", [128, 4, 3], F32)
        sqj = sb("sqj", [128, 1024], BF16)
        rsx = [sb("rsx%d" % i, [128, 4], F32) for i in range(2)]
        rsy = [sb("rsy%d" % i, [128, 4], F32) for i in range(2)]
        rsa = [sb("rsa%d" % i, [128, 4], F32) for i in range(2)]

        ident = c16[:, 0:128]
        permT = c16[:, 128:256]
        MHALF, LAM, NLAM, LS, LE, SPE, SP, C8, C4, HB, CARRY, CARRYM = 0, 1, 2, 3, 5, 7, 11, 15, 19, 23, 31, 35
        SSA, RSA, SSE, RSE, SSD, RSD, RL = 80, 84, 44, 46, 48, 56, 64

        def col(i, n=1, rows=128):
            return sc[0:rows, i:i + n]

        pj = [ps("pj%d" % i, [128, 512], F32) for i in range(2)]
        ptr = ps("ptr", [128, 8, 128], BF16)
        psc = [ps("psc%d" % i, [128, 512], F32) for i in range(2)]
        pv = [ps("pv%d" % i, [128, 512], F32) for i in range(3)]

        PJB = [pj[0], pj[1], psc[0], psc[1], pv[0], pv[1], pv[2]]
        PJN = [("pj", 0), ("pj", 1), ("psc", 0), ("psc", 1), ("pv", 0), ("pv", 1), ("pv", 2)]
        NPJ = 7
        cnt = {"st": 0, "pj": 0, "xn": 0, "rt": 0, "tg": 0, "psc": 0, "ET": 0, "ot": 0, "xres": 0, "sl": 0, "sla": 0, "pjc": 0}

        def nxt(key, n):
            v = cnt[key] % n
            cnt[key] += 1
            return v

        P.dma(sm[:], small_d, writes=[R("sm")])
        P.dma(c16[:], c16_d, writes=[R("c16")])
        P.dma(csm[:], cs_d[:, :, 0:NMETA], writes=[R("csm")])
        P.op("pool", lambda e: e.memset(Vaug[:, :, 128:129], 1.0), writes=[R("V", 0, NT + 1)])
        P.op("pool", lambda e: e.memset(xlm[:, :, 0:3], 0.0), writes=[R(("xlm", ct), 0, 3) for ct in range(4)])
        P.op("pool", lambda e: e.memset(col(CARRY, 4), 0.0), writes=[R("carry")])
        P.op("pool", lambda e: e.memset(sc[0:64, 39:40], 0.0), writes=[R("maskb", 0, 1)])
        P.op("pool", lambda e: e.memset(sc[64:128, 39:40], -30000.0), writes=[R("maskb", 1, 2)])
        for i in range(2):
            P.op("dve", lambda e, i=i: e.tensor_tensor(out=lt[:, 64 * i:64 * i + 64], in0=sm[:, C_LAM + 128 * i:C_LAM + 128 * i + 64],
                                                       in1=sm[:, C_LAM + 128 * i + 64:C_LAM + 128 * i + 128], op=ALU.mult),
                 reads=[R("sm")], writes=[R("lt", i, i + 1)])
            P.op("dve", lambda e, i=i: e.reduce_sum(out=col(LS + i), in_=lt[:, 64 * i:64 * i + 64], axis=AX),
                 reads=[R("lt", i, i + 1)], writes=[R("ls", i, i + 1)])
        P.op("act", lambda e: e.activation(out=col(LE, 2), in_=col(LS, 2), func=AF.Exp), reads=[R("ls", 0, 2)], writes=[R("le")])
        P.op("pool", lambda e: e.tensor_tensor(out=col(LAM), in0=col(LE), in1=col(LE + 1), op=ALU.subtract),
             reads=[R("le")], writes=[R("lam")])
        P.op("pool", lambda e: e.tensor_scalar(out=col(NLAM), in0=col(LAM), scalar1=-1.0, scalar2=-LAM_INIT, op0=ALU.mult, op1=ALU.add),
             reads=[R("lam")], writes=[R("nlam")])
        P.op("act", lambda e: e.activation(out=col(SPE, 4), in_=sm[:, C_L:C_L + 4], func=AF.Exp, scale=-1.0),
             reads=[R("sm")], writes=[R("spe")])
        P.op("act", lambda e: e.activation(out=col(SP, 4), in_=col(SPE, 4), func=AF.Ln, bias=1.0, scale=1.0),
             reads=[R("spe")], writes=[R("sp")])
        P.op("pool", lambda e: e.tensor_scalar(out=col(C8, 4), in0=col(SP, 4), scalar1=-8.0, scalar2=None, op0=ALU.mult),
             reads=[R("sp")], writes=[R("c8")])
        P.op("pool", lambda e: e.tensor_scalar(out=col(C4, 4), in0=col(SP, 4), scalar1=-4.0, scalar2=None, op0=ALU.mult),
             reads=[R("sp")], writes=[R("c4")])
        P.op("pool", lambda e: e.tensor_scalar(out=col(HB, 8), in0=sm[:, C_BR:C_BR + 8], scalar1=0.5, scalar2=None, op0=ALU.mult),
             reads=[R("sm")], writes=[R("hb")])
        P.op("pool", lambda e: e.tensor_scalar(out=sm[:, C_SUB:C_SUB + 512], in0=sm[:, C_SUB:C_SUB + 512], scalar1=0.5 * (1.0 - LAM_INIT),
                                               scalar2=None, op0=ALU.mult),
             reads=[R("sm")], writes=[R("subg")])
        sublng = sm[:, C_SUB:C_SUB + 512]
        fnw = sm[:, C_FNW:C_FNW + 1024]
        b = nxt("st", NST)
        P.dma(stage[b][:], wgate_d, writes=[R(("st", b))])
        P.op("pool", lambda e, b=b: e.tensor_copy(out=wg_bf[:].rearrange("p a b -> p (a b)"), in_=stage[b][:]),
             reads=[R(("st", b))], writes=[R("wg")])

        I32 = mybir.dt.int32
        RS_MAGIC = 1597463007.0
        rs_cnt = [0]

        def rsqrt_dve(dst, src, scale, eps, src_regs, dst_regs, rows=128, n=1):
            u = rs_cnt[0] % 2
            rs_cnt[0] += 1
            X = rsx[u][0:rows, 0:n]
            Y = rsy[u][0:rows, 0:n]
            A = rsa[u][0:rows, 0:n]
            rx, ry, ra = R(("rsx", u)), R(("rsy", u)), R(("rsa", u))
            P.op("dve", lambda e: e.tensor_scalar(out=X, in0=src, scalar1=scale, scalar2=eps, op0=ALU.mult, op1=ALU.add),
                 reads=src_regs, writes=[rx])
            P.op("dve", lambda e: e.tensor_scalar(out=Y.bitcast(I32), in0=X.bitcast(I32), scalar1=-0.5, scalar2=RS_MAGIC, op0=ALU.mult, op1=ALU.add),
                 reads=[rx], writes=[ry])
            for it in range(3):
                last = it == 2
                if n == 1:
                    P.op("dve", lambda e: e.scalar_tensor_tensor(out=A, in0=Y, scalar=Y, in1=X, op0=ALU.mult, op1=ALU.mult),
                         reads=[ry, rx], writes=[ra])
                else:
                    P.op("dve", lambda e: e.tensor_tensor(out=A, in0=Y, in1=Y, op=ALU.mult), reads=[ry], writes=[ra])
                    P.op("dve", lambda e: e.tensor_tensor(out=A, in0=A, in1=X, op=ALU.mult), reads=[ra, rx], writes=[ra])
                P.op("dve", lambda e: e.tensor_scalar(out=A, in0=A, scalar1=-0.5, scalar2=1.5, op0=ALU.mult, op1=ALU.add), reads=[ra], writes=[ra])
                if last:
                    P.op("dve", lambda e: e.tensor_tensor(out=dst, in0=Y, in1=A, op=ALU.mult), reads=[ry, ra], writes=dst_regs)
                else:
                    P.op("dve", lambda e: e.tensor_tensor(out=Y, in0=Y, in1=A, op=ALU.mult), reads=[ry, ra], writes=[ry])

        def stage_A1a(src_ap, rows, defer_rs=False):
            b = nxt("st", NST)
            sl = nxt("sla", 4)
            P.dma(stage[b][0:rows, :], src_ap, writes=[R(("st", b))])
            P.op("dve", lambda e: e.scalar_tensor_tensor(out=sqj[0:rows, :], in0=stage[b][0:rows, :], scalar=1.0, in1=stage[b][0:rows, :],
                                                         op0=ALU.mult, op1=ALU.mult, accum_out=col(SSA + sl, 1, rows)),
                 reads=[R(("st", b))], writes=[R("sqj"), R(("ssa", sl))])
            if not defer_rs:
                rsqrt_dve(col(RSA + sl, 1, rows), col(SSA + sl, 1, rows), 1.0 / D, EPS, [R(("ssa", sl))], [R(("rsa", sl))], rows=rows)
            return (b, sl, rows)

        def rsqrt_act(dst, src, scale, eps, src_regs, dst_regs, n=1):
            u = rs_cnt[0] % 2
            rs_cnt[0] += 1
            X = rsx[u][:, 0:n]
            P.op("act", lambda e: e.activation(out=X, in_=src, func=AF.Sqrt, bias=eps, scale=scale), reads=src_regs, writes=[R(("rsx", u))])
            P.op("dve", lambda e: e.reciprocal(out=dst, in_=X), reads=[R(("rsx", u))], writes=dst_regs)

        def stage_A1b(ctx):
            b, sl, rows = ctx
            xb = nxt("xn", 2)
            P.op("dve", lambda e: e.tensor_scalar(out=xn[xb][0:rows, :], in0=stage[b][0:rows, :], scalar1=col(RSA + sl, 1, rows),
                                                  scalar2=None, op0=ALU.mult),
                 reads=[R(("st", b)), R(("rsa", sl))], writes=[R(("xn", xb))])
            return xb

        def stage_A1(src_ap, rows):
            return stage_A1b(stage_A1a(src_ap, rows))

        def stage_A2(xb, rows, dst, dst_name, col0):
            for k in range(8):
                P.op("pe", lambda e, k=k: e.transpose(ptr[:, k, 0:rows], xn[xb][0:rows, 128 * k:128 * k + 128], ident[0:rows, 0:rows]),
                     reads=[R(("xn", xb)), R("c16")], writes=[R("ptr", k, k + 1)])
            P.op("dve", lambda e: e.tensor_copy(out=dst[:, :, col0:col0 + rows], in_=ptr[:, :, 0:rows]),
                 reads=[R("ptr", 0, 8)], writes=[R(dst_name, col0, col0 + rows)])

        def stage_A(src_ap, rows, dst, dst_name, col0):
            b = nxt("st", NST)
            sl = nxt("sla", 4)
            xb = nxt("xn", 2)
            P.dma(stage[b][0:rows, :], src_ap, writes=[R(("st", b))])
            P.op("act", lambda e: e.activation(out=sqj[0:rows, :], in_=stage[b][0:rows, :], func=AF.Square, accum_out=col(SSA + sl, 1, rows)),
                 reads=[R(("st", b))], writes=[R("sqj"), R(("ssa", sl))])
            u = rs_cnt[0] % 2
            rs_cnt[0] += 1
            X = rsx[u][0:rows, 0:1]
            P.op("act", lambda e: e.activation(out=X, in_=col(SSA + sl, 1, rows), func=AF.Sqrt, bias=EPS, scale=1.0 / D),
                 reads=[R(("ssa", sl))], writes=[R(("rsx", u))])
            P.op("dve", lambda e: e.reciprocal(out=col(RSA + sl, 1, rows), in_=X), reads=[R(("rsx", u))], writes=[R(("rsa", sl))])
            P.op("act", lambda e: e.activation(out=xn[xb][0:rows, :], in_=stage[b][0:rows, :], func=AF.Copy, scale=col(RSA + sl, 1, rows)),
                 reads=[R(("st", b)), R(("rsa", sl))], writes=[R(("xn", xb))])
            stage_A2(xb, rows, dst, dst_name, col0)

        def weight_pieces():
            L = []
            cntw = [0]

            def win_piece(cg, k):
                def f():
                    b = cntw[0] % 3
                    cntw[0] += 1
                    P.dma(xres[b][:], win_d[128 * k:128 * k + 128, 1024 * cg:1024 * cg + 1024], writes=[R(("xres", b))])
                    if cntw[0] % 2 == 0:
                        P.op("dve", lambda e: e.tensor_scalar(out=win_bf[:, k, 1024 * cg:1024 * cg + 1024], in0=xres[b][:],
                                                              scalar1=sm[:, C_NW + k:C_NW + k + 1], scalar2=None, op0=ALU.mult),
                             reads=[R(("xres", b)), R("sm")], writes=[R(("win", k), 1024 * cg, 1024 * cg + 1024)])
                    else:
                        P.op("act", lambda e: e.activation(out=win_bf[:, k, 1024 * cg:1024 * cg + 1024], in_=xres[b][:],
                                                           func=AF.Copy, scale=sm[:, C_NW + k:C_NW + k + 1]),
                             reads=[R(("xres", b)), R("sm")], writes=[R(("win", k), 1024 * cg, 1024 * cg + 1024)])
                return f

            def wout_piece(k):
                def f():
                    b = cntw[0] % 3
                    cntw[0] += 1
                    P.dma(xres[b][:], wout_d[128 * k:128 * k + 128, :], writes=[R(("xres", b))])
                    if k % 2 == 0:
                        P.op("dve", lambda e: e.tensor_scalar(out=wout_bf[:, k, :], in0=xres[b][:], scalar1=(1.0 if k < 4 else 0.5),
                                                              scalar2=None, op0=ALU.mult),
                             reads=[R(("xres", b))], writes=[R(("wout", k), 0, 1024)])
                    else:
                        P.op("act", lambda e: e.activation(out=wout_bf[:, k, :], in_=xres[b][:], func=AF.Copy, scale=(1.0 if k < 4 else 0.5)),
                             reads=[R(("xres", b))], writes=[R(("wout", k), 0, 1024)])
                return f

            for cg in range(3):
                for k in range(8):
                    L.append(win_piece(cg, k))
            for k in range(8):
                L.append(wout_piece(k))
            return L

        def proj_group(lhs_fn, rhs_fn, outrows, N, reads_fn):
            b = nxt("pj", NPJ)
            for k in range(8):
                P.op("pe", lambda e, k=k: e.matmul(PJB[b][0:outrows, 0:N], lhsT=lhs_fn(k), rhs=rhs_fn(k), start=(k == 0), stop=(k == 7)),
                     reads=reads_fn(k), writes=[R(PJN[b])])
            return b

        def stage_B(u, u_name, N, p0, meta, tiles, after_xl0=None):
            cos = csm[:, 0, :] if meta else cs[:, 0, :]
            sin = csm[:, 1, :] if meta else cs[:, 1, :]
            cs_name = "csm" if meta else "cs"

            def fm(c0):
                return proj_group(lambda k: win_bf[:, k, c0:c0 + 128], lambda k: u[:, k, 0:N], 128, N,
                                  lambda k: [R(("win", k), c0, c0 + 128), R(u_name, 0, N)])

            ropes = []
            for h in range(4):
                b = fm(512 + 128 * h)
                dest = kT[:, h, p0:p0 + N]
                dreg = R(("kT", h), p0, p0 + N)
                P.op("act", lambda e, b=b, dest=dest: e.activation(out=dest, in_=PJB[b][:, 0:N], func=AF.Copy),
                     reads=[R(PJN[b])], writes=[dreg])
                ropes.append((dest, dreg))
            if not meta:
                for h in range(4):
                    b = fm(128 * h)
                    dest = qT[:, h, 0:N]
                    dreg = R(("qT", h), 0, N)
                    P.op("act", lambda e, b=b, dest=dest: e.activation(out=dest, in_=PJB[b][:, 0:N], func=AF.Copy),
                         reads=[R(PJN[b])], writes=[dreg])
                    ropes.append((dest, dreg))

            def do_rope(dest, dreg):
                b2 = nxt("pj", NPJ)
                r = nxt("rt", 1)
                t1, t2 = rt[2 * r], rt[2 * r + 1]
                P.op("pe", lambda e: e.matmul(PJB[b2][:, 0:N], lhsT=permT, rhs=dest, start=True, stop=True),
                     reads=[dreg, R("c16")], writes=[R(PJN[b2])])
                P.op("pool", lambda e: e.tensor_tensor(out=t1[:, 0:N], in0=dest, in1=cos, op=ALU.mult),
                     reads=[dreg, R(cs_name)], writes=[R(("rt", 2 * r), 0, 512)])
                P.op("dve", lambda e: e.tensor_tensor(out=t2[:, 0:N], in0=PJB[b2][:, 0:N], in1=sin, op=ALU.mult),
                     reads=[R(PJN[b2]), R(cs_name)], writes=[R(("rt", 2 * r + 1), 0, 512)])
                P.op("pool", lambda e: e.tensor_tensor(out=dest, in0=t1[:, 0:N], in1=t2[:, 0:N], op=ALU.add),
                     reads=[R(("rt", 2 * r), 0, 512), R(("rt", 2 * r + 1), 0, 512)], writes=[dreg])

            for (rows, vt, c0l) in tiles:
                b = proj_group(lambda k, c0l=c0l, rows=rows: u[:, k, c0l:c0l + rows], lambda k: win_bf[:, k, 1024:1536], rows, 512,
                               lambda k: [R(("win", k), 1024, 1536), R(u_name, c0l, c0l + rows)])
                P.op("dve", lambda e, b=b, rows=rows, vt=vt: e.tensor_copy(out=Vaug[0:rows, 4 * vt:4 * vt + 4, 0:128],
                                                                          in_=PJB[b][0:rows, :].rearrange("p (h d) -> p h d", h=4)),
                     reads=[R(PJN[b])], writes=[R("V", vt, vt + 1)])
            for (dest, dreg) in ropes[0:4]:
                do_rope(dest, dreg)
            if not meta:
                for jl, (rows, vt, c0l) in enumerate(tiles):
                    b = proj_group(lambda k, c0l=c0l, rows=rows: u[:, k, c0l:c0l + rows], lambda k: win_bf[:, k, 1536:2048], rows, 512,
                                   lambda k: [R(("win", k), 1536, 2048), R(u_name, c0l, c0l + rows)])
                    g = nxt("tg", 2)
                    P.op("act", lambda e, b=b, g=g: e.activation(out=tg[g][:], in_=PJB[b][:, :], func=AF.Tanh, scale=0.5),
                         reads=[R(PJN[b])], writes=[R(TGN[g])])
                    P.op("dve", lambda e, b=b, g=g: e.scalar_tensor_tensor(out=tg[g][:], in0=tg[g][:], scalar=1.0, in1=PJB[b][:, :],
                                                                           op0=ALU.add, op1=ALU.mult),
                         reads=[R(PJN[b]), R(TGN[g])], writes=[R(TGN[g])])
                    P.op("pool", lambda e, g=g, jl=jl: e.tensor_tensor(out=gw[:, jl, :], in0=tg[g][:], in1=sublng, op=ALU.mult),
                         reads=[R(TGN[g]), R("subg")], writes=[R(("gw", jl), 0, 512)])
                for (dest, dreg) in ropes[4:8]:
                    do_rope(dest, dreg)
            for ct in range(4):
                if ct == 1 and after_xl0 is not None:
                    after_xl0()
                b = fm(2048 + 128 * ct)
                P.op("act", lambda e, b=b, ct=ct: e.activation(out=(xlm if meta else xl)[:, ct, 3:3 + N], in_=PJB[b][:, 0:N], func=AF.Copy),
                     reads=[R(PJN[b])], writes=[R(("xlm" if meta else "xl", ct), 3, 3 + N)])
            if not meta:
                for ct in range(4):
                    b = fm(2560 + 128 * ct)
                    g = nxt("tg", 2)
                    P.op("act", lambda e, b=b, g=g: e.activation(out=tg[g][:, 0:N], in_=PJB[b][:, 0:N], func=AF.Tanh, scale=0.5),
                         reads=[R(PJN[b])], writes=[R(TGN[g])])
                    P.op("dve", lambda e, b=b, g=g, ct=ct: e.scalar_tensor_tensor(out=gl[:, ct, 0:N], in0=tg[g][:, 0:N], scalar=1.0,
                                                                                 in1=PJB[b][:, 0:N], op0=ALU.add, op1=ALU.mult),
                         reads=[R(PJN[b]), R(TGN[g])], writes=[R(("gl", ct), 0, N)])

        def stage_C_conv(ct, N, meta, xl=xl):
            xc, xcb = xc2[ct % 2], xcb2[ct % 2]
            xcn, xcbn = ("xc", ct % 2), ("xcb", ct % 2)
            xln = "xlm" if meta else "xl"
            if meta:
                xl = xlm
            xlr = R((xln, ct), 0, 3 + N)
            P.op("pool", lambda e: e.tensor_scalar(out=xc[:, 0:N], in0=xl[:, ct, 0:N], scalar1=sm[:, C_CW + 4 * ct:C_CW + 4 * ct + 1],
                                                   scalar2=sm[:, C_CB + ct:C_CB + ct + 1], op0=ALU.mult, op1=ALU.add),
                 reads=[xlr, R("sm")], writes=[R(xcn)])
            for j in range(1, 4):
                P.op("dve", lambda e, j=j: e.scalar_tensor_tensor(out=xc[:, 0:N], in0=xl[:, ct, j:j + N],
                                                                  scalar=sm[:, C_CW + 4 * ct + j:C_CW + 4 * ct + j + 1],
                                                                  in1=xc[:, 0:N], op0=ALU.mult, op1=ALU.add),
                     reads=[xlr, R("sm"), R(xcn)], writes=[R(xcn)])
            if meta:
                P.op("pool", lambda e: e.tensor_copy(out=xlm_tail[:, ct, :], in_=xl[:, ct, N:N + 3]),
                     reads=[R((xln, ct), N, N + 3)], writes=[R("xlm_tail", ct, ct + 1)])
            else:
                P.op("pool", lambda e: e.tensor_copy(out=xl[:, ct, 0:3], in_=xl[:, ct, N:N + 3]),
                     reads=[R(("xl", ct), N, N + 3)], writes=[R(("xl", ct), 0, 3)])
            P.op("pool", lambda e: e.tensor_copy(out=xcb[:, 0:N], in_=xc[:, 0:N]), reads=[R(xcn)], writes=[R(xcbn)])

        def stage_C_rest(ct, N, meta, in_hook=False):
            xc, xcb = xc2[ct % 2], xcb2[ct % 2]
            xcn, xcbn = ("xc", ct % 2), ("xcb", ct % 2)
            for gi, dst, dname in ((0, l_tr, "l_tr"), (1, l_ti, "l_ti")):
                b = nxt("pjc", 2) if in_hook else nxt("pj", NPJ)
                P.op("pe", lambda e, b=b, gi=gi: e.matmul(PJB[b][:, 0:N], lhsT=wg_bf[:, 4 * gi + ct, :], rhs=xcb[:, 0:N],
                                                         start=True, stop=True),
                     reads=[R("wg"), R(xcbn)], writes=[R(PJN[b])])
                P.op("act", lambda e, b=b, gi=gi, dst=dst: e.activation(out=dst[:, 0:N], in_=PJB[b][:, 0:N], func=AF.Tanh,
                                                                       bias=col(HB + 4 * gi + ct), scale=0.5),
                     reads=[R(PJN[b]), R("hb")], writes=[R(dname)])
            P.op("act", lambda e: e.activation(out=l_a[:, 0:N], in_=l_tr[:, 0:N], func=AF.Exp, bias=col(C4 + ct), scale=col(C4 + ct)),
                 reads=[R("l_tr"), R("c4")], writes=[R("l_a")])
            P.op("pool", lambda e: e.tensor_tensor(out=l_m[:, 0:N], in0=l_a[:, 0:N], in1=l_a[:, 0:N], op=ALU.mult),
                 reads=[R("l_a")], writes=[R("l_m")])
            P.op("act", lambda e: e.activation(out=l_m[:, 0:N], in_=l_m[:, 0:N], func=AF.Sqrt, bias=0.25, scale=-0.25),
                 reads=[R("l_m")], writes=[R("l_m")])
            if meta:
                P.op("pool", lambda e: e.memset(l_m[:, 0:1], 0.5), reads=[], writes=[R("l_m")])
            P.op("dve", lambda e: e.scalar_tensor_tensor(out=l_ti[:, 0:N], in0=l_ti[:, 0:N], scalar=1.0, in1=xc[:, 0:N],
                                                         op0=ALU.add, op1=ALU.mult),
                 reads=[R("l_ti"), R(xcn)], writes=[R("l_ti")])
            P.op("pool", lambda e: e.tensor_tensor(out=l_ti[:, 0:N], in0=l_ti[:, 0:N], in1=l_m[:, 0:N], op=ALU.mult),
                 reads=[R("l_ti"), R("l_m")], writes=[R("l_ti")])
            P.op("dve", lambda e: e.tensor_tensor_scan(out=l_tr[:, 0:N], data0=l_a[:, 0:N], data1=l_ti[:, 0:N],
                                                       initial=col(CARRY + ct), op0=ALU.mult, op1=ALU.add),
                 reads=[R("l_a"), R("l_ti"), R("carry", ct, ct + 1)], writes=[R("l_tr")])
            P.op("pool", lambda e: e.tensor_copy(out=col(CARRY + ct), in_=l_tr[:, N - 1:N]),
                 reads=[R("l_tr")], writes=[R("carry", ct, ct + 1)])
            if meta:
                P.op("pool", lambda e: e.tensor_copy(out=col(CARRYM + ct), in_=l_tr[:, N - 1:N]),
                     reads=[R("l_tr")], writes=[R("carrym", ct, ct + 1)])
            else:
                P.op("pool", lambda e: e.tensor_tensor(out=mixT[:, 4 + ct, 0:N], in0=l_tr[:, 0:N], in1=gl[:, ct, 0:N], op=ALU.mult),
                     reads=[R("l_tr"), R(("gl", ct), 0, N)], writes=[R(("mixT", 4 + ct), 0, N)])

        def stage_C(N, meta):
            for ct in range(4):
                stage_C_conv(ct, N, meta)
                stage_C_rest(ct, N, meta)

        def c_pieces(ct, N):
            xc, xcb = xc2[ct % 2], xcb2[ct % 2]
            xcn, xcbn = ("xc", ct % 2), ("xcb", ct % 2)

            def gate(gi, dst, dname):
                def f():
                    b = nxt("pjc", 2)
                    P.op("pe", lambda e: e.matmul(PJB[b][:, 0:N], lhsT=wg_bf[:, 4 * gi + ct, :], rhs=xcb[:, 0:N], start=True, stop=True),
                         reads=[R("wg"), R(xcbn)], writes=[R(PJN[b])])
                    P.op("act", lambda e: e.activation(out=dst[:, 0:N], in_=PJB[b][:, 0:N], func=AF.Tanh, bias=col(HB + 4 * gi + ct), scale=0.5),
                         reads=[R(PJN[b]), R("hb")], writes=[R(dname)])
                return f

            def p_exp():
                P.op("act", lambda e: e.activation(out=l_a[:, 0:N], in_=l_tr[:, 0:N], func=AF.Exp, bias=col(C4 + ct), scale=col(C4 + ct)),
                     reads=[R("l_tr"), R("c4")], writes=[R("l_a")])
                P.op("pool", lambda e: e.tensor_tensor(out=l_m[:, 0:N], in0=l_a[:, 0:N], in1=l_a[:, 0:N], op=ALU.mult),
                     reads=[R("l_a")], writes=[R("l_m")])
                P.op("dve", lambda e: e.scalar_tensor_tensor(out=l_ti[:, 0:N], in0=l_ti[:, 0:N], scalar=1.0, in1=xc[:, 0:N],
                                                             op0=ALU.add, op1=ALU.mult),
                     reads=[R("l_ti"), R(xcn)], writes=[R("l_ti")])

            def p_sqrt():
                P.op("act", lambda e: e.activation(out=l_m[:, 0:N], in_=l_m[:, 0:N], func=AF.Sqrt, bias=0.25, scale=-0.25),
                     reads=[R("l_m")], writes=[R("l_m")])

            def p_tail():
                P.op("pool", lambda e: e.tensor_tensor(out=l_ti[:, 0:N], in0=l_ti[:, 0:N], in1=l_m[:, 0:N], op=ALU.mult),
                     reads=[R("l_ti"), R("l_m")], writes=[R("l_ti")])
                P.op("dve", lambda e: e.tensor_tensor_scan(out=l_tr[:, 0:N], data0=l_a[:, 0:N], data1=l_ti[:, 0:N],
                                                           initial=col(CARRY + ct), op0=ALU.mult, op1=ALU.add),
                     reads=[R("l_a"), R("l_ti"), R("carry", ct, ct + 1)], writes=[R("l_tr")])
                P.op("pool", lambda e: e.tensor_copy(out=col(CARRY + ct), in_=l_tr[:, N - 1:N]),
                     reads=[R("l_tr")], writes=[R("carry", ct, ct + 1)])
                P.op("pool", lambda e: e.tensor_tensor(out=mixT[:, 4 + ct, 0:N], in0=l_tr[:, 0:N], in1=gl[:, ct, 0:N], op=ALU.mult),
                     reads=[R("l_tr"), R(("gl", ct), 0, N)], writes=[R(("mixT", 4 + ct), 0, N)])

            return [gate(0, l_tr, "l_tr"), gate(1, l_ti, "l_ti"), p_exp, p_sqrt, p_tail]

        def att_T(h):
            for jl in range(4):
                P.op("pe", lambda e, jl=jl: e.transpose(ptr[:, jl, :], att[:, jl, 128 * h:128 * h + 128], ident),
                     reads=[R(("att", jl), 128 * h, 128 * h + 128), R("c16")], writes=[R("ptr", jl, jl + 1)])
            P.op("dve", lambda e: e.tensor_copy(out=mixT[:, h, :], in_=ptr[:, 0:4, :].rearrange("p j n -> p (j n)")),
                 reads=[R("ptr", 0, 4)], writes=[R(("mixT", h), 0, 512)])

        def stage_D(c, pieces_for=None):
            ktiles = [None] + list(range(4 * c + 4))
            def head(h, pieces):
                items = [(kt, m) for kt in ktiles for m in range(2)]
                started = set()
                info = {}

                def emit_scores(n):
                    kt, m = items[n]
                    b = nxt("psc", 2)
                    if kt is None:
                        K, kp0, qlo = NMETA, 0, 0
                    else:
                        K, kp0, qlo = 128, NMETA + 128 * kt, max(0, kt - 4 * c)
                    Nq = 512 - 128 * qlo
                    P.op("pe", lambda e: e.matmul(psc[b][0:K, 0:Nq], lhsT=kT[64 * m:64 * m + 64, h, kp0:kp0 + K],
                                                  rhs=qT[64 * m:64 * m + 64, h, 128 * qlo:512], start=True, stop=True),
                         reads=[R(("kT", h), kp0, kp0 + K), R(("qT", h), 128 * qlo, 512)], writes=[R(("psc", b))])
                    et = nxt("ET", NET)
                    if kt is not None and kt >= 4 * c:
                        P.op("act", lambda e: e.activation(out=ET[et][:, 0:64], in_=psc[b][:, 0:64], func=AF.Exp, scale=0.125, bias=sc[:, 39:40]),
                             reads=[R(("psc", b)), R("maskb", 0, 2)], writes=[R(("ET", et), 0, 64)])
                        P.op("act", lambda e: e.activation(out=ET[et][:, 64:Nq], in_=psc[b][:, 64:Nq], func=AF.Exp, scale=0.125),
                             reads=[R(("psc", b))], writes=[R(("ET", et), 64, 512)])
                    else:
                        P.op("act", lambda e: e.activation(out=ET[et][0:K, 0:Nq], in_=psc[b][0:K, 0:Nq], func=AF.Exp, scale=0.125),
                             reads=[R(("psc", b))], writes=[R(("ET", et), 0, 512)])
                    info[n] = (et, K, qlo)

                def emit_pv(n):
                    kt, m = items[n]
                    et, K, qlo = info[n]
                    vt = 0 if kt is None else kt + 1
                    for jl in range(qlo, 4):
                        a = 2 * jl + m
                        bank, off = a // 3, 130 * (a % 3)
                        first = bank not in started
                        started.add(bank)
                        last = (kt is not None) and (kt == 4 * c + jl)
                        P.op("pe", lambda e, jl=jl, bank=bank, off=off, first=first, last=last: e.matmul(
                            pv[bank][:, off:off + 129], lhsT=ET[et][0:K, 128 * (jl - qlo):128 * (jl - qlo) + 128],
                            rhs=Vaug[0:K, 4 * vt + h, 0:129], start=first, stop=last, skip_group_check=True),
                            reads=[R(("ET", et), 0, 512), R("V", vt, vt + 1)], writes=[R(("pv", bank), off, off + 129)])

                emit_scores(0)
                emit_scores(1)
                pi = 0
                pstride = max(1, (len(items) - 2) // max(1, len(pieces)))
                for n in range(len(items)):
                    emit_pv(n)
                    if n + 2 < len(items):
                        emit_scores(n + 2)
                    if n >= 1 and (n - 1) % pstride == 0 and pi < len(pieces):
                        pieces[pi]()
                        pi += 1
                sl = nxt("sl", 2)
                accs = []
                for a in range(8):
                    bank, off = a // 3, 130 * (a % 3)
                    raw = rt[a // 4][:, 128 * (a % 4):128 * (a % 4) + 128]
                    rawr = R(("rt", a // 4), 128 * (a % 4), 128 * (a % 4) + 128)
                    accs.append((None, None, raw, rawr, R(("rl", sl), a, a + 1), RL + 8 * sl + a))
                for bank, na in ((0, 3), (1, 3), (2, 2)):
                    a0 = 3 * bank
                    P.op("dve", lambda e, bank=bank, na=na, a0=a0: e.reciprocal(
                        out=col(RL + 8 * sl + a0, na), in_=pv[bank][:, 0:130 * na].rearrange("p (a c) -> p a c", c=130)[:, :, 128]),
                        reads=[R(("pv", bank), 0, 130 * na)], writes=[R(("rl", sl), a0, a0 + na)])
                def rawslots(lo, n):
                    t = rt[lo // 4]
                    return t[:, 128 * (lo % 4):128 * (lo % 4) + 128 * n].rearrange("p (a c) -> p a c", c=128)

                def accslots(bank, j0, n):
                    return pv[bank][:, 130 * j0:130 * (j0 + n)].rearrange("p (a c) -> p a c", c=130)[:, :, 0:128]

                for eng, bank, j0, n, lo in (("act", 0, 0, 3, 0), ("dve", 1, 0, 1, 3), ("dve", 1, 1, 2, 4), ("act", 2, 0, 2, 6)):
                    rd = [R(("pv", bank), 130 * j0, 130 * (j0 + n))]
                    wr = [R(("rt", lo // 4), 128 * (lo % 4), 128 * (lo % 4) + 128 * n)]
                    if eng == "act":
                        P.op("act", lambda e, bank=bank, j0=j0, n=n, lo=lo: e.activation(out=rawslots(lo, n), in_=accslots(bank, j0, n), func=AF.Copy),
                             reads=rd, writes=wr)
                    else:
                        P.op("dve", lambda e, bank=bank, j0=j0, n=n, lo=lo: e.tensor_copy(out=rawslots(lo, n), in_=accslots(bank, j0, n)),
                             reads=rd, writes=wr)
                while pi < len(pieces):
                    pieces[pi]()
                    pi += 1

                def p_norm():
                    for jl in range(4):
                        (acc0, rr0, raw0, rawr0, rl0, c0) = accs[2 * jl]
                        (acc1, rr1, raw1, rawr1, rl1, c1) = accs[2 * jl + 1]
                        P.op("dve", lambda e, c1=c1: e.tensor_tensor(out=col(c1), in0=col(c1), in1=col(NLAM), op=ALU.mult),
                             reads=[rl1, R("nlam")], writes=[rl1])
                        P.op("dve", lambda e, raw0=raw0, c0=c0: e.tensor_scalar(out=raw0, in0=raw0, scalar1=col(c0), scalar2=None, op0=ALU.mult),
                             reads=[rawr0, rl0], writes=[rawr0])
                        P.op("dve", lambda e, raw0=raw0, raw1=raw1, c1=c1: e.scalar_tensor_tensor(out=raw0, in0=raw1, scalar=col(c1), in1=raw0,
                                                                                                op0=ALU.mult, op1=ALU.add),
                             reads=[rawr1, rl1, rawr0], writes=[rawr0])

                o_ap = [accs[2 * jl][2] for jl in range(4)]
                o_rg = [accs[2 * jl][3] for jl in range(4)]

                class Post:
                    pass
                post = Post()
                rs_args = (col(RSD + 4 * sl, 4), col(SSD + 4 * sl, 4), 1.0 / 128, SUBLN_EPS, [R(("ssd", sl), 0, 4)], [R(("rsd", sl), 0, 4)])

                def p_sq():
                    p_norm()
                    for jl in range(4):
                        P.op("dve", lambda e, jl=jl: e.scalar_tensor_tensor(out=oj[:], in0=o_ap[jl], scalar=1.0, in1=o_ap[jl], op0=ALU.mult, op1=ALU.mult,
                                                                            accum_out=col(SSD + 4 * sl + jl)),
                             reads=[o_rg[jl]], writes=[R("oj"), R(("ssd", sl), jl, jl + 1)])

                def p_stt():
                    for jl in range(4):
                        P.op("dve", lambda e, jl=jl: e.scalar_tensor_tensor(out=att[:, jl, 128 * h:128 * h + 128], in0=o_ap[jl],
                                                                            scalar=col(RSD + 4 * sl + jl), in1=gw[:, jl, 128 * h:128 * h + 128],
                                                                            op0=ALU.mult, op1=ALU.mult),
                             reads=[o_rg[jl], R(("rsd", sl), jl, jl + 1), R(("gw", jl), 128 * h, 128 * h + 128)],
                             writes=[R(("att", jl), 128 * h, 128 * h + 128)])

                def p_full():
                    p_sq()
                    rsqrt_act(*rs_args, n=4)
                    p_stt()

                post.sq, post.rs_args, post.stt, post.full = p_sq, rs_args, p_stt, p_full
                return post

            pend = None
            for h in range(4):
                pc = pieces_for(h)
                L = []
                if pend is not None:
                    L.append(pend.sq)
                L += [pc["g0"], pc["g1"], pc["p_exp"]]
                if pc.get("A2"):
                    L.append(pc["A2"])

                def sqrt_window(pc=pc, pend=pend):
                    pc["p_sqrt"]()
                    if pend is not None:
                        rsqrt_act(*pend.rs_args, n=4)
                    if pc.get("a1a_rs"):
                        rsqrt_act(*pc["a1a_rs"]())
                L.append(sqrt_window)
                if pend is not None:
                    L.append(pend.stt)
                if h >= 1:
                    L.append(lambda h=h: att_T(h - 1))
                if pc.get("A1b"):
                    L.append(pc["A1b"])
                if pc.get("A1a"):
                    L.append(pc["A1a"])
                if pc.get("conv"):
                    L.append(pc["conv"])
                L.append(pc["p_tail"])
                pend = head(h, L)
            pend.full()


        xr = {}

        def e_load(s, c, jl):
            r0 = 512 * c + 128 * jl
            xb = nxt("xres", 3)
            P.dma(xres[xb][:], x_d[s, r0:r0 + 128, :], writes=[R(("xres", xb))])
            xr[jl] = xb

        KORD = (4, 5, 6, 0, 1, 2, 7, 3)

        def stage_E(s, c, mid=None):
            hbs = {}
            for jl in range(3):
                hbs[jl] = [nxt("pj", NPJ), nxt("pj", NPJ)]
                e_mm(jl, hbs[jl], range(0, 7))
            if mid is not None:
                mid()
            for jl in range(3):
                e_mm(jl, hbs[jl], range(7, 8))
                e_epilogue(s, c, jl, hbs[jl])
            hbs[3] = [nxt("pj", NPJ), nxt("pj", NPJ)]
            e_mm(3, hbs[3], range(0, 8))
            e_epilogue(s, c, 3, hbs[3])

        def e_mm(jl, hb, kis):
            for half in range(2):
                for ki in kis:
                    k = KORD[ki]
                    P.op("pe", lambda e, half=half, k=k, ki=ki: e.matmul(PJB[hb[half]][:, :], lhsT=mixT[:, k, 128 * jl:128 * jl + 128],
                                                                        rhs=wout_bf[:, k, 512 * half:512 * half + 512], start=(ki == 0), stop=(ki == 7)),
                         reads=[R(("mixT", k), 128 * jl, 128 * jl + 128), R(("wout", k), 512 * half, 512 * half + 512)],
                         writes=[R(PJN[hb[half]])])

        def e_epilogue(s, c, jl, hb):
            r0 = 512 * c + 128 * jl
            xb = xr[jl]
            for half in range(2):
                P.op("dve", lambda e, half=half: e.tensor_tensor(out=xres[xb][:, 512 * half:512 * half + 512], in0=PJB[hb[half]][:, :],
                                                                 in1=xres[xb][:, 512 * half:512 * half + 512], op=ALU.add),
                     reads=[R(PJN[hb[half]]), R(("xres", xb))], writes=[R(("xres", xb))])
            sl = nxt("sl", 2)
            P.op("act", lambda e: e.activation(out=sqj[:], in_=xres[xb][:], func=AF.Square, accum_out=col(SSE + sl)),
                 reads=[R(("xres", xb))], writes=[R("sqj"), R(("sse", sl))])
            rsqrt_act(col(RSE + sl), col(SSE + sl), 1.0 / D, EPS, [R(("sse", sl))], [R(("rse", sl))])
            P.op("act", lambda e: e.activation(out=xres[xb][:], in_=xres[xb][:], func=AF.Copy, scale=col(RSE + sl)),
                 reads=[R(("xres", xb)), R(("rse", sl))], writes=[R(("xres", xb))])
            P.op("pool", lambda e: e.tensor_tensor(out=xres[xb][:], in0=xres[xb][:], in1=fnw, op=ALU.mult),
                 reads=[R(("xres", xb)), R("sm")], writes=[R(("xres", xb))])
            P.dma(out_d[s, r0:r0 + 128, :], xres[xb][:], reads=[R(("xres", xb))], out=True)
            if jl == 0:
                e_load(s, c, 3)

        chunks = [(s, c) for s in range(S) for c in range(NCH)]

        def x_tile(i, jl):
            s, c = chunks[i]
            r0 = 512 * c + 128 * jl
            return x_d[s, r0:r0 + 128, :]

        wp = weight_pieces()
        stage_A(meta_d, NMETA, muT, "muT", 0)
        for jl in range(4):
            for f in wp[4 * jl:4 * jl + 4]:
                f()
            stage_A(x_tile(0, jl), 128, uT, "uT", 128 * jl)
        for f in wp[16:]:
            f()
        def restore_state():
            P.op("pool", lambda e: e.tensor_copy(out=xl[:, :, 0:3], in_=xlm_tail[:, :, :]),
                 reads=[R("xlm_tail", 0, 4)], writes=[R(("xl", ct), 0, 3) for ct in range(4)])
            P.op("pool", lambda e: e.tensor_copy(out=col(CARRY, 4), in_=col(CARRYM, 4)),
                 reads=[R("carrym", 0, 4)], writes=[R("carry", 0, 4)])

        for i, (s, c) in enumerate(chunks):
            P.dma(cs[:], cs_d[:, :, NMETA + 512 * c:NMETA + 512 * c + 512], writes=[R("cs")])
            if i == 0:
                def meta_pass():
                    stage_B(muT, "muT", NMETA, 0, True, [(NMETA, 0, 0)])
                    stage_C(NMETA, True)
                    restore_state()
                    stage_C_conv(0, 512, False)

                stage_B(uT, "uT", 512, NMETA + 512 * c, False, [(128, 1 + 4 * c + jl, 128 * jl) for jl in range(4)], after_xl0=meta_pass)
            if c == 0 and i > 0:
                restore_state()
            if i > 0:
                stage_B(uT, "uT", 512, NMETA + 512 * c, False, [(128, 1 + 4 * c + jl, 128 * jl) for jl in range(4)],
                        after_xl0=lambda: stage_C_conv(0, 512, False))
            nxt_i = i + 1 if i + 1 < len(chunks) else None
            st = {}

            def a_pieces(k, nxt_i=nxt_i, st=st):
                Dd = {}
                if nxt_i is None:
                    return Dd
                if 2 <= k <= 5:
                    Dd["A2"] = lambda: stage_A2(st[("xb", k - 2)], 128, uT, "uT", 128 * (k - 2))
                if 1 <= k <= 4:
                    def f1():
                        st[("xb", k - 1)] = stage_A1b(st[("a", k - 1)])
                    Dd["A1b"] = f1
                    if k <= 4:
                        def rs():
                            b, sl, rows = st[("a", k - 1)]
                            return (col(RSA + sl), col(SSA + sl), 1.0 / D, EPS, [R(("ssa", sl))], [R(("rsa", sl))])
                        Dd["a1a_rs"] = rs
                if k <= 3:
                    def f2():
                        st[("a", k)] = stage_A1a(x_tile(nxt_i, k), 128, defer_rs=True)
                    Dd["A1a"] = f2
                return Dd

            def pieces_for(h):
                g0, g1, p_exp, p_sqrt, p_tail = c_pieces(h, 512)
                pc = {"g0": g0, "g1": g1, "p_exp": p_exp, "p_sqrt": p_sqrt, "p_tail": p_tail}
                if h < 3:
                    pc["conv"] = lambda: stage_C_conv(h + 1, 512, False)
                pc.update(a_pieces(h))
                return pc

            for jl in range(3):
                e_load(s, c, jl)
            stage_D(c, pieces_for)
            ap4 = a_pieces(4)
            if "a1a_rs" in ap4:
                rsqrt_act(*ap4["a1a_rs"]())
            for key in ("A2", "A1b"):
                if key in ap4:
                    ap4[key]()

            def mid():
                att_T(3)
                ap5 = a_pieces(5)
                if "A2" in ap5:
                    ap5["A2"]()

            stage_E(s, c, mid=mid)
        P.emit()
    return nc


def _host_layout(inputs, n_cores):
    f32 = np.float32
    x = np.asarray(inputs["x"], dtype=f32)
    B, SEQ, _ = x.shape
    S = B // n_cores
    T = NMETA + SEQ
    small = np.zeros((128, NSM), f32)
    small[:, C_NW:C_NW + 8] = np.asarray(inputs["norm_w"], f32).reshape(8, 128).T
    cw = np.asarray(inputs["conv_w"], f32).reshape(4, 4, 128)
    small[:, C_CW:C_CW + 16] = cw.transpose(2, 1, 0).reshape(128, 16)
    small[:, C_CB:C_CB + 4] = np.asarray(inputs["conv_b"], f32).reshape(4, 128).T
    small[:, C_BR:C_BR + 4] = np.asarray(inputs["b_rgate"], f32).reshape(4, 128).T
    small[:, C_BI:C_BI + 4] = np.asarray(inputs["b_igate"], f32).reshape(4, 128).T
    small[:, C_L:C_L + 4] = np.asarray(inputs["lru_L"], f32).reshape(4, 128).T
    lamv = np.concatenate([np.asarray(inputs[k], f32).reshape(64) for k in ("lambda_q1", "lambda_k1", "lambda_q2", "lambda_k2")])
    small[:, C_LAM:C_LAM + 256] = np.broadcast_to(lamv[None, :], (128, 256))
    small[:, C_SUB:C_SUB + 512] = np.broadcast_to(np.tile(np.asarray(inputs["subln_w"], f32).reshape(128), 4)[None, :], (128, 512))
    small[:, C_FNW:C_FNW + 1024] = np.broadcast_to(np.asarray(inputs["final_norm_w"], f32).reshape(1024)[None, :], (128, 1024))
    wg = np.zeros((128, 2, 4, 128), f32)
    for gi, key in enumerate(("w_rgate", "w_igate")):
        w = np.asarray(inputs[key], f32).reshape(8, 64, 64)
        for ct in range(4):
            wg[0:64, gi, ct, 0:64] = w[2 * ct]
            wg[64:128, gi, ct, 64:128] = w[2 * ct + 1]
    wg = wg.reshape(128, 1024)
    c16 = np.zeros((128, 256), f32)
    c16[:, 0:128] = np.eye(128, dtype=f32)
    for p in range(128):
        d = p % 64
        if d < 8:
            c16[p + 8, 128 + p] = -1.0
        elif d < 16:
            c16[p - 8, 128 + p] = 1.0
    c16 = c16.astype(ml_dtypes.bfloat16)
    pos = np.arange(T, dtype=f32)
    inv = np.power(f32(ROPE_THETA), -np.arange(0, 16, 2, dtype=f32) / f32(16)).astype(f32)
    ang = (pos[:, None] * inv[None, :]).astype(f32)
    cs = np.zeros((128, 2, T), f32)
    cs[:, 0, :] = 1.0
    for p in range(128):
        d = p % 64
        if d < 16:
            cs[p, 0, :] = np.cos(ang[:, d % 8])
            cs[p, 1, :] = np.sin(ang[:, d % 8])
    shared = {
        "meta": np.ascontiguousarray(np.asarray(inputs["meta_tokens"], f32)),
        "w_in": np.ascontiguousarray(np.asarray(inputs["w_in"], f32).reshape(D, DIN)),
        "w_out": np.ascontiguousarray(np.asarray(inputs["w_out"], f32).reshape(D, D)),
        "small": small, "wgate": wg, "c16": c16, "cs": cs,
    }
    in_maps = []
    for i in range(n_cores):
        m = dict(shared)
        m["x"] = np.ascontiguousarray(x[i * S:(i + 1) * S])
        in_maps.append(m)
    return in_maps, S, SEQ


def kernel(**inputs):
    n_cores = 8
    in_maps, S, SEQ = _host_layout(inputs, n_cores)
    nc = build_nc(S, SEQ // 512)
    res = run_bass_kernel_spmd(nc, in_maps, core_ids=list(range(n_cores)))
    return np.concatenate([np.asarray(r["out"]) for r in res.results], axis=0).astype(np.float32)
```
